# Optimizing a Trainium2 kernel written in Bass

```python
import math
import jax, jax.numpy as jnp
from jax import lax
import numpy as np

D_MODEL = 1024
BATCH = 4
SEQ = 4096
DEPTH = 4

GRID_W = 64
CTX_LEN = 256
N_MIXERS = 3
N_LAYERS_A = (DEPTH + 2) // 3
N_LAYERS_B = (DEPTH + 1) // 3
N_LAYERS_C = DEPTH // 3

A_HEADS = 8
A_KV_HEADS = 2
A_HEAD_DIM = 128
B_HEADS = 16
B_HEAD_DIM = D_MODEL // B_HEADS
B_MAX_KH = 8
B_KW = 16
C_HEADS = 8
C_NOPE = 128
C_ROPE = 64
C_V = 128
C_Q_RANK = 384
C_KV_RANK = 256
D_FF = -(-8 * D_MODEL // (3 * 256)) * 256

ROPE_BASE = 10000.0
Q_BLOCK = 128
RMS_EPS = 1e-6
LN_EPS = 1e-5
DEEPNORM_ALPHA = (2.0 * DEPTH) ** 0.25
DEEPNORM_BETA = (8.0 * DEPTH) ** -0.25

kernel_name = "hybrid_interleaved_gqa_natten_mla_deepnorm"


def rms_norm(t, g):
    tf = t.astype(jnp.float32)
    y = tf * lax.rsqrt(jnp.mean(tf * tf, axis=-1, keepdims=True) + RMS_EPS)
    return (y * g.astype(jnp.float32)).astype(t.dtype)


def layer_norm(t, g, b):
    tf = t.astype(jnp.float32)
    mu = jnp.mean(tf, axis=-1, keepdims=True)
    var = jnp.mean(jnp.square(tf - mu), axis=-1, keepdims=True)
    y = (tf - mu) * lax.rsqrt(var + LN_EPS)
    return (y * g.astype(jnp.float32) + b.astype(jnp.float32)).astype(t.dtype)


def modulate(t, shift, scale):
    return t * (1 + scale) + shift


def rope_1d(t, ang):
    cos = jnp.cos(ang)[None, :, None, :].astype(t.dtype)
    sin = jnp.sin(ang)[None, :, None, :].astype(t.dtype)
    t1, t2 = jnp.split(t, 2, axis=-1)
    return jnp.concatenate([t1 * cos - t2 * sin, t2 * cos + t1 * sin], axis=-1)


def axial_rope(t, pos_r, pos_c):
    half = t.shape[-1] // 2
    freqs = ROPE_BASE ** (-jnp.arange(0, half, 2, dtype=jnp.float32) / half)
    ang_r = pos_r.astype(jnp.float32)[:, None] * freqs[None, :]
    ang_c = pos_c.astype(jnp.float32)[:, None] * freqs[None, :]
    t_r, t_c = jnp.split(t, 2, axis=-1)
    return jnp.concatenate([rope_1d(t_r, ang_r), rope_1d(t_c, ang_c)], axis=-1)


def softmax_attend(q, keys, vals, scale):
    s = jnp.concatenate([jnp.einsum('bqgrd,bkgd->bgrqk', q, k) for k in keys], axis=-1)
    p = jax.nn.softmax(s.astype(jnp.float32) * scale, axis=-1).astype(q.dtype)
    offs = np.cumsum([0] + [k.shape[1] for k in keys])
    out = None
    for j, vj in enumerate(vals):
        pj = p[..., int(offs[j]):int(offs[j + 1])]
        oj = jnp.einsum('bgrqk,bkgd->bqgrd', pj, vj)
        out = oj if out is None else out + oj
    return out


def blocked_attention(q, k, v, kc, vc, scale):
    B, N, G, R, dk = q.shape
    nb = N // Q_BLOCK
    qb = q.reshape(B, nb, Q_BLOCK, G, R, dk).transpose(1, 0, 2, 3, 4, 5)
    ob = lax.map(lambda qi: softmax_attend(qi, [k, kc], [v, vc], scale), qb)
    return ob.transpose(1, 0, 2, 3, 4, 5).reshape(B, N, G, R, v.shape[-1])


def swiglu(t, w_gate, w_up, w_down):
    return (jax.nn.silu(t @ w_gate) * (t @ w_up)) @ w_down


def mixer_gqa(h, hc, w_qkv, q_gain, k_gain, w_o, pos_r, pos_c, need_ctx):
    B, N, _ = h.shape
    L = hc.shape[1]
    rep = A_HEADS // A_KV_HEADS
    split = [A_HEADS * A_HEAD_DIM, (A_HEADS + A_KV_HEADS) * A_HEAD_DIM]

    def proj(t):
        bt, tt, _ = t.shape
        q, k, v = jnp.split(t @ w_qkv, split, axis=-1)
        q = rms_norm(q.reshape(bt, tt, A_HEADS, A_HEAD_DIM), q_gain)
        k = rms_norm(k.reshape(bt, tt, A_KV_HEADS, A_HEAD_DIM), k_gain)
        v = v.reshape(bt, tt, A_KV_HEADS, A_HEAD_DIM)
        return q, k, v

    q, k, v = proj(h)
    qc, kc, vc = proj(hc)
    q = axial_rope(q, pos_r, pos_c)
    k = axial_rope(k, pos_r, pos_c)
    scale = A_HEAD_DIM ** -0.5
    o = blocked_attention(q.reshape(B, N, A_KV_HEADS, rep, A_HEAD_DIM), k, v, kc, vc, scale)
    y = o.reshape(B, N, A_HEADS * A_HEAD_DIM) @ w_o
    yc = None
    if need_ctx:
        oc = softmax_attend(qc.reshape(B, L, A_KV_HEADS, rep, A_HEAD_DIM), [kc], [vc], scale)
        yc = oc.reshape(B, L, A_HEADS * A_HEAD_DIM) @ w_o
    return y, yc


def mixer_neighbourhood(h, hc, w_qkv, rpb, w_o, need_ctx):
    B, N, _ = h.shape
    L = hc.shape[1]
    rows = N // GRID_W
    kh = min(B_MAX_KH, rows)
    scale = B_HEAD_DIM ** -0.5

    def proj(t):
        bt, tt, _ = t.shape
        q, k, v = jnp.split(t @ w_qkv, 3, axis=-1)
        shp = (bt, tt, B_HEADS, B_HEAD_DIM)
        return q.reshape(shp), k.reshape(shp), v.reshape(shp)

    q, k, v = proj(h)
    qc, kc, vc = proj(hc)
    grid = (B, rows, GRID_W, B_HEADS, B_HEAD_DIM)
    k = k.reshape(grid)
    v = v.reshape(grid)
    q_rows = q.reshape(grid).transpose(1, 0, 2, 3, 4)

    col = jnp.arange(GRID_W)
    col_start = jnp.clip(col - B_KW // 2, 0, GRID_W - B_KW)
    col_idx = col_start[:, None] + jnp.arange(B_KW)[None, :]
    dc = col_idx - col[:, None] + (B_KW - 1)
    bias_c = rpb[:, :, dc]

    def row_block(args):
        r, qr = args
        rs = jnp.clip(r - kh // 2, 0, rows - kh)
        k_rows = lax.dynamic_slice_in_dim(k, rs, kh, axis=1)
        v_rows = lax.dynamic_slice_in_dim(v, rs, kh, axis=1)
        k_win = k_rows[:, :, col_idx]
        v_win = v_rows[:, :, col_idx]
        dr = rs + jnp.arange(kh) - r + (B_MAX_KH - 1)
        bias = jnp.take(bias_c, dr, axis=1).transpose(0, 2, 1, 3)
        s_loc = (jnp.einsum('bwhd,biwjhd->bhwij', qr, k_win).astype(jnp.float32) * scale
                 + bias[None].astype(jnp.float32)).reshape(B, B_HEADS, GRID_W, kh * B_KW)
        s_ctx = jnp.einsum('bwhd,blhd->bhwl', qr, kc).astype(jnp.float32) * scale
        p = jax.nn.softmax(jnp.concatenate([s_loc, s_ctx], axis=-1), axis=-1).astype(qr.dtype)
        p_loc = p[..., :kh * B_KW].reshape(B, B_HEADS, GRID_W, kh, B_KW)
        p_ctx = p[..., kh * B_KW:]
        return (jnp.einsum('bhwij,biwjhd->bwhd', p_loc, v_win)
                + jnp.einsum('bhwl,blhd->bwhd', p_ctx, vc))

    o_rows = lax.map(row_block, (jnp.arange(rows), q_rows))
    y = o_rows.transpose(1, 0, 2, 3, 4).reshape(B, N, D_MODEL) @ w_o
    yc = None
    if need_ctx:
        oc = softmax_attend(qc[:, :, :, None, :], [kc], [vc], scale)
        yc = oc.reshape(B, L, D_MODEL) @ w_o
    return y, yc


def mixer_mla(h, hc, w_dqkv, q_a_gain, kv_a_gain, w_uq, w_ukv, w_o, pos_r, pos_c, need_ctx):
    B, N, _ = h.shape
    L = hc.shape[1]
    scale = (C_NOPE + C_ROPE) ** -0.5

    def proj(t, rotate):
        bt, tt, _ = t.shape
        q_lat, kv_lat, k_rope = jnp.split(t @ w_dqkv, [C_Q_RANK, C_Q_RANK + C_KV_RANK], axis=-1)
        q = (rms_norm(q_lat, q_a_gain) @ w_uq).reshape(bt, tt, C_HEADS, C_NOPE + C_ROPE)
        kv = (rms_norm(kv_lat, kv_a_gain) @ w_ukv).reshape(bt, tt, C_HEADS, C_NOPE + C_V)
        q_nope, q_rope = jnp.split(q, [C_NOPE], axis=-1)
        k_nope, v = jnp.split(kv, [C_NOPE], axis=-1)
        k_rope = k_rope[:, :, None, :]
        if rotate:
            q_rope = axial_rope(q_rope, pos_r, pos_c)
            k_rope = axial_rope(k_rope, pos_r, pos_c)
        q = jnp.concatenate([q_nope, q_rope], axis=-1)
        k = jnp.concatenate([k_nope, jnp.broadcast_to(k_rope, (bt, tt, C_HEADS, C_ROPE))], axis=-1)
        return q[:, :, :, None, :], k, v

    q, k, v = proj(h, True)
    qc, kc, vc = proj(hc, False)
    o = blocked_attention(q, k, v, kc, vc, scale)
    y = o.reshape(B, N, C_HEADS * C_V) @ w_o
    yc = None
    if need_ctx:
        oc = softmax_attend(qc, [kc], [vc], scale)
        yc = oc.reshape(B, L, C_HEADS * C_V) @ w_o
    return y, yc


def setup_inputs(seed: int = 0) -> dict:
    key = jax.random.key(seed)
    ks = iter(jax.random.split(key, 32))

    def nrm(shape, scale):
        return jax.random.normal(next(ks), shape, jnp.float32) * scale

    def gain(shape):
        return 1.0 + nrm(shape, 0.02)

    D = D_MODEL
    inp = {}
    inp["x"] = nrm((BATCH, SEQ, D), 1.0)
    inp["c"] = nrm((BATCH, D), 1.0)
    inp["ctx"] = nrm((BATCH, CTX_LEN, D), 1.0)
    inp["c_ctx"] = nrm((D,), 1.0)
    inp["w_ada"] = nrm((DEPTH, D, 6 * D), 0.5 * D ** -0.5)
    inp["b_ada"] = nrm((DEPTH, 6 * D), 0.02)
    inp["ln_g"] = gain((DEPTH, 2, D))
    inp["ln_b"] = nrm((DEPTH, 2, D), 0.02)
    inp["w_ffn_gate"] = nrm((DEPTH, D, D_FF), D ** -0.5)
    inp["w_ffn_up"] = nrm((DEPTH, D, D_FF), D ** -0.5)
    inp["w_ffn_down"] = nrm((DEPTH, D_FF, D), DEEPNORM_BETA * D_FF ** -0.5)
    inp["a_w_qkv"] = nrm((N_LAYERS_A, D, (A_HEADS + 2 * A_KV_HEADS) * A_HEAD_DIM), D ** -0.5)
    inp["a_q_gain"] = gain((N_LAYERS_A, A_HEAD_DIM))
    inp["a_k_gain"] = gain((N_LAYERS_A, A_HEAD_DIM))
    inp["a_w_o"] = nrm((N_LAYERS_A, A_HEADS * A_HEAD_DIM, D), DEEPNORM_BETA * (A_HEADS * A_HEAD_DIM) ** -0.5)
    inp["b_w_qkv"] = nrm((N_LAYERS_B, D, 3 * D), D ** -0.5)
    inp["b_rpb"] = nrm((N_LAYERS_B, B_HEADS, 2 * B_MAX_KH - 1, 2 * B_KW - 1), 0.5)
    inp["b_w_o"] = nrm((N_LAYERS_B, D, D), DEEPNORM_BETA * D ** -0.5)
    inp["c_w_dqkv"] = nrm((N_LAYERS_C, D, C_Q_RANK + C_KV_RANK + C_ROPE), D ** -0.5)
    inp["c_q_a_gain"] = gain((N_LAYERS_C, C_Q_RANK))
    inp["c_kv_a_gain"] = gain((N_LAYERS_C, C_KV_RANK))
    inp["c_w_uq"] = nrm((N_LAYERS_C, C_Q_RANK, C_HEADS * (C_NOPE + C_ROPE)), C_Q_RANK ** -0.5)
    inp["c_w_ukv"] = nrm((N_LAYERS_C, C_KV_RANK, C_HEADS * (C_NOPE + C_V)), C_KV_RANK ** -0.5)
    inp["c_w_o"] = nrm((N_LAYERS_C, C_HEADS * C_V, D), DEEPNORM_BETA * (C_HEADS * C_V) ** -0.5)
    return inp


def reference(x, c, ctx, c_ctx, w_ada, b_ada, ln_g, ln_b, w_ffn_gate, w_ffn_up, w_ffn_down,
              a_w_qkv, a_q_gain, a_k_gain, a_w_o,
              b_w_qkv, b_rpb, b_w_o,
              c_w_dqkv, c_q_a_gain, c_kv_a_gain, c_w_uq, c_w_ukv, c_w_o):
    N = x.shape[1]
    t = jnp.arange(N)
    pos_r = t // GRID_W
    pos_c = t % GRID_W
    cond_lat = jax.nn.silu(c)
    cond_ctx = jax.nn.silu(c_ctx)
    xc = ctx
    for i in range(DEPTH):
        need_ctx = i < DEPTH - 1
        mod = cond_lat @ w_ada[i] + b_ada[i]
        mod_c = cond_ctx @ w_ada[i] + b_ada[i]
        sh1, sc1, g1, sh2, sc2, g2 = jnp.split(mod[:, None, :], 6, axis=-1)
        sh1c, sc1c, g1c, sh2c, sc2c, g2c = jnp.split(mod_c, 6, axis=-1)
        h = modulate(x, sh1, sc1)
        hc = modulate(xc, sh1c, sc1c)
        kind, j = i % N_MIXERS, i // N_MIXERS
        if kind == 0:
            y, yc = mixer_gqa(h, hc, a_w_qkv[j], a_q_gain[j], a_k_gain[j], a_w_o[j],
                              pos_r, pos_c, need_ctx)
        elif kind == 1:
            y, yc = mixer_neighbourhood(h, hc, b_w_qkv[j], b_rpb[j], b_w_o[j], need_ctx)
        else:
            y, yc = mixer_mla(h, hc, c_w_dqkv[j], c_q_a_gain[j], c_kv_a_gain[j], c_w_uq[j],
                              c_w_ukv[j], c_w_o[j], pos_r, pos_c, need_ctx)
        x = layer_norm(DEEPNORM_ALPHA * x + g1 * y, ln_g[i, 0], ln_b[i, 0])
        f = swiglu(modulate(x, sh2, sc2), w_ffn_gate[i], w_ffn_up[i], w_ffn_down[i])
        x = layer_norm(DEEPNORM_ALPHA * x + g2 * f, ln_g[i, 1], ln_b[i, 1])
        if need_ctx:
            xc = layer_norm(DEEPNORM_ALPHA * xc + g1c * yc, ln_g[i, 0], ln_b[i, 0])
            fc = swiglu(modulate(xc, sh2c, sc2c), w_ffn_gate[i], w_ffn_up[i], w_ffn_down[i])
            xc = layer_norm(DEEPNORM_ALPHA * xc + g2c * fc, ln_g[i, 1], ln_b[i, 1])
    return x
```

```python
import math
from contextlib import ExitStack

import numpy as np
import concourse.bass as bass
import concourse.mybir as mybir
from concourse.bass_utils import run_bass_kernel_spmd

F32 = mybir.dt.float32
BF16 = mybir.dt.bfloat16
AF = mybir.ActivationFunctionType
ALU = mybir.AluOpType

D = 1024
DC = 8
NT = 17
TOK = NT * 128
NKT = 2 * NT
DFF = 2816
DEPTH = 4
ALPHA = (2.0 * DEPTH) ** 0.25
LN_EPS = 1e-5
RMS_EPS = 1e-6
GROUPS = [[0, 1], [2, 3], [4, 5], [6, 7]]
CHUNKS = [(0, 4), (4, 4), (8, 4), (12, 4), (16, 1)]
NEG = -30000.0


class Buf:
    __slots__ = ("w", "r", "excl")

    def __init__(self, excl=False):
        self.w = None
        self.r = {}
        self.excl = excl


class Eng:
    def __init__(self, name, handle, sem):
        self.name, self.h, self.sem, self.count, self.seen = name, handle, sem, 0, {}


class FW:
    def __init__(self, nc, stack, n_dma_sems=40):
        self.nc, self.stack = nc, stack
        mk = lambda n: stack.enter_context(nc.semaphore(n))
        self.pe = Eng("pe", nc.tensor, mk("s_pe"))
        self.act = Eng("act", nc.scalar, mk("s_act"))
        self.dve = Eng("dve", nc.vector, mk("s_dve"))
        self.pool = Eng("pool", nc.gpsimd, mk("s_pool"))
        self.sp = Eng("sp", nc.sync, mk("s_sp"))
        self.engines = [self.pe, self.act, self.dve, self.pool, self.sp]
        self.cc_sem, self.cc_count = mk("s_cc"), 0
        self.dma_pools = {"sp": [[mk(f"s_dh{i}"), 0] for i in range(n_dma_sems - 12)],
                          "pool": [[mk(f"s_ds{i}"), 0] for i in range(12)]}
        self.dma_sems = self.dma_pools["sp"] + self.dma_pools["pool"]
        self.dma_rr = {"sp": 0, "pool": 0}

    def _deps(self, e, reads, writes, skip_self):
        toks = {}

        def add(t):
            if t is None:
                return
            s, v = t
            if skip_self and s.num == e.sem.num:
                return
            if toks.get(s.num, (None, 0))[1] < v:
                toks[s.num] = (s, v)
        for b in reads:
            add(b.w)
            if b.excl:
                for t in b.r.values():
                    if t[0].num != e.sem.num:
                        add(t)
        for b in writes:
            add(b.w)
            for t in b.r.values():
                add(t)
        for s, v in toks.values():
            if e.seen.get(s.num, 0) < v:
                e.h.wait_ge(s, v)
                e.seen[s.num] = v

    @staticmethod
    def _mark(tok, reads, writes):
        s, v = tok
        for b in reads:
            o = b.r.get(s.num)
            if o is None or o[1] < v:
                b.r[s.num] = tok
        for b in writes:
            b.w = tok
            b.r = {}

    def op(self, e, fn, reads=(), writes=(), signal=True):
        self._deps(e, reads, writes, skip_self=(e is self.pe))
        ins = fn()
        if signal:
            e.count += 1
            ins.then_inc(e.sem, 1)
            tok = (e.sem, e.count)
        else:
            tok = (e.sem, e.count + 1)
        self._mark(tok, reads, writes)
        return ins

    def dma(self, q, out, in_, reads=(), writes=()):
        pl = self.dma_pools[q.name]
        slot = pl[self.dma_rr[q.name]]
        self.dma_rr[q.name] = (self.dma_rr[q.name] + 1) % len(pl)
        sem, val = slot
        self._deps(q, reads, writes, skip_self=False)
        if val > 0 and q.seen.get(sem.num, 0) < val:
            q.h.wait_ge(sem, val)
            q.seen[sem.num] = val
        q.h.dma_start(out=out, in_=in_).then_inc(sem, 16)
        slot[1] = val + 16
        self._mark((sem, val + 16), reads, writes)

    def collective(self, kind, in_ap, out_ap, reads=(), writes=()):
        q = self.pool
        self._deps(q, reads, writes, skip_self=False)
        ins = q.h.collective_compute(kind, ALU.bypass, replica_groups=GROUPS, ins=[in_ap], outs=[out_ap])
        self.cc_count += 1
        ins.then_inc(self.cc_sem, 1)
        self._mark((self.cc_sem, self.cc_count), reads, writes)

    def barrier(self):
        for e in self.engines:
            for o in self.engines:
                if o is not e and o.count > e.seen.get(o.sem.num, 0):
                    e.h.wait_ge(o.sem, o.count)
                    e.seen[o.sem.num] = o.count
            for sem, val in self.dma_sems:
                if val > e.seen.get(sem.num, 0):
                    e.h.wait_ge(sem, val)
                    e.seen[sem.num] = val
            if self.cc_count > e.seen.get(self.cc_sem.num, 0):
                e.h.wait_ge(self.cc_sem, self.cc_count)
                e.seen[self.cc_sem.num] = self.cc_count


class Rot:
    def __init__(self, items):
        self.items, self.i = items, 0

    def next(self):
        it = self.items[self.i]
        self.i = (self.i + 1) % len(self.items)
        return it


def build_program(n_layers=DEPTH, dbg=False):
    nc = bass.Bass("TRN2", target_bir_lowering=False)
    dt_in = lambda name, shape, dt=F32: nc.dram_tensor(name, list(shape), dt, kind="ExternalInput").ap()
    xin = dt_in("xin", [TOK, D])
    cc = dt_in("cc", [2, D])
    w_ada = dt_in("w_ada", [DEPTH, D, 6 * D])
    b_ada = dt_in("b_ada", [DEPTH, 6 * D])
    ln_g = dt_in("ln_g", [DEPTH, 2, D])
    ln_b = dt_in("ln_b", [DEPTH, 2, D])
    w_gate = dt_in("w_ffn_gate", [DEPTH, D, DFF])
    w_up = dt_in("w_ffn_up", [DEPTH, D, DFF])
    w_down = dt_in("w_ffn_down", [DEPTH, DFF, D])
    a_wqkv = dt_in("a_wqkv", [2, D, 2816])
    a_gain = dt_in("a_gain", [2, 4, 128])
    a_wo = dt_in("a_w_o", [2, D, D])
    b_wqkv = dt_in("b_w_qkv", [1, D, 3072])
    b_tab = dt_in("b_tab", [8, 5, 128, 12, 128])
    b_wo = dt_in("b_w_o", [1, D, D])
    c_wdqkv = dt_in("c_wdqkv", [D, 896])
    c_gain = dt_in("c_gain", [5, 128])
    c_wuq = dt_in("c_wuq", [384, 2048])
    c_wukv = dt_in("c_wukv", [256, 2048])
    c_wo = dt_in("c_w_o", [1, D, D])
    ropeA = dt_in("ropeA", [2, 128, TOK])
    ropeC = dt_in("ropeC", [2, 128, TOK])
    out = nc.dram_tensor("out", [2048, D], F32, kind="ExternalOutput").ap()
    if dbg:
        dbg_x = nc.dram_tensor("dbg_x", [TOK, D], F32, kind="ExternalOutput").ap()

    modd = nc.dram_tensor("modd", [DEPTH, 2, 6 * D], F32).ap()
    KINDS = ["A", "B", "C", "A"]
    NQ = {"A": 8, "B": 8, "C": 12}
    NK = {"A": 2, "B": 8, "C": 9}
    VC = {"A": 256, "B": 1024, "C": 1024}
    scr = []
    for l in range(DEPTH):
        k = KINDS[l]
        nkp = (NK[k] + 1) // 2
        krows = [min(2, NK[k] - 2 * i) * 128 for i in range(nkp)]
        nvp = VC[k] // 256
        d = dict(
            QT=nc.dram_tensor(f"QT{l}", [NQ[k] * 128, TOK], BF16).ap(),
            KTin=[nc.dram_tensor(f"KTin{l}_{i}", [krows[i], TOK], BF16).ap() for i in range(nkp)],
            KTall=[nc.dram_tensor(f"KTall{l}_{i}", [2 * krows[i], TOK], BF16).ap() for i in range(nkp)],
            Vin=[nc.dram_tensor(f"Vin{l}_{i}", [TOK, 256], BF16).ap() for i in range(nvp)],
            Vall=[nc.dram_tensor(f"Vall{l}_{i}", [2 * TOK, 256], BF16).ap() for i in range(nvp)],
            krows=krows, bQT=Buf(), bKTin=Buf(), bKTall=Buf(), bVin=Buf(), bVall=Buf())
        scr.append(d)

    def kt_in(sc, u):
        return sc["KTin"][u // 2][(u % 2) * 128:(u % 2) * 128 + 128, :]

    def kt_all(sc, s, u):
        r0 = s * sc["krows"][u // 2] + (u % 2) * 128
        return sc["KTall"][u // 2][r0:r0 + 128, :]

    with ExitStack() as top:
        fw = FW(nc, top)
        pe, act, dve, pool, sp = fw.pe, fw.act, fw.dve, fw.pool, fw.sp
        V, S, G, T = nc.vector, nc.scalar, nc.gpsimd, nc.tensor

        uid = [0]

        def sb(stack, name, shape, dt):
            uid[0] += 1
            return stack.enter_context(nc.sbuf_tensor(f"{name}_{uid[0]}", list(shape), dt))

        x_sb = sb(top, "x_sb", [128, NT, D], F32)
        bx = [Buf() for _ in range(NT)]
        ident = sb(top, "ident", [128, 128], F32)
        ones = sb(top, "ones", [128, 128], BF16)
        ones32 = sb(top, "ones32", [128, 128], F32)
        onesp = sb(top, "onesp", [128, 2, 128], BF16)
        epsc = sb(top, "epsc", [128, 2], F32)
        bconst = Buf()
        modT = sb(top, "modT", [128, 2, 48], F32)
        bmodT = Buf()
        condT = sb(top, "condT", [128, DC, 2], F32)
        bcond = Buf()
        psum = [top.enter_context(nc.psum_tensor(f"ps{i}", [128, 512], F32)) for i in range(8)]
        bps = [Buf(excl=True) for _ in range(8)]

        fw.op(pool, lambda: G.memset(ident[:], 0.0), writes=[bconst])
        fw.op(pool, lambda: G.affine_select(out=ident[:], in_=ident[:], compare_op=ALU.not_equal, fill=1.0,
                                            base=0, pattern=[[-1, 128]], channel_multiplier=1),
              reads=[bconst], writes=[bconst])
        fw.op(dve, lambda: V.memset(ones[:], 1.0), writes=[bconst])
        fw.op(dve, lambda: V.memset(ones32[:], 1.0), writes=[bconst])
        fw.op(dve, lambda: V.memset(onesp[:], 0.0), writes=[bconst])
        fw.op(dve, lambda: V.memset(onesp[:, 0, 0:64], 1.0), writes=[bconst])
        fw.op(dve, lambda: V.memset(onesp[:, 1, 64:128], 1.0), writes=[bconst])
        fw.op(dve, lambda: V.memset(epsc[:, 0:1], LN_EPS), writes=[bconst])
        fw.op(dve, lambda: V.memset(epsc[:, 1:2], RMS_EPS), writes=[bconst])
        for (t0, n) in CHUNKS:
            fw.dma(sp, x_sb[:, t0:t0 + n, :], xin[t0 * 128:(t0 + n) * 128, :].rearrange("(t p) d -> p t d", p=128),
                   writes=bx[t0:t0 + n])

        def mm_group(pi, out_ap, pairs, reads):
            n = len(pairs)
            for i, (l_ap, r_ap) in enumerate(pairs):
                fw.op(pe, lambda: T.matmul(out_ap, lhsT=l_ap, rhs=r_ap, start=(i == 0), stop=(i == n - 1)),
                      reads=reads, writes=[bps[pi]], signal=(i == n - 1))

        def load_colvecs(stack, name, rows_ap, n):
            tmp = sb(stack, name + "_r", [n, 128], F32)
            dst = sb(stack, name, [128, n], F32)
            b1, b2 = Buf(), Buf()
            fw.dma(sp, tmp[:], rows_ap, writes=[b1])
            fw.op(pe, lambda: T.transpose(psum[7][:, 0:n], tmp[:], ident[0:n, 0:n]), reads=[b1, bconst], writes=[bps[7]])
            fw.op(dve, lambda: V.tensor_copy(out=dst[:], in_=psum[7][:, 0:n]), reads=[bps[7]], writes=[b2])
            return dst, b2

        tr_banks = Rot([6, 7])

        def make_hT(hT, bhT, t0, n, kind_sc, kind_sh):
            lc = 1 if t0 == 16 else 0
            N = n * 128
            for c in range(DC):
                pi = tr_banks.next()
                for t in range(n):
                    fw.op(pe, lambda: T.transpose(psum[pi][:, t * 128:(t + 1) * 128], x_sb[:, t0 + t, c * 128:(c + 1) * 128], ident[:]),
                          reads=[bx[t0 + t], bconst], writes=[bps[pi]], signal=(t == n - 1))
                sc_ap = modT[:, lc, kind_sc * 8 + c:kind_sc * 8 + c + 1]
                sh_ap = modT[:, lc, kind_sh * 8 + c:kind_sh * 8 + c + 1]
                if c % 2 == 0:
                    fw.op(act, lambda: S.activation(out=hT[:, c, 0:N], in_=psum[pi][:, 0:N], func=AF.Identity, scale=sc_ap, bias=sh_ap),
                          reads=[bps[pi], bmodT], writes=[bhT])
                else:
                    fw.op(dve, lambda: V.tensor_scalar(out=hT[:, c, 0:N], in0=psum[pi][:, 0:N], scalar1=sc_ap, scalar2=sh_ap,
                                                       op0=ALU.mult, op1=ALU.add),
                          reads=[bps[pi], bmodT], writes=[bhT])

        def load_w_bf16(dst, bdst, src, kc, ncols, step=512):
            for c0 in range(0, ncols, step):
                c1 = min(ncols, c0 + step)
                fw.dma(pool, dst[:, 0:kc, c0:c1], src[:, c0:c1].rearrange("(c p) f -> p c f", p=128), writes=[bdst])

        with ExitStack() as st:
            st.enter_context(nc.named_scope("prologue"))
            crow = sb(st, "crow", [16, 128], F32)
            bcrow = Buf()
            fw.dma(sp, crow[:], cc.rearrange("r (c p) -> (r c) p", p=128), writes=[bcrow])
            fw.op(pe, lambda: T.transpose(psum[7][:, 0:16], crow[:], ident[0:16, 0:16]), reads=[bcrow, bconst], writes=[bps[7]])
            fw.op(act, lambda: S.activation(out=condT[:].rearrange("p c r -> p r c"),
                                            in_=psum[7][:, 0:16].rearrange("p (r c) -> p r c", r=2), func=AF.Silu),
                  reads=[bps[7]], writes=[bcond])
            wa = [sb(st, f"wa{i}", [128, DC, 512], F32) for i in range(2)]
            bwa = [Buf(), Buf()]
            modrow = sb(st, "modrow", [2, 6 * D], F32)
            bada = sb(st, "bada", [2, 6 * D], F32)
            bmr, bba = Buf(), Buf()
            k = 0
            for l in range(1):
                fw.dma(sp, bada[:], b_ada[l, :].partition_broadcast(2), writes=[bba])
                for cb in range(12):
                    i = k % 2
                    k += 1
                    fw.dma(sp, wa[i][:], w_ada[l][:, cb * 512:(cb + 1) * 512].rearrange("(c p) f -> p c f", p=128), writes=[bwa[i]])
                    pi = 4 + (k % 2)
                    mm_group(pi, psum[pi][0:2, :], [(condT[:, c, :], wa[i][:, c, :]) for c in range(DC)], [bcond, bwa[i]])
                    fw.op(dve, lambda: V.tensor_tensor(out=modrow[:, cb * 512:(cb + 1) * 512], in0=psum[pi][0:2, :],
                                                       in1=bada[:, cb * 512:(cb + 1) * 512], op=ALU.add),
                          reads=[bps[pi], bba], writes=[bmr])
                fw.dma(sp, modd[l], modrow[:], reads=[bmr], writes=[Buf()])
            fw.barrier()

        def layer_norm_tile(st_bufs, t, lg, lb, bbc):
            bn, mv, sd, bsm = st_bufs
            fw.op(dve, lambda: V.bn_stats(out=bn[:, 0, :], in_=x_sb[:, t, 0:512]), reads=[bx[t]], writes=[bsm])
            fw.op(dve, lambda: V.bn_stats(out=bn[:, 1, :], in_=x_sb[:, t, 512:1024]), reads=[bx[t]], writes=[bsm])
            fw.op(dve, lambda: V.bn_aggr(out=mv[:], in_=bn[:].rearrange("p a b -> p (a b)")), reads=[bsm], writes=[bsm])
            fw.op(act, lambda: S.activation(out=sd[:], in_=mv[:, 1:2], func=AF.Sqrt, bias=epsc[:, 0:1], scale=1.0),
                  reads=[bsm, bconst], writes=[bsm])
            fw.op(dve, lambda: V.reciprocal(out=sd[:], in_=sd[:]), reads=[bsm], writes=[bsm])
            fw.op(dve, lambda: V.tensor_scalar(out=x_sb[:, t, :], in0=x_sb[:, t, :], scalar1=mv[:, 0:1], scalar2=sd[:, 0:1],
                                               op0=ALU.subtract, op1=ALU.mult),
                  reads=[bx[t], bsm], writes=[bx[t]])
            fw.op(pool, lambda: G.tensor_tensor(out=x_sb[:, t, :], in0=x_sb[:, t, :], in1=lg[:], op=ALU.mult),
                  reads=[bx[t], bbc], writes=[bx[t]])
            fw.op(pool, lambda: G.tensor_tensor(out=x_sb[:, t, :], in0=x_sb[:, t, :], in1=lb[:], op=ALU.add),
                  reads=[bx[t], bbc], writes=[bx[t]])

        def load_bcast(stack, l, sub):
            gl = sb(stack, f"gl{sub}", [128, D], F32)
            gc = sb(stack, f"gc{sub}", [128, D], F32)
            lg = sb(stack, f"lg{sub}", [128, D], F32)
            lb = sb(stack, f"lb{sub}", [128, D], F32)
            bbc = Buf()
            goff = 2 * D if sub == 0 else 5 * D
            fw.dma(sp, gl[:], modd[l, 0, goff:goff + D].partition_broadcast(128), writes=[bbc])
            fw.dma(sp, gc[:], modd[l, 1, goff:goff + D].partition_broadcast(128), writes=[bbc])
            fw.dma(sp, lg[:], ln_g[l, sub, :].partition_broadcast(128), writes=[bbc])
            fw.dma(sp, lb[:], ln_b[l, sub, :].partition_broadcast(128), writes=[bbc])
            return gl, gc, lg, lb, bbc

        def attn_dense(pieces_fn, v_fn, key_tiles, N, scale, o_dst, bo_dst, reads, pbufs, rec, brec, par, psums):
            po, pr = 2 + par, 4 + par
            nk = len(key_tiles)

            def qk(i):
                pi = i % 2
                mm_group(pi, psum[pi][:, 0:N], pieces_fn(key_tiles[i]), reads)
            qk(0)
            for i, kt in enumerate(key_tiles):
                if i + 1 < nk:
                    qk(i + 1)
                pT, bp = pbufs.next()
                fw.op(act, lambda: S.activation(out=pT[:, 0:N], in_=psum[i % 2][:, 0:N], func=AF.Exp, scale=scale),
                      reads=[bps[i % 2]], writes=[bp])
                lv, lo = v_fn(kt)
                fw.op(pe, lambda: T.matmul(psum[po][:, 0:N], lhsT=lv, rhs=pT[:, 0:N], start=(i == 0), stop=(i == nk - 1)),
                      reads=reads + [bp], writes=[bps[po]])
                if i % 2 == 0:
                    prev = (pT, bp)
                else:
                    p2, bp2 = psums.next()
                    fw.op(dve, lambda: V.tensor_tensor(out=p2[:, 0:N], in0=prev[0][:, 0:N], in1=pT[:, 0:N], op=ALU.add),
                          reads=[prev[1], bp], writes=[bp2])
                    fw.op(pe, lambda: T.matmul(psum[pr][:, 0:N], lhsT=lo, rhs=p2[:, 0:N], start=(i == 1), stop=(i == nk - 1)),
                          reads=[bp2, bconst], writes=[bps[pr]])
            fw.op(dve, lambda: V.reciprocal(out=rec[:, 0:N], in_=psum[pr][:, 0:N]), reads=[bps[pr]], writes=[brec])
            fw.op(dve, lambda: V.tensor_tensor(out=o_dst, in0=psum[po][:, 0:N], in1=rec[:, 0:N], op=ALU.mult),
                  reads=[bps[po], brec], writes=[bo_dst])

        def make_mod_stepper(stack, layers):
            blocks = [(l_, cb) for l_ in layers for cb in range(12)]
            wa_ = [sb(stack, f"mwa{i}", [128, DC, 512], F32) for i in range(2)]
            bb_ = [sb(stack, f"mbb{i}", [2, 512], F32) for i in range(2)]
            mr_ = [sb(stack, f"mmr{i}", [2, 512], F32) for i in range(2)]
            bw_, bm_ = [Buf(), Buf()], [Buf(), Buf()]
            state = [0]

            def step():
                k = state[0]
                if k > len(blocks):
                    return False
                state[0] += 1
                if k < len(blocks):
                    l_, cb = blocks[k]
                    i = k % 2
                    fw.dma(pool, wa_[i][:], w_ada[l_][:, cb * 512:(cb + 1) * 512].rearrange("(c p) f -> p c f", p=128), writes=[bw_[i]])
                    fw.dma(pool, bb_[i][:], b_ada[l_, cb * 512:(cb + 1) * 512].partition_broadcast(2), writes=[bw_[i]])
                if k >= 1:
                    l_, cb = blocks[k - 1]
                    i = (k - 1) % 2
                    pi = 6 + i
                    mm_group(pi, psum[pi][0:2, :], [(condT[:, c, :], wa_[i][:, c, :]) for c in range(DC)], [bcond, bw_[i]])
                    fw.op(dve, lambda: V.tensor_tensor(out=mr_[i][:], in0=psum[pi][0:2, :], in1=bb_[i][:], op=ALU.add),
                          reads=[bps[pi], bw_[i]], writes=[bm_[i]])
                    fw.dma(sp, modd[l_][:, cb * 512:(cb + 1) * 512], mr_[i][:], reads=[bm_[i]], writes=[Buf()])
                return True
            return step

        for l in range(n_layers):
            kind = KINDS[l]
            j = l // 3
            need_ctx = l < DEPTH - 1
            sc = scr[l]
            q_chunks = CHUNKS if need_ctx else CHUNKS[:4]
            n_tiles = NT if need_ctx else 16
            qcols = n_tiles * 128

            with ExitStack() as st:
                st.enter_context(nc.named_scope(f"L{l}_mod"))
                mrow = sb(st, "mrow", [96, 128], F32)
                bm = Buf()
                fw.dma(sp, mrow[:], modd[l].rearrange("r (k p) -> (r k) p", p=128), writes=[bm])
                fw.op(pe, lambda: T.transpose(psum[7][:, 0:96], mrow[:], ident[0:96, 0:96]), reads=[bm, bconst], writes=[bps[7]])
                fw.op(dve, lambda: V.tensor_copy(out=modT[:].rearrange("p r k -> p (r k)"), in_=psum[7][:, 0:96]),
                      reads=[bps[7]], writes=[bmodT])
                for kk in (1, 4):
                    fw.op(dve, lambda: V.tensor_scalar_add(modT[:, :, kk * 8:kk * 8 + 8], modT[:, :, kk * 8:kk * 8 + 8], 1.0),
                          reads=[bmodT], writes=[bmodT])
                fw.barrier()

            with ExitStack() as st:
                st.enter_context(nc.named_scope(f"L{l}_A"))
                hTs = [sb(st, f"hT{i}", [128, DC, 512], BF16) for i in range(2)]
                hTr = Rot(list(zip(hTs, [Buf(), Buf()])))
                stg = Rot([(sb(st, f"stg{i}", [128, 512], BF16), Buf()) for i in range(4)])
                stv = Rot([(sb(st, f"stv{i}", [128, 1024], BF16), Buf()) for i in range(2)])
                f32t = lambda nm: (sb(st, nm, [128, 512], F32), Buf())
                if kind in ("A", "C"):
                    cosT = sb(st, "cosT", [128, TOK], F32)
                    sinT = sb(st, "sinT", [128, TOK], F32)
                    brope = Buf()
                    rsrc = ropeA if kind == "A" else ropeC
                    fw.dma(sp, cosT[:], rsrc[0], writes=[brope])
                    fw.dma(sp, sinT[:], rsrc[1], writes=[brope])
                    sqs = Rot([(sb(st, f"sq{i}", [128, 512], BF16), Buf()) for i in range(3)])
                    tsets = Rot([[f32t(f"t{nm}{i}") for nm in "ABUVR"] for i in range(2)])
                    tA, tB, tU, tV, tR = tsets.items[0]

                def rope_combine(pa, pb, N, c0, dst, bdst, rstd=None, gain=None, gain_sw=None):
                    cs, sn = cosT[:, c0:c0 + N], sinT[:, c0:c0 + N]
                    if rstd is not None:
                        fw.op(dve, lambda: V.tensor_tensor(out=tA[0][:, 0:N], in0=rstd[:, 0:N], in1=cs, op=ALU.mult),
                              reads=[tR[1], brope], writes=[tA[1]])
                        fw.op(dve, lambda: V.tensor_tensor(out=tB[0][:, 0:N], in0=rstd[:, 0:N], in1=sn, op=ALU.mult),
                              reads=[tR[1], brope], writes=[tB[1]])
                        fw.op(dve, lambda: V.scalar_tensor_tensor(out=tU[0][:, 0:N], in0=psum[pa][:, 0:N], scalar=gain, in1=tA[0][:, 0:N],
                                                                  op0=ALU.mult, op1=ALU.mult),
                              reads=[bps[pa], tA[1], bgain], writes=[tU[1]])
                        fw.op(dve, lambda: V.scalar_tensor_tensor(out=tV[0][:, 0:N], in0=psum[pb][:, 0:N], scalar=gain_sw, in1=tB[0][:, 0:N],
                                                                  op0=ALU.mult, op1=ALU.mult),
                              reads=[bps[pb], tB[1], bgain], writes=[tV[1]])
                    else:
                        fw.op(dve, lambda: V.tensor_tensor(out=tU[0][:, 0:N], in0=psum[pa][:, 0:N], in1=cs, op=ALU.mult),
                              reads=[bps[pa], brope], writes=[tU[1]])
                        fw.op(dve, lambda: V.tensor_tensor(out=tV[0][:, 0:N], in0=psum[pb][:, 0:N], in1=sn, op=ALU.mult),
                              reads=[bps[pb], brope], writes=[tV[1]])
                    fw.op(pool, lambda: G.tensor_tensor(out=dst[:, 0:N], in0=tU[0][:, 0:N], in1=tV[0][:, 0:N], op=ALU.add),
                          reads=[tU[1], tV[1]], writes=[bdst])

                def rstd_from(banks, N, nfeat):
                    sq_l = []
                    for pb_ in banks:
                        sq, bsq = sqs.next()
                        fw.op(act, lambda: S.activation(out=sq[:, 0:N], in_=psum[pb_][:, 0:N], func=AF.Square),
                              reads=[bps[pb_]], writes=[bsq])
                        sq_l.append((sq, bsq))
                    mm_group(5, psum[5][:, 0:N], [(ones[:], sq[:, 0:N]) for sq, _ in sq_l], [bconst] + [b for _, b in sq_l])
                    fw.op(act, lambda: S.activation(out=tR[0][:, 0:N], in_=psum[5][:, 0:N], func=AF.Sqrt, bias=epsc[:, 1:2], scale=1.0 / nfeat),
                          reads=[bps[5], bconst], writes=[tR[1]])
                    fw.op(dve, lambda: V.reciprocal(out=tR[0][:, 0:N], in_=tR[0][:, 0:N]), reads=[tR[1]], writes=[tR[1]])

                def evac_to(pi, N, dst_dram, bdst_dram, use_act):
                    s_, bs_ = stg.next()
                    if use_act:
                        fw.op(act, lambda: S.copy(out=s_[:, 0:N], in_=psum[pi][:, 0:N]), reads=[bps[pi]], writes=[bs_])
                    else:
                        fw.op(dve, lambda: V.tensor_copy(out=s_[:, 0:N], in_=psum[pi][:, 0:N]), reads=[bps[pi]], writes=[bs_])
                    fw.dma(sp, dst_dram, s_[:, 0:N], reads=[bs_], writes=[bdst_dram])

                def v_proj(hT, bhT, t0, n, wv_fn, kc, vcols, reads):
                    for t in range(n):
                        sv, bsv = stv.next()
                        for hf in range(0, vcols, 512):
                            w_ = min(512, vcols - hf)
                            pi = 6 + ((hf // 512) % 2)
                            mm_group(pi, psum[pi][:, 0:w_], [(hT[:, c, t * 128:(t + 1) * 128], wv_fn(c, hf, w_)) for c in range(kc)], reads + [bhT])
                            if (hf // 512) % 2 == 0:
                                fw.op(act, lambda: S.copy(out=sv[:, hf:hf + w_], in_=psum[pi][:, 0:w_]), reads=[bps[pi]], writes=[bsv])
                            else:
                                fw.op(dve, lambda: V.tensor_copy(out=sv[:, hf:hf + w_], in_=psum[pi][:, 0:w_]), reads=[bps[pi]], writes=[bsv])
                        for cp in range(vcols // 256):
                            fw.dma(sp, sc["Vin"][cp][(t0 + t) * 128:(t0 + t + 1) * 128, :], sv[:, cp * 256:(cp + 1) * 256], reads=[bsv], writes=[sc["bVin"]])

                if kind == "A":
                    wq = sb(st, "wq", [128, DC, 2816], BF16)
                    bwq = Buf()
                    load_w_bf16(wq, bwq, a_wqkv[j], DC, 2816)
                    gains, bgain = load_colvecs(st, "gainA", a_gain[j], 4)
                    for (t0, n) in CHUNKS:
                        N = n * 128
                        hT, bhT = hTr.next()
                        make_hT(hT, bhT, t0, n, 1, 0)
                        for slab in range(10):
                            isq = slab < 8
                            if isq and not need_ctx and t0 == 16:
                                continue
                            pa, pb = (0, 1) if slab % 2 == 0 else (2, 3)
                            tA, tB, tU, tV, tR = tsets.next()
                            mm_group(pa, psum[pa][:, 0:N], [(wq[:, c, slab * 128:(slab + 1) * 128], hT[:, c, 0:N]) for c in range(DC)], [bwq, bhT])
                            mm_group(pb, psum[pb][:, 0:N], [(wq[:, c, 1536 + slab * 128:1536 + (slab + 1) * 128], hT[:, c, 0:N]) for c in range(DC)], [bwq, bhT])
                            rstd_from([pa], N, 128.0)
                            s_, bs_ = stg.next()
                            go = 0 if isq else 2
                            rope_combine(pa, pb, N, t0 * 128, s_, bs_, rstd=tR[0], gain=gains[:, go:go + 1], gain_sw=gains[:, go + 1:go + 2])
                            if isq:
                                fw.dma(sp, sc["QT"][slab * 128:(slab + 1) * 128, t0 * 128:t0 * 128 + N], s_[:, 0:N], reads=[bs_], writes=[sc["bQT"]])
                            else:
                                g = slab - 8
                                fw.dma(sp, kt_in(sc, g)[:, t0 * 128:t0 * 128 + N], s_[:, 0:N], reads=[bs_], writes=[sc["bKTin"]])
                        v_proj(hT, bhT, t0, n, lambda c, hf, w_: wq[:, c, 1280 + hf:1280 + hf + w_], DC, 256, [bwq])
                elif kind == "B":
                    wq = sb(st, "wq", [128, DC, 3072], BF16)
                    bwq = Buf()
                    load_w_bf16(wq, bwq, b_wqkv[0], DC, 3072)
                    k_ = 0
                    for (t0, n) in CHUNKS:
                        N = n * 128
                        hT, bhT = hTr.next()
                        make_hT(hT, bhT, t0, n, 1, 0)
                        for slab in range(16):
                            pi = k_ % 6
                            k_ += 1
                            mm_group(pi, psum[pi][:, 0:N], [(wq[:, c, slab * 128:(slab + 1) * 128], hT[:, c, 0:N]) for c in range(DC)], [bwq, bhT])
                            if slab < 8:
                                evac_to(pi, N, sc["QT"][slab * 128:(slab + 1) * 128, t0 * 128:t0 * 128 + N], sc["bQT"], slab % 2 == 0)
                            else:
                                p_ = slab - 8
                                evac_to(pi, N, kt_in(sc, p_)[:, t0 * 128:t0 * 128 + N], sc["bKTin"], slab % 2 == 0)
                        v_proj(hT, bhT, t0, n, lambda c, hf, w_: wq[:, c, 2048 + hf:2048 + hf + w_], DC, 1024, [bwq])
                else:
                    wd = sb(st, "wd", [128, DC, 896], BF16)
                    wuq = sb(st, "wuq", [128, 3, 2048], BF16)
                    wukv = sb(st, "wukv", [128, 2, 2048], BF16)
                    bwd, bwuq, bwukv = Buf(), Buf(), Buf()
                    load_w_bf16(wd, bwd, c_wdqkv, DC, 896)
                    load_w_bf16(wuq, bwuq, c_wuq, 3, 2048)
                    load_w_bf16(wukv, bwukv, c_wukv, 2, 2048)
                    gains, bgain = load_colvecs(st, "gainC", c_gain, 5)
                    qln = [sb(st, f"qln{i}", [128, 3, 512], BF16) for i in range(2)]
                    kvn = [sb(st, f"kvn{i}", [128, 2, 512], BF16) for i in range(2)]
                    lat_r = Rot(list(zip(qln, kvn, [Buf(), Buf()], [Buf(), Buf()])))
                    for (t0, n) in CHUNKS:
                        N = n * 128
                        c0 = t0 * 128
                        hT, bhT = hTr.next()
                        make_hT(hT, bhT, t0, n, 1, 0)
                        ql, kv, bql, bkv = lat_r.next()
                        for s_i in range(3):
                            mm_group(s_i, psum[s_i][:, 0:N], [(wd[:, c, s_i * 128:(s_i + 1) * 128], hT[:, c, 0:N]) for c in range(DC)], [bwd, bhT])
                        tA, tB, tU, tV, tR = tsets.next()
                        rstd_from([0, 1, 2], N, 384.0)
                        for s_i in range(3):
                            fw.op(dve, lambda: V.scalar_tensor_tensor(out=ql[:, s_i, 0:N], in0=psum[s_i][:, 0:N], scalar=gains[:, s_i:s_i + 1],
                                                                      in1=tR[0][:, 0:N], op0=ALU.mult, op1=ALU.mult),
                                  reads=[bps[s_i], tR[1], bgain], writes=[bql])
                        for s_i in range(2):
                            mm_group(3 + s_i, psum[3 + s_i][:, 0:N], [(wd[:, c, 384 + s_i * 128:384 + (s_i + 1) * 128], hT[:, c, 0:N]) for c in range(DC)], [bwd, bhT])
                        tA, tB, tU, tV, tR = tsets.next()
                        rstd_from([3, 4], N, 256.0)
                        for s_i in range(2):
                            fw.op(dve, lambda: V.scalar_tensor_tensor(out=kv[:, s_i, 0:N], in0=psum[3 + s_i][:, 0:N], scalar=gains[:, 3 + s_i:4 + s_i],
                                                                      in1=tR[0][:, 0:N], op0=ALU.mult, op1=ALU.mult),
                                  reads=[bps[3 + s_i], tR[1], bgain], writes=[bkv])
                        mm_group(0, psum[0][:, 0:N], [(wd[:, c, 640:768], hT[:, c, 0:N]) for c in range(DC)], [bwd, bhT])
                        mm_group(1, psum[1][:, 0:N], [(wd[:, c, 768:896], hT[:, c, 0:N]) for c in range(DC)], [bwd, bhT])
                        s_, bs_ = stg.next()
                        tA, tB, tU, tV, tR = tsets.next()
                        rope_combine(0, 1, N, c0, s_, bs_)
                        fw.dma(sp, kt_in(sc, 8)[:, c0:c0 + N], s_[:, 0:N], reads=[bs_], writes=[sc["bKTin"]])
                        do_q = need_ctx or t0 != 16
                        k_ = 0
                        for h in range(8):
                            if do_q:
                                pi = 2 + (k_ % 4)
                                k_ += 1
                                mm_group(pi, psum[pi][:, 0:N], [(wuq[:, s_i, h * 128:(h + 1) * 128], ql[:, s_i, 0:N]) for s_i in range(3)], [bwuq, bql])
                                evac_to(pi, N, sc["QT"][h * 128:(h + 1) * 128, c0:c0 + N], sc["bQT"], k_ % 2 == 0)
                            pi = 2 + (k_ % 4)
                            k_ += 1
                            mm_group(pi, psum[pi][:, 0:N], [(wukv[:, s_i, h * 128:(h + 1) * 128], kv[:, s_i, 0:N]) for s_i in range(2)], [bwukv, bkv])
                            evac_to(pi, N, kt_in(sc, h)[:, c0:c0 + N], sc["bKTin"], k_ % 2 == 0)
                        if do_q:
                            for pr_ in range(4):
                                mm_group(0, psum[0][:, 0:N], [(wuq[:, s_i, 1024 + pr_ * 128:1024 + (pr_ + 1) * 128], ql[:, s_i, 0:N]) for s_i in range(3)], [bwuq, bql])
                                mm_group(1, psum[1][:, 0:N], [(wuq[:, s_i, 1536 + pr_ * 128:1536 + (pr_ + 1) * 128], ql[:, s_i, 0:N]) for s_i in range(3)], [bwuq, bql])
                                s_, bs_ = stg.next()
                                tA, tB, tU, tV, tR = tsets.next()
                                rope_combine(0, 1, N, c0, s_, bs_)
                                fw.dma(sp, sc["QT"][(8 + pr_) * 128:(9 + pr_) * 128, c0:c0 + N], s_[:, 0:N], reads=[bs_], writes=[sc["bQT"]])
                        v_proj(kv, bkv, t0, n, lambda c, hf, w_: wukv[:, c, 1024 + hf:1024 + hf + w_], 2, 1024, [bwukv])
                for a_, b_ in zip(sc["KTin"], sc["KTall"]):
                    fw.collective("AllGather", a_, b_, reads=[sc["bKTin"]], writes=[sc["bKTall"]])
                for a_, b_ in zip(sc["Vin"], sc["Vall"]):
                    fw.collective("AllGather", a_, b_, reads=[sc["bVin"]], writes=[sc["bVall"]])
                fw.barrier()

            with ExitStack() as stO:
                OT = sb(stO, "OT", [128, 8, TOK], BF16)
                bOT = [[Buf() for _ in range(5)] for _ in range(8)]
                with ExitStack() as st:
                    st.enter_context(nc.named_scope(f"L{l}_B"))
                    pbufs = Rot([(sb(st, f"pT{i}", [128, 512], BF16), Buf()) for i in range(6)])
                    psums = Rot([(sb(st, f"pS{i}", [128, 512], BF16), Buf()) for i in range(3)])
                    rec = sb(st, "rec", [128, 512], F32)
                    brec = Buf()
                    nk = NK[kind]
                    par = 0
                    mod_step = make_mod_stepper(st, list(range(1, n_layers))) if (l == 0 and n_layers > 1) else None
                    if kind in ("A", "C"):
                        nunits = 2 if kind == "A" else 8
                        KTs = Rot([(sb(st, f"KT{i}", [128, NKT * 128], BF16), Buf()) for i in range(2)])
                        Vs = Rot([(sb(st, f"Vt{i}", [128, NKT, 128], BF16), Buf()) for i in range(2)])
                        QTs = Rot([(sb(st, f"QTh{i}", [128, TOK], BF16), Buf()) for i in range(2)])
                        scale = 128.0 ** -0.5 if kind == "A" else 192.0 ** -0.5
                        if kind == "C":
                            KRs = [sb(st, f"KR{i}", [128, NKT * 128], BF16) for i in range(2)]
                            bKR = Buf()
                            fw.op(pool, lambda: G.memset(KRs[0][64:128, :], 0.0), writes=[bKR])
                            fw.op(pool, lambda: G.memset(KRs[1][0:64, :], 0.0), writes=[bKR])
                            for s_ in range(2):
                                fw.dma(sp, KRs[0][0:64, s_ * TOK:(s_ + 1) * TOK], kt_all(sc, s_, 8)[0:64, :],
                                       reads=[sc["bKTall"]], writes=[bKR])
                                fw.dma(sp, KRs[1][64:128, s_ * TOK:(s_ + 1) * TOK], kt_all(sc, s_, 8)[64:128, :],
                                       reads=[sc["bKTall"]], writes=[bKR])
                            QRs = Rot([(sb(st, f"QR{i}", [128, TOK], BF16), Buf()) for i in range(2)])
                        for u in range(nunits):
                            KT, bKT = KTs.next()
                            Vt, bVt = Vs.next()
                            for s_ in range(2):
                                fw.dma(sp, KT[:, s_ * TOK:(s_ + 1) * TOK], kt_all(sc, s_, u),
                                       reads=[sc["bKTall"]], writes=[bKT])
                                fw.dma(sp, Vt[:, s_ * NT:(s_ + 1) * NT, :],
                                       sc["Vall"][u // 2][s_ * TOK:(s_ + 1) * TOK, (u % 2) * 128:(u % 2) * 128 + 128].rearrange("(t p) c -> p t c", p=128),
                                       reads=[sc["bVall"]], writes=[bVt])
                            heads = [u * 4 + i for i in range(4)] if kind == "A" else [u]
                            for h in heads:
                                QT, bQT = QTs.next()
                                fw.dma(sp, QT[:, 0:qcols], sc["QT"][h * 128:(h + 1) * 128, 0:qcols], reads=[sc["bQT"]], writes=[bQT])
                                if kind == "C" and h % 2 == 0:
                                    QR, bQR = QRs.next()
                                    fw.dma(sp, QR[:], sc["QT"][(8 + h // 2) * 128:(9 + h // 2) * 128, :], reads=[sc["bQT"]], writes=[bQR])
                                for ci, (t0, n) in enumerate(q_chunks):
                                    N = n * 128
                                    c0 = t0 * 128
                                    kts = list(range(NKT)) if t0 != 16 else [16, 33]
                                    if mod_step is not None:
                                        mod_step()
                                    if kind == "A":
                                        pf = lambda kt: [(KT[:, kt * 128:(kt + 1) * 128], QT[:, c0:c0 + N])]
                                        rd = [bKT, bVt, bQT]
                                    else:
                                        KR = KRs[h % 2]
                                        pf = lambda kt: [(KT[:, kt * 128:(kt + 1) * 128], QT[:, c0:c0 + N]),
                                                         (KR[:, kt * 128:(kt + 1) * 128], QR[:, c0:c0 + N])]
                                        rd = [bKT, bVt, bQT, bKR, bQR]
                                    attn_dense(pf, lambda kt: (Vt[:, kt, :], ones[:]), kts, N, scale,
                                               OT[:, h, c0:c0 + N], bOT[h][ci], rd, pbufs, rec, brec, par, psums)
                                    par ^= 1
                    else:
                        KTp = Rot([(sb(st, f"KTp{i}", [128, 22 * 128], BF16), Buf()) for i in range(2)])
                        Vp = [(sb(st, f"Vp{i}", [128, 22, 2, 128], BF16), Buf()) for i in range(2)]
                        for vp_, bvp_ in Vp:
                            fw.op(pool, lambda: G.memset(vp_[:], 0.0), writes=[bvp_])
                        Vpr = Rot(Vp)
                        QTp = Rot([(sb(st, f"QTp{i}", [128, TOK], BF16), Buf()) for i in range(2)])
                        tabs = sb(st, "tabs", [128, 5, 12, 128], F32)
                        btab = [Buf() for _ in range(5)]
                        sbs = Rot([(sb(st, f"sbs{i}", [128, 512], F32), Buf()) for i in range(4)])
                        pT4 = Rot([(sb(st, f"pq{i}", [128, 4, 512], BF16), Buf()) for i in range(2)])
                        E_SRCS = [("all", 0, 14 * 128, 256, 0), ("in", None, 0, 2048, 2),
                                  ("all", 1, 0, 256, 18), ("all", 0, 2048, 128, 20), ("all", 1, 2048, 128, 21)]
                        for p_ in range(8):
                            KT, bKT = KTp.next()
                            Vt, bVt = Vpr.next()
                            QT, bQT = QTp.next()
                            for (srck, slot, col0, ncol, e0) in E_SRCS:
                                if srck == "in":
                                    src, bsrc = kt_in(sc, p_), sc["bKTin"]
                                else:
                                    src, bsrc = kt_all(sc, slot, p_), sc["bKTall"]
                                fw.dma(sp, KT[:, e0 * 128:e0 * 128 + ncol], src[:, col0:col0 + ncol], reads=[bsrc], writes=[bKT])
                                for s2 in range(2):
                                    hc = (2 * p_ + s2) * 64
                                    if srck == "in":
                                        vsrc, bvs, rr0 = sc["Vin"][hc // 256], sc["bVin"], col0
                                    else:
                                        vsrc, bvs, rr0 = sc["Vall"][hc // 256], sc["bVall"], slot * TOK + col0
                                    fw.dma(sp, Vt[:, e0:e0 + ncol // 128, s2, 64 * s2:64 * s2 + 64],
                                           vsrc[rr0:rr0 + ncol, hc % 256:hc % 256 + 64].rearrange("(t p) c -> p t c", p=128), reads=[bvs], writes=[bVt])
                            fw.dma(sp, QT[:], sc["QT"][p_ * 128:(p_ + 1) * 128, :], reads=[sc["bQT"]], writes=[bQT])
                            for cf in range(5):
                                fw.dma(sp, tabs[:, cf, :, :], b_tab[p_, cf], writes=[btab[cf]])
                            for il in range(NT):
                                c0 = il * 128
                                isctx = il == 16
                                cfg = 0 if il == 0 else 1 if il == 1 else 3 if il == 14 else 4 if il == 15 else 2
                                base = min(il, 14)
                                units = []
                                for s2 in range(2):
                                    if not isctx:
                                        for jj in range(6):
                                            units.append((s2 * 8 + jj, s2, base + jj))
                                    for c_ in range(2):
                                        units.append((s2 * 8 + 6 + c_, s2, 20 + c_))
                                last_in_bank = {}
                                for (uu, s2, e) in units:
                                    last_in_bank[uu // 4] = uu
                                for (uu, s2, e) in units:
                                    pi = uu // 4
                                    fw.op(pe, lambda: T.matmul(psum[pi][:, (uu % 4) * 128:(uu % 4 + 1) * 128],
                                                               lhsT=KT[64 * s2:64 * s2 + 64, e * 128:(e + 1) * 128],
                                                               rhs=QT[64 * s2:64 * s2 + 64, c0:c0 + 128], start=True, stop=True),
                                          reads=[bKT, bQT], writes=[bps[pi]], signal=(last_in_bank[pi] == uu))
                                pq, bpq = pT4.next()
                                for s2 in range(2):
                                    b0, b1 = 2 * s2, 2 * s2 + 1
                                    if not isctx:
                                        sb_, bsb_ = sbs.next()
                                        fw.op(dve, lambda: V.scalar_tensor_tensor(out=sb_[:], in0=psum[b0][:], scalar=0.125,
                                                                                  in1=tabs[:, cfg, s2 * 6:s2 * 6 + 4, :].rearrange("p u q -> p (u q)"),
                                                                                  op0=ALU.mult, op1=ALU.add),
                                              reads=[bps[b0], btab[cfg]], writes=[bsb_])
                                        fw.op(act, lambda: S.activation(out=pq[:, b0, :], in_=sb_[:], func=AF.Exp), reads=[bsb_], writes=[bpq])
                                        sb_, bsb_ = sbs.next()
                                        fw.op(dve, lambda: V.scalar_tensor_tensor(out=sb_[:, 0:256], in0=psum[b1][:, 0:256], scalar=0.125,
                                                                                  in1=tabs[:, cfg, s2 * 6 + 4:s2 * 6 + 6, :].rearrange("p u q -> p (u q)"),
                                                                                  op0=ALU.mult, op1=ALU.add),
                                              reads=[bps[b1], btab[cfg]], writes=[bsb_])
                                        fw.op(act, lambda: S.activation(out=pq[:, b1, 0:256], in_=sb_[:, 0:256], func=AF.Exp), reads=[bsb_], writes=[bpq])
                                    fw.op(act, lambda: S.activation(out=pq[:, b1, 256:512], in_=psum[b1][:, 256:512], func=AF.Exp, scale=0.125),
                                          reads=[bps[b1]], writes=[bpq])
                                po = 4 + par
                                par ^= 1
                                nu = len(units)
                                for i_, (uu, s2, e) in enumerate(units):
                                    fw.op(pe, lambda: T.matmul(psum[po][:, 0:128], lhsT=Vt[:, e, s2, :], rhs=pq[:, uu // 4, (uu % 4) * 128:(uu % 4 + 1) * 128],
                                                               start=(i_ == 0), stop=(i_ == nu - 1)),
                                          reads=[bVt, bpq], writes=[bps[po]], signal=False)
                                for i_, (uu, s2, e) in enumerate(units):
                                    fw.op(pe, lambda: T.matmul(psum[po][:, 128:256], lhsT=onesp[:, s2, :], rhs=pq[:, uu // 4, (uu % 4) * 128:(uu % 4 + 1) * 128],
                                                               start=(i_ == 0), stop=(i_ == nu - 1)),
                                          reads=[bconst, bpq], writes=[bps[po]], signal=(i_ == nu - 1))
                                fw.op(dve, lambda: V.reciprocal(out=rec[:, 0:128], in_=psum[po][:, 128:256]), reads=[bps[po]], writes=[brec])
                                fw.op(dve, lambda: V.tensor_tensor(out=OT[:, p_, c0:c0 + 128], in0=psum[po][:, 0:128], in1=rec[:, 0:128], op=ALU.mult),
                                      reads=[bps[po], brec], writes=[bOT[p_][min(il // 4, 4)]])
                    if mod_step is not None:
                        while mod_step():
                            pass
                    fw.barrier()

                with ExitStack() as st:
                    st.enter_context(nc.named_scope(f"L{l}_C"))
                    wo = sb(st, "wo", [128, DC, D], BF16)
                    bwo = Buf()
                    wsrc = {"A": a_wo[j], "B": b_wo[0], "C": c_wo[0]}[kind]
                    load_w_bf16(wo, bwo, wsrc, DC, D)
                    gl, gc, lg, lb, bbc = load_bcast(st, l, 0)
                    tmp = Rot([(sb(st, f"tmp{i}", [128, D], F32), Buf()) for i in range(2)])
                    stt = Rot([((sb(st, f"bn{i}", [128, 2, 6], F32), sb(st, f"mv{i}", [128, 2], F32), sb(st, f"sd{i}", [128, 1], F32), Buf())) for i in range(2)])
                    allO = [b for row in bOT for b in row]
                    for t in range(n_tiles):
                        gt = gc if t == 16 else gl
                        tm, btm = tmp.next()
                        for hf in range(2):
                            pi = 2 * (t % 2) + hf
                            mm_group(pi, psum[pi][:], [(OT[:, s_i, t * 128:(t + 1) * 128], wo[:, s_i, hf * 512:(hf + 1) * 512]) for s_i in range(8)], [bwo] + allO)
                            fw.op(dve, lambda: V.tensor_tensor(out=tm[:, hf * 512:(hf + 1) * 512], in0=psum[pi][:], in1=gt[:, hf * 512:(hf + 1) * 512], op=ALU.mult),
                                  reads=[bps[pi], bbc], writes=[btm])
                        fw.op(dve, lambda: V.scalar_tensor_tensor(out=x_sb[:, t, :], in0=x_sb[:, t, :], scalar=ALPHA, in1=tm[:], op0=ALU.mult, op1=ALU.add),
                              reads=[bx[t], btm], writes=[bx[t]])
                        layer_norm_tile(stt.next(), t, lg, lb, bbc)
                    fw.barrier()

            with ExitStack() as st:
                st.enter_context(nc.named_scope(f"L{l}_D"))
                h2 = sb(st, "h2", [128, DC, TOK], BF16)
                bh2 = [Buf() for _ in CHUNKS]
                gl, gc, lg, lb, bbc = load_bcast(st, l, 1)
                PARTS = [(0, 4), (4, 4), (8, 4), (12, 4), (16, 4), (20, 2)]
                wgu = Rot([(sb(st, f"wg{i}", [128, DC, 512], BF16), sb(st, f"wu{i}", [128, DC, 512], BF16), Buf()) for i in range(2)])
                wdn = Rot([(sb(st, f"wdn{i}", [128, 4, D], BF16), sb(st, f"wdc{i}", [128, 4, D], BF16), Buf()) for i in range(2)])
                aT = Rot([(sb(st, f"aT{i}", [128, 4, 512], BF16), Buf()) for i in range(2)])
                sg = Rot([(sb(st, f"sg{i}", [128, 512], F32), Buf()) for i in range(2)])
                stt = Rot([((sb(st, f"bn{i}", [128, 2, 6], F32), sb(st, f"mv{i}", [128, 2], F32), sb(st, f"sd{i}", [128, 1], F32), Buf())) for i in range(2)])
                chs = q_chunks
                for ci, (t0, n) in enumerate(chs):
                    hv = h2[:, :, t0 * 128:(t0 + n) * 128]
                    make_hT(hv, bh2[ci], t0, n, 4, 3)
                for pi_, (f0, nf) in enumerate(PARTS):
                    wg, wu, bwgu = wgu.next()
                    wd_, wdc, bwdn = wdn.next()
                    fcols = slice(f0 * 128, (f0 + nf) * 128)
                    fw.dma(pool, wg[:, :, 0:nf * 128], w_gate[l][:, fcols].rearrange("(c p) f -> p c f", p=128), writes=[bwgu])
                    fw.dma(pool, wu[:, :, 0:nf * 128], w_up[l][:, fcols].rearrange("(c p) f -> p c f", p=128), writes=[bwgu])
                    fw.dma(pool, wd_[:, 0:nf, :], w_down[l][f0 * 128:(f0 + nf) * 128, :].rearrange("(c p) f -> p c f", p=128), writes=[bwdn])
                    if need_ctx:
                        for fi in range(nf):
                            fw.op(pool, lambda: G.tensor_tensor(out=wdc[:, fi, :], in0=wd_[:, fi, :], in1=gc[:], op=ALU.mult),
                                  reads=[bwdn, bbc], writes=[bwdn])
                    for fi in range(nf):
                        fw.op(pool, lambda: G.tensor_tensor(out=wd_[:, fi, :], in0=wd_[:, fi, :], in1=gl[:], op=ALU.mult),
                              reads=[bwdn, bbc], writes=[bwdn])
                    for ci, (t0, n) in enumerate(chs):
                        N = n * 128
                        c0 = t0 * 128
                        a_, ba_ = aT.next()
                        for fi in range(nf):
                            pg, pu = (0, 1) if fi % 2 == 0 else (2, 3)
                            mm_group(pg, psum[pg][:, 0:N], [(wg[:, c, fi * 128:(fi + 1) * 128], h2[:, c, c0:c0 + N]) for c in range(DC)], [bwgu, bh2[ci]])
                            mm_group(pu, psum[pu][:, 0:N], [(wu[:, c, fi * 128:(fi + 1) * 128], h2[:, c, c0:c0 + N]) for c in range(DC)], [bwgu, bh2[ci]])
                            s_, bs_ = sg.next()
                            fw.op(act, lambda: S.activation(out=s_[:, 0:N], in_=psum[pg][:, 0:N], func=AF.Silu), reads=[bps[pg]], writes=[bs_])
                            fw.op(dve, lambda: V.tensor_tensor(out=a_[:, fi, 0:N], in0=psum[pu][:, 0:N], in1=s_[:, 0:N], op=ALU.mult),
                                  reads=[bps[pu], bs_], writes=[ba_])
                        for t in range(n):
                            tt = t0 + t
                            wsel = wdc if tt == 16 else wd_
                            for hf in range(2):
                                pd = 4 + 2 * (t % 2) + hf
                                mm_group(pd, psum[pd][:], [(a_[:, fi, t * 128:(t + 1) * 128], wsel[:, fi, hf * 512:(hf + 1) * 512]) for fi in range(nf)], [ba_, bwdn])
                                xs = x_sb[:, tt, hf * 512:(hf + 1) * 512]
                                if pi_ == 0:
                                    fw.op(dve, lambda: V.scalar_tensor_tensor(out=xs, in0=xs, scalar=ALPHA, in1=psum[pd][:], op0=ALU.mult, op1=ALU.add),
                                          reads=[bx[tt], bps[pd]], writes=[bx[tt]])
                                else:
                                    fw.op(dve, lambda: V.tensor_tensor(out=xs, in0=xs, in1=psum[pd][:], op=ALU.add),
                                          reads=[bx[tt], bps[pd]], writes=[bx[tt]])
                for t in range(n_tiles):
                    layer_norm_tile(stt.next(), t, lg, lb, bbc)
                fw.barrier()

        if dbg:
            fw.dma(sp, dbg_x.rearrange("(t p) d -> p t d", p=128), x_sb[:], reads=bx, writes=[Buf()])
        for (t0, n) in CHUNKS[:4]:
            fw.dma(sp, out[t0 * 128:(t0 + n) * 128, :].rearrange("(t p) d -> p t d", p=128), x_sb[:, t0:t0 + n, :],
                   reads=bx[t0:t0 + n], writes=[Buf()])
        fw.barrier()
    return nc


def _rope_tables(half, hd, dup):
    hh = hd // 2
    nf = hh // 2
    freqs = (10000.0 ** (-np.arange(0, hh, 2, dtype=np.float32) / np.float32(hh))).astype(np.float32)
    t = np.arange(half * 2048, (half + 1) * 2048)
    pos_r = (t // 64).astype(np.float32)
    pos_c = (t % 64).astype(np.float32)
    cos = np.ones((hd, TOK), np.float32)
    sin = np.zeros((hd, TOK), np.float32)
    for blk in range(4):
        pos = pos_r if blk < 2 else pos_c
        ang = (pos[None, :] * freqs[:, None]).astype(np.float32)
        cos[blk * nf:(blk + 1) * nf, :2048] = np.cos(ang)
        sgn = -1.0 if blk % 2 == 0 else 1.0
        sin[blk * nf:(blk + 1) * nf, :2048] = sgn * np.sin(ang)
    if dup:
        cos = np.concatenate([cos, cos], 0)
        sin = np.concatenate([sin, sin], 0)
    return np.ascontiguousarray(np.stack([cos, sin], 0))


def _swap_perm(hd):
    nf = hd // 4
    idx = np.arange(hd).reshape(4, nf)
    return idx[[1, 0, 3, 2]].reshape(-1)


def _nbr_tables(rpb, rank):
    tab = np.full((8, 5, 128, 12, 128), NEG, np.float32)
    kk = np.arange(128)
    qq = np.arange(128)
    cfg_il = [0, 1, 5, 14, 15]
    for cf, il in enumerate(cfg_il):
        base = min(il, 14)
        gq = rank * 16 + il
        qrow = 2 * gq + qq // 64
        qc = qq % 64
        rs = np.clip(qrow - 4, 0, 56)
        cs = np.clip(qc - 8, 0, 48)
        for slot in range(6):
            e = base + slot
            if e < 2:
                g = None if rank == 0 else 14 + e
            elif e < 18:
                g = rank * 16 + (e - 2)
            else:
                g = 16 + (e - 18) if rank == 0 else None
            if g is None:
                continue
            krow = 2 * g + kk // 64
            kc = kk % 64
            valid = ((krow[:, None] >= rs[None, :]) & (krow[:, None] < rs[None, :] + 8) &
                     (kc[:, None] >= cs[None, :]) & (kc[:, None] < cs[None, :] + 16))
            dr = np.clip(krow[:, None] - qrow[None, :] + 7, 0, 14)
            dc = np.clip(kc[:, None] - qc[None, :] + 15, 0, 30)
            for h in range(16):
                vals = rpb[h][dr, dc]
                tab[h // 2, cf, :, (h % 2) * 6 + slot, :] = np.where(valid, vals, np.float32(NEG))
    return tab


_PROG = {}


def _host_inputs(inp):
    f = lambda a: np.ascontiguousarray(np.asarray(a, dtype=np.float32))
    x, c, ctx, c_ctx = f(inp["x"]), f(inp["c"]), f(inp["ctx"]), f(inp["c_ctx"])
    p128, p64 = _swap_perm(128), _swap_perm(64)
    aw = f(inp["a_w_qkv"])
    qk = aw[:, :, :1280].reshape(2, D, 10, 128)
    a_wqkv = np.concatenate([aw, qk[:, :, :, p128].reshape(2, D, 1280)], axis=2)
    aq, ak = f(inp["a_q_gain"]), f(inp["a_k_gain"])
    a_gain = np.stack([aq, aq[:, p128], ak, ak[:, p128]], axis=1)
    cw = f(inp["c_w_dqkv"])[0]
    rope = cw[:, 640:704]
    c_wdqkv = np.concatenate([cw[:, :640], rope, rope, rope[:, p64], rope[:, p64]], axis=1)
    uq = f(inp["c_w_uq"])[0].reshape(384, 8, 192)
    nope = uq[:, :, :128].reshape(384, 1024)
    rp = uq[:, :, 128:]
    c_wuq = np.concatenate([nope, rp.reshape(384, 512), rp[:, :, p64].reshape(384, 512)], axis=1)
    ukv = f(inp["c_w_ukv"])[0].reshape(256, 8, 256)
    c_wukv = np.concatenate([ukv[:, :, :128].reshape(256, 1024), ukv[:, :, 128:].reshape(256, 1024)], axis=1)
    c_gain = np.concatenate([f(inp["c_q_a_gain"])[0].reshape(3, 128), f(inp["c_kv_a_gain"])[0].reshape(2, 128)], axis=0)
    shared = dict(
        w_ada=f(inp["w_ada"]), b_ada=f(inp["b_ada"]), ln_g=f(inp["ln_g"]), ln_b=f(inp["ln_b"]),
        w_ffn_gate=f(inp["w_ffn_gate"]), w_ffn_up=f(inp["w_ffn_up"]), w_ffn_down=f(inp["w_ffn_down"]),
        a_wqkv=np.ascontiguousarray(a_wqkv), a_gain=np.ascontiguousarray(a_gain), a_w_o=f(inp["a_w_o"]),
        b_w_qkv=f(inp["b_w_qkv"]), b_w_o=f(inp["b_w_o"]),
        c_wdqkv=np.ascontiguousarray(c_wdqkv), c_gain=np.ascontiguousarray(c_gain),
        c_wuq=np.ascontiguousarray(c_wuq), c_wukv=np.ascontiguousarray(c_wukv), c_w_o=f(inp["c_w_o"]))
    rpb = f(inp["b_rpb"])[0]
    per_rank = [dict(ropeA=_rope_tables(r, 128, False), ropeC=_rope_tables(r, 64, True), b_tab=_nbr_tables(rpb, r)) for r in range(2)]
    maps = []
    for core in range(8):
        b, r = core // 2, core % 2
        m = dict(shared)
        m.update(per_rank[r])
        m["xin"] = np.ascontiguousarray(np.concatenate([x[b, r * 2048:(r + 1) * 2048], ctx[b, r * 128:(r + 1) * 128]], axis=0))
        m["cc"] = np.ascontiguousarray(np.stack([c[b], c_ctx], axis=0))
        maps.append(m)
    return maps


def kernel(**inputs):
    maps = _host_inputs(inputs)
    if "nc" not in _PROG:
        _PROG["nc"] = build_program()
    res = run_bass_kernel_spmd(_PROG["nc"], maps, core_ids=list(range(8)))
    out = np.zeros((4, 4096, D), np.float32)
    for core in range(8):
        b, r = core // 2, core % 2
        out[b, r * 2048:(r + 1) * 2048] = np.asarray(res.results[core]["out"], dtype=np.float32)
    return out
```

```python
import math
from contextlib import ExitStack

import numpy as np
import concourse.bass as bass
import concourse.mybir as mybir
from concourse.bass_utils import run_bass_kernel_spmd

F32 = mybir.dt.float32
BF16 = mybir.dt.bfloat16
AF = mybir.ActivationFunctionType
ALU = mybir.AluOpType

D = 1024
DC = 8
NT = 17
TOK = NT * 128
NKT = 2 * NT
DFF = 2816
DEPTH = 4
ALPHA = (2.0 * DEPTH) ** 0.25
LN_EPS = 1e-5
RMS_EPS = 1e-6
GROUPS = [[0, 1], [2, 3], [4, 5], [6, 7]]
CHUNKS = [(0, 4), (4, 4), (8, 4), (12, 4), (16, 1)]
NEG = -30000.0


class Buf:
    __slots__ = ("w", "r", "excl")

    def __init__(self, excl=False):
        self.w = None
        self.r = {}
        self.excl = excl


class Eng:
    def __init__(self, name, handle, sem):
        self.name, self.h, self.sem, self.count, self.seen = name, handle, sem, 0, {}


class FW:
    def __init__(self, nc, stack, n_dma_sems=40):
        self.nc, self.stack = nc, stack
        mk = lambda n: stack.enter_context(nc.semaphore(n))
        self.pe = Eng("pe", nc.tensor, mk("s_pe"))
        self.act = Eng("act", nc.scalar, mk("s_act"))
        self.dve = Eng("dve", nc.vector, mk("s_dve"))
        self.pool = Eng("pool", nc.gpsimd, mk("s_pool"))
        self.sp = Eng("sp", nc.sync, mk("s_sp"))
        self.engines = [self.pe, self.act, self.dve, self.pool, self.sp]
        self.cc_sem, self.cc_count = mk("s_cc"), 0
        self.dma_pools = {"sp": [[mk(f"s_dh{i}"), 0] for i in range(n_dma_sems - 12)],
                          "pool": [[mk(f"s_ds{i}"), 0] for i in range(12)]}
        self.dma_sems = self.dma_pools["sp"] + self.dma_pools["pool"]
        self.dma_rr = {"sp": 0, "pool": 0}

    def _deps(self, e, reads, writes, skip_self):
        toks = {}

        def add(t):
            if t is None:
                return
            s, v = t
            if skip_self and s.num == e.sem.num:
                return
            if toks.get(s.num, (None, 0))[1] < v:
                toks[s.num] = (s, v)
        for b in reads:
            add(b.w)
            if b.excl:
                for t in b.r.values():
                    if t[0].num != e.sem.num:
                        add(t)
        for b in writes:
            add(b.w)
            for t in b.r.values():
                add(t)
        for s, v in toks.values():
            if e.seen.get(s.num, 0) < v:
                e.h.wait_ge(s, v)
                e.seen[s.num] = v

    @staticmethod
    def _mark(tok, reads, writes):
        s, v = tok
        for b in reads:
            o = b.r.get(s.num)
            if o is None or o[1] < v:
                b.r[s.num] = tok
        for b in writes:
            b.w = tok
            b.r = {}

    def op(self, e, fn, reads=(), writes=(), signal=True):
        self._deps(e, reads, writes, skip_self=(e is self.pe))
        ins = fn()
        if signal:
            e.count += 1
            ins.then_inc(e.sem, 1)
            tok = (e.sem, e.count)
        else:
            tok = (e.sem, e.count + 1)
        self._mark(tok, reads, writes)
        return ins

    def dma(self, q, out, in_, reads=(), writes=()):
        pl = self.dma_pools[q.name]
        slot = pl[self.dma_rr[q.name]]
        self.dma_rr[q.name] = (self.dma_rr[q.name] + 1) % len(pl)
        sem, val = slot
        self._deps(q, reads, writes, skip_self=False)
        if val > 0 and q.seen.get(sem.num, 0) < val:
            q.h.wait_ge(sem, val)
            q.seen[sem.num] = val
        q.h.dma_start(out=out, in_=in_).then_inc(sem, 16)
        slot[1] = val + 16
        self._mark((sem, val + 16), reads, writes)

    def collective(self, kind, in_ap, out_ap, reads=(), writes=()):
        q = self.pool
        self._deps(q, reads, writes, skip_self=False)
        ins = q.h.collective_compute(kind, ALU.bypass, replica_groups=GROUPS, ins=[in_ap], outs=[out_ap])
        self.cc_count += 1
        ins.then_inc(self.cc_sem, 1)
        self._mark((self.cc_sem, self.cc_count), reads, writes)

    def barrier(self):
        for e in self.engines:
            for o in self.engines:
                if o is not e and o.count > e.seen.get(o.sem.num, 0):
                    e.h.wait_ge(o.sem, o.count)
                    e.seen[o.sem.num] = o.count
            for sem, val in self.dma_sems:
                if val > e.seen.get(sem.num, 0):
                    e.h.wait_ge(sem, val)
                    e.seen[sem.num] = val
            if self.cc_count > e.seen.get(self.cc_sem.num, 0):
                e.h.wait_ge(self.cc_sem, self.cc_count)
                e.seen[self.cc_sem.num] = self.cc_count


class Rot:
    def __init__(self, items):
        self.items, self.i = items, 0

    def next(self):
        it = self.items[self.i]
        self.i = (self.i + 1) % len(self.items)
        return it


def build_program(n_layers=DEPTH, dbg=False):
    nc = bass.Bass("TRN2", target_bir_lowering=False)
    dt_in = lambda name, shape, dt=F32: nc.dram_tensor(name, list(shape), dt, kind="ExternalInput").ap()
    xin = dt_in("xin", [TOK, D])
    cc = dt_in("cc", [2, D])
    w_ada = dt_in("w_ada", [DEPTH, D, 6 * D])
    b_ada = dt_in("b_ada", [DEPTH, 6 * D])
    ln_g = dt_in("ln_g", [DEPTH, 2, D])
    ln_b = dt_in("ln_b", [DEPTH, 2, D])
    w_gate = dt_in("w_ffn_gate", [DEPTH, D, DFF])
    w_up = dt_in("w_ffn_up", [DEPTH, D, DFF])
    w_down = dt_in("w_ffn_down", [DEPTH, DFF, D])
    a_wqkv = dt_in("a_wqkv", [2, D, 2816])
    a_gain = dt_in("a_gain", [2, 4, 128])
    a_wo = dt_in("a_w_o", [2, D, D])
    b_wqkv = dt_in("b_w_qkv", [1, D, 3072])
    b_tab = dt_in("b_tab", [8, 5, 128, 12, 128])
    b_wo = dt_in("b_w_o", [1, D, D])
    c_wdqkv = dt_in("c_wdqkv", [D, 896])
    c_gain = dt_in("c_gain", [5, 128])
    c_wuq = dt_in("c_wuq", [384, 2048])
    c_wukv = dt_in("c_wukv", [256, 2048])
    c_wo = dt_in("c_w_o", [1, D, D])
    ropeA = dt_in("ropeA", [2, 128, TOK])
    ropeC = dt_in("ropeC", [2, 128, TOK])
    out = nc.dram_tensor("out", [2048, D], F32, kind="ExternalOutput").ap()
    if dbg:
        dbg_x = nc.dram_tensor("dbg_x", [TOK, D], F32, kind="ExternalOutput").ap()

    modd = nc.dram_tensor("modd", [DEPTH, 2, 6 * D], F32).ap()
    KINDS = ["A", "B", "C", "A"]
    NQ = {"A": 8, "B": 8, "C": 12}
    NK = {"A": 2, "B": 8, "C": 9}
    VC = {"A": 256, "B": 1024, "C": 1024}
    scr = []
    for l in range(DEPTH):
        k = KINDS[l]
        nkp = (NK[k] + 1) // 2
        krows = [min(2, NK[k] - 2 * i) * 128 for i in range(nkp)]
        nvp = VC[k] // 256
        d = dict(
            QT=nc.dram_tensor(f"QT{l}", [NQ[k] * 128, TOK], BF16).ap(),
            KTin=[nc.dram_tensor(f"KTin{l}_{i}", [krows[i], TOK], BF16).ap() for i in range(nkp)],
            KTall=[nc.dram_tensor(f"KTall{l}_{i}", [2 * krows[i], TOK], BF16).ap() for i in range(nkp)],
            Vin=[nc.dram_tensor(f"Vin{l}_{i}", [TOK, 256], BF16).ap() for i in range(nvp)],
            Vall=[nc.dram_tensor(f"Vall{l}_{i}", [2 * TOK, 256], BF16).ap() for i in range(nvp)],
            krows=krows, bQT=Buf(), bKTin=Buf(), bKTall=Buf(), bVin=Buf(), bVall=Buf())
        scr.append(d)

    def kt_in(sc, u):
        return sc["KTin"][u // 2][(u % 2) * 128:(u % 2) * 128 + 128, :]

    def kt_all(sc, s, u):
        r0 = s * sc["krows"][u // 2] + (u % 2) * 128
        return sc["KTall"][u // 2][r0:r0 + 128, :]

    with ExitStack() as top:
        fw = FW(nc, top)
        pe, act, dve, pool, sp = fw.pe, fw.act, fw.dve, fw.pool, fw.sp
        V, S, G, T = nc.vector, nc.scalar, nc.gpsimd, nc.tensor

        uid = [0]

        def sb(stack, name, shape, dt):
            uid[0] += 1
            return stack.enter_context(nc.sbuf_tensor(f"{name}_{uid[0]}", list(shape), dt))

        x_sb = sb(top, "x_sb", [128, NT, D], F32)
        bx = [Buf() for _ in range(NT)]
        ident = sb(top, "ident", [128, 128], F32)
        ones = sb(top, "ones", [128, 128], BF16)
        ones32 = sb(top, "ones32", [128, 128], F32)
        onesp = sb(top, "onesp", [128, 2, 128], BF16)
        epsc = sb(top, "epsc", [128, 2], F32)
        bconst = Buf()
        modT = sb(top, "modT", [128, 2, 48], F32)
        bmodT = Buf()
        condT = sb(top, "condT", [128, DC, 2], F32)
        bcond = Buf()
        psum = [top.enter_context(nc.psum_tensor(f"ps{i}", [128, 512], F32)) for i in range(8)]
        bps = [Buf(excl=True) for _ in range(8)]

        fw.op(pool, lambda: G.memset(ident[:], 0.0), writes=[bconst])
        fw.op(pool, lambda: G.affine_select(out=ident[:], in_=ident[:], compare_op=ALU.not_equal, fill=1.0,
                                            base=0, pattern=[[-1, 128]], channel_multiplier=1),
              reads=[bconst], writes=[bconst])
        fw.op(dve, lambda: V.memset(ones[:], 1.0), writes=[bconst])
        fw.op(dve, lambda: V.memset(ones32[:], 1.0), writes=[bconst])
        fw.op(dve, lambda: V.memset(onesp[:], 0.0), writes=[bconst])
        fw.op(dve, lambda: V.memset(onesp[:, 0, 0:64], 1.0), writes=[bconst])
        fw.op(dve, lambda: V.memset(onesp[:, 1, 64:128], 1.0), writes=[bconst])
        fw.op(dve, lambda: V.memset(epsc[:, 0:1], LN_EPS), writes=[bconst])
        fw.op(dve, lambda: V.memset(epsc[:, 1:2], RMS_EPS), writes=[bconst])
        for (t0, n) in CHUNKS:
            fw.dma(sp, x_sb[:, t0:t0 + n, :], xin[t0 * 128:(t0 + n) * 128, :].rearrange("(t p) d -> p t d", p=128),
                   writes=bx[t0:t0 + n])

        def mm_group(pi, out_ap, pairs, reads):
            n = len(pairs)
            for i, (l_ap, r_ap) in enumerate(pairs):
                fw.op(pe, lambda: T.matmul(out_ap, lhsT=l_ap, rhs=r_ap, start=(i == 0), stop=(i == n - 1)),
                      reads=reads, writes=[bps[pi]], signal=(i == n - 1))

        def load_colvecs(stack, name, rows_ap, n):
            tmp = sb(stack, name + "_r", [n, 128], F32)
            dst = sb(stack, name, [128, n], F32)
            b1, b2 = Buf(), Buf()
            fw.dma(sp, tmp[:], rows_ap, writes=[b1])
            fw.op(pe, lambda: T.transpose(psum[7][:, 0:n], tmp[:], ident[0:n, 0:n]), reads=[b1, bconst], writes=[bps[7]])
            fw.op(dve, lambda: V.tensor_copy(out=dst[:], in_=psum[7][:, 0:n]), reads=[bps[7]], writes=[b2])
            return dst, b2

        tr_banks = Rot([6, 7])

        def make_hT(hT, bhT, t0, n, kind_sc, kind_sh):
            lc = 1 if t0 == 16 else 0
            N = n * 128
            for c in range(DC):
                pi = tr_banks.next()
                for t in range(n):
                    fw.op(pe, lambda: T.transpose(psum[pi][:, t * 128:(t + 1) * 128], x_sb[:, t0 + t, c * 128:(c + 1) * 128], ident[:]),
                          reads=[bx[t0 + t], bconst], writes=[bps[pi]], signal=(t == n - 1))
                sc_ap = modT[:, lc, kind_sc * 8 + c:kind_sc * 8 + c + 1]
                sh_ap = modT[:, lc, kind_sh * 8 + c:kind_sh * 8 + c + 1]
                if c % 2 == 0:
                    fw.op(act, lambda: S.activation(out=hT[:, c, 0:N], in_=psum[pi][:, 0:N], func=AF.Identity, scale=sc_ap, bias=sh_ap),
                          reads=[bps[pi], bmodT], writes=[bhT])
                else:
                    fw.op(dve, lambda: V.tensor_scalar(out=hT[:, c, 0:N], in0=psum[pi][:, 0:N], scalar1=sc_ap, scalar2=sh_ap,
                                                       op0=ALU.mult, op1=ALU.add),
                          reads=[bps[pi], bmodT], writes=[bhT])

        def load_w_bf16(dst, bdst, src, kc, ncols, step=512):
            for c0 in range(0, ncols, step):
                c1 = min(ncols, c0 + step)
                fw.dma(pool, dst[:, 0:kc, c0:c1], src[:, c0:c1].rearrange("(c p) f -> p c f", p=128), writes=[bdst])

        with ExitStack() as st:
            st.enter_context(nc.named_scope("prologue"))
            crow = sb(st, "crow", [16, 128], F32)
            bcrow = Buf()
            fw.dma(sp, crow[:], cc.rearrange("r (c p) -> (r c) p", p=128), writes=[bcrow])
            fw.op(pe, lambda: T.transpose(psum[7][:, 0:16], crow[:], ident[0:16, 0:16]), reads=[bcrow, bconst], writes=[bps[7]])
            fw.op(act, lambda: S.activation(out=condT[:].rearrange("p c r -> p r c"),
                                            in_=psum[7][:, 0:16].rearrange("p (r c) -> p r c", r=2), func=AF.Silu),
                  reads=[bps[7]], writes=[bcond])
            wa = [sb(st, f"wa{i}", [128, DC, 512], F32) for i in range(2)]
            bwa = [Buf(), Buf()]
            modrow = sb(st, "modrow", [2, 6 * D], F32)
            bada = sb(st, "bada", [2, 6 * D], F32)
            bmr, bba = Buf(), Buf()
            k = 0
            for l in range(1):
                fw.dma(sp, bada[:], b_ada[l, :].partition_broadcast(2), writes=[bba])
                for cb in range(12):
                    i = k % 2
                    k += 1
                    fw.dma(sp, wa[i][:], w_ada[l][:, cb * 512:(cb + 1) * 512].rearrange("(c p) f -> p c f", p=128), writes=[bwa[i]])
                    pi = 4 + (k % 2)
                    mm_group(pi, psum[pi][0:2, :], [(condT[:, c, :], wa[i][:, c, :]) for c in range(DC)], [bcond, bwa[i]])
                    fw.op(dve, lambda: V.tensor_tensor(out=modrow[:, cb * 512:(cb + 1) * 512], in0=psum[pi][0:2, :],
                                                       in1=bada[:, cb * 512:(cb + 1) * 512], op=ALU.add),
                          reads=[bps[pi], bba], writes=[bmr])
                fw.dma(sp, modd[l], modrow[:], reads=[bmr], writes=[Buf()])
            fw.barrier()

        def layer_norm_tile(st_bufs, t, lg, lb, bbc):
            bn, mv, sd, bsm = st_bufs
            fw.op(dve, lambda: V.bn_stats(out=bn[:, 0, :], in_=x_sb[:, t, 0:512]), reads=[bx[t]], writes=[bsm])
            fw.op(dve, lambda: V.bn_stats(out=bn[:, 1, :], in_=x_sb[:, t, 512:1024]), reads=[bx[t]], writes=[bsm])
            fw.op(dve, lambda: V.bn_aggr(out=mv[:], in_=bn[:].rearrange("p a b -> p (a b)")), reads=[bsm], writes=[bsm])
            fw.op(act, lambda: S.activation(out=sd[:], in_=mv[:, 1:2], func=AF.Sqrt, bias=epsc[:, 0:1], scale=1.0),
                  reads=[bsm, bconst], writes=[bsm])
            fw.op(dve, lambda: V.reciprocal(out=sd[:], in_=sd[:]), reads=[bsm], writes=[bsm])
            fw.op(dve, lambda: V.tensor_scalar(out=x_sb[:, t, :], in0=x_sb[:, t, :], scalar1=mv[:, 0:1], scalar2=sd[:, 0:1],
                                               op0=ALU.subtract, op1=ALU.mult),
                  reads=[bx[t], bsm], writes=[bx[t]])
            fw.op(pool, lambda: G.tensor_tensor(out=x_sb[:, t, :], in0=x_sb[:, t, :], in1=lg[:], op=ALU.mult),
                  reads=[bx[t], bbc], writes=[bx[t]])
            fw.op(pool, lambda: G.tensor_tensor(out=x_sb[:, t, :], in0=x_sb[:, t, :], in1=lb[:], op=ALU.add),
                  reads=[bx[t], bbc], writes=[bx[t]])

        def load_bcast(stack, l, sub):
            gl = sb(stack, f"gl{sub}", [128, D], F32)
            gc = sb(stack, f"gc{sub}", [128, D], F32)
            lg = sb(stack, f"lg{sub}", [128, D], F32)
            lb = sb(stack, f"lb{sub}", [128, D], F32)
            bbc = Buf()
            goff = 2 * D if sub == 0 else 5 * D
            fw.dma(sp, gl[:], modd[l, 0, goff:goff + D].partition_broadcast(128), writes=[bbc])
            fw.dma(sp, gc[:], modd[l, 1, goff:goff + D].partition_broadcast(128), writes=[bbc])
            fw.dma(sp, lg[:], ln_g[l, sub, :].partition_broadcast(128), writes=[bbc])
            fw.dma(sp, lb[:], ln_b[l, sub, :].partition_broadcast(128), writes=[bbc])
            return gl, gc, lg, lb, bbc

        def attn_dense(pieces_fn, v_fn, key_tiles, N, scale, o_dst, bo_dst, reads, pbufs, rec, brec, par, psums):
            po, pr = 2 + par, 4 + par
            nk = len(key_tiles)

            def qk(i):
                pi = i % 2
                mm_group(pi, psum[pi][:, 0:N], pieces_fn(key_tiles[i]), reads)
            qk(0)
            pending = []
            for i, kt in enumerate(key_tiles):
                if i + 1 < nk:
                    qk(i + 1)
                pT, bp = pbufs.next()
                fw.op(act, lambda: S.activation(out=pT[:, 0:N], in_=psum[i % 2][:, 0:N], func=AF.Exp, scale=scale),
                      reads=[bps[i % 2]], writes=[bp])
                lv, lo = v_fn(kt)
                fw.op(pe, lambda: T.matmul(psum[po][:, 0:N], lhsT=lv, rhs=pT[:, 0:N], start=(i == 0), stop=(i == nk - 1)),
                      reads=reads + [bp], writes=[bps[po]])
                if i % 2 == 0:
                    prev = (pT, bp)
                else:
                    p2, bp2 = psums.next()
                    fw.op(dve, lambda: V.tensor_tensor(out=p2[:, 0:N], in0=prev[0][:, 0:N], in1=pT[:, 0:N], op=ALU.add),
                          reads=[prev[1], bp], writes=[bp2])
                    pending.append((i + 2, p2, bp2, lo, i == 1, i == nk - 1))
                while pending and (pending[0][0] <= i or i == nk - 1):
                    _, q2, bq2, lo2, st_, sp_ = pending.pop(0)
                    fw.op(pe, lambda: T.matmul(psum[pr][:, 0:N], lhsT=lo2, rhs=q2[:, 0:N], start=st_, stop=sp_),
                          reads=[bq2, bconst], writes=[bps[pr]])
            fw.op(dve, lambda: V.reciprocal(out=rec[:, 0:N], in_=psum[pr][:, 0:N]), reads=[bps[pr]], writes=[brec])
            fw.op(dve, lambda: V.tensor_tensor(out=o_dst, in0=psum[po][:, 0:N], in1=rec[:, 0:N], op=ALU.mult),
                  reads=[bps[po], brec], writes=[bo_dst])

        def make_mod_stepper(stack, layers):
            blocks = [(l_, cb) for l_ in layers for cb in range(12)]
            wa_ = [sb(stack, f"mwa{i}", [128, DC, 512], F32) for i in range(2)]
            bb_ = [sb(stack, f"mbb{i}", [2, 512], F32) for i in range(2)]
            mr_ = [sb(stack, f"mmr{i}", [2, 512], F32) for i in range(2)]
            bw_, bm_ = [Buf(), Buf()], [Buf(), Buf()]
            state = [0]

            def step():
                k = state[0]
                if k > len(blocks):
                    return False
                state[0] += 1
                if k < len(blocks):
                    l_, cb = blocks[k]
                    i = k % 2
                    fw.dma(pool, wa_[i][:], w_ada[l_][:, cb * 512:(cb + 1) * 512].rearrange("(c p) f -> p c f", p=128), writes=[bw_[i]])
                    fw.dma(pool, bb_[i][:], b_ada[l_, cb * 512:(cb + 1) * 512].partition_broadcast(2), writes=[bw_[i]])
                if k >= 1:
                    l_, cb = blocks[k - 1]
                    i = (k - 1) % 2
                    pi = 6 + i
                    mm_group(pi, psum[pi][0:2, :], [(condT[:, c, :], wa_[i][:, c, :]) for c in range(DC)], [bcond, bw_[i]])
                    fw.op(dve, lambda: V.tensor_tensor(out=mr_[i][:], in0=psum[pi][0:2, :], in1=bb_[i][:], op=ALU.add),
                          reads=[bps[pi], bw_[i]], writes=[bm_[i]])
                    fw.dma(sp, modd[l_][:, cb * 512:(cb + 1) * 512], mr_[i][:], reads=[bm_[i]], writes=[Buf()])
                return True
            return step

        for l in range(n_layers):
            kind = KINDS[l]
            j = l // 3
            need_ctx = l < DEPTH - 1
            sc = scr[l]
            q_chunks = CHUNKS if need_ctx else CHUNKS[:4]
            n_tiles = NT if need_ctx else 16
            qcols = n_tiles * 128

            with ExitStack() as st:
                st.enter_context(nc.named_scope(f"L{l}_mod"))
                mrow = sb(st, "mrow", [96, 128], F32)
                bm = Buf()
                fw.dma(sp, mrow[:], modd[l].rearrange("r (k p) -> (r k) p", p=128), writes=[bm])
                fw.op(pe, lambda: T.transpose(psum[7][:, 0:96], mrow[:], ident[0:96, 0:96]), reads=[bm, bconst], writes=[bps[7]])
                fw.op(dve, lambda: V.tensor_copy(out=modT[:].rearrange("p r k -> p (r k)"), in_=psum[7][:, 0:96]),
                      reads=[bps[7]], writes=[bmodT])
                for kk in (1, 4):
                    fw.op(dve, lambda: V.tensor_scalar_add(modT[:, :, kk * 8:kk * 8 + 8], modT[:, :, kk * 8:kk * 8 + 8], 1.0),
                          reads=[bmodT], writes=[bmodT])
                fw.barrier()

            with ExitStack() as st:
                st.enter_context(nc.named_scope(f"L{l}_A"))
                hTs = [sb(st, f"hT{i}", [128, DC, 512], BF16) for i in range(2)]
                hTr = Rot(list(zip(hTs, [Buf(), Buf()])))
                stg = Rot([(sb(st, f"stg{i}", [128, 512], BF16), Buf()) for i in range(4)])
                stv = Rot([(sb(st, f"stv{i}", [128, 1024], BF16), Buf()) for i in range(2)])
                f32t = lambda nm: (sb(st, nm, [128, 512], F32), Buf())
                if kind in ("A", "C"):
                    cosT = sb(st, "cosT", [128, TOK], F32)
                    sinT = sb(st, "sinT", [128, TOK], F32)
                    brope = Buf()
                    rsrc = ropeA if kind == "A" else ropeC
                    fw.dma(sp, cosT[:], rsrc[0], writes=[brope])
                    fw.dma(sp, sinT[:], rsrc[1], writes=[brope])
                    sqs = Rot([(sb(st, f"sq{i}", [128, 512], BF16), Buf()) for i in range(3)])
                    tsets = Rot([[f32t(f"t{nm}{i}") for nm in "ABUVR"] for i in range(2)])
                    tA, tB, tU, tV, tR = tsets.items[0]

                def rope_combine(pa, pb, N, c0, dst, bdst, rstd=None, gain=None, gain_sw=None):
                    cs, sn = cosT[:, c0:c0 + N], sinT[:, c0:c0 + N]
                    if rstd is not None:
                        fw.op(dve, lambda: V.tensor_tensor(out=tA[0][:, 0:N], in0=rstd[:, 0:N], in1=cs, op=ALU.mult),
                              reads=[tR[1], brope], writes=[tA[1]])
                        fw.op(dve, lambda: V.tensor_tensor(out=tB[0][:, 0:N], in0=rstd[:, 0:N], in1=sn, op=ALU.mult),
                              reads=[tR[1], brope], writes=[tB[1]])
                        fw.op(dve, lambda: V.scalar_tensor_tensor(out=tU[0][:, 0:N], in0=psum[pa][:, 0:N], scalar=gain, in1=tA[0][:, 0:N],
                                                                  op0=ALU.mult, op1=ALU.mult),
                              reads=[bps[pa], tA[1], bgain], writes=[tU[1]])
                        fw.op(dve, lambda: V.scalar_tensor_tensor(out=tV[0][:, 0:N], in0=psum[pb][:, 0:N], scalar=gain_sw, in1=tB[0][:, 0:N],
                                                                  op0=ALU.mult, op1=ALU.mult),
                              reads=[bps[pb], tB[1], bgain], writes=[tV[1]])
                    else:
                        fw.op(dve, lambda: V.tensor_tensor(out=tU[0][:, 0:N], in0=psum[pa][:, 0:N], in1=cs, op=ALU.mult),
                              reads=[bps[pa], brope], writes=[tU[1]])
                        fw.op(dve, lambda: V.tensor_tensor(out=tV[0][:, 0:N], in0=psum[pb][:, 0:N], in1=sn, op=ALU.mult),
                              reads=[bps[pb], brope], writes=[tV[1]])
                    fw.op(pool, lambda: G.tensor_tensor(out=dst[:, 0:N], in0=tU[0][:, 0:N], in1=tV[0][:, 0:N], op=ALU.add),
                          reads=[tU[1], tV[1]], writes=[bdst])

                def rstd_from(banks, N, nfeat):
                    sq_l = []
                    for pb_ in banks:
                        sq, bsq = sqs.next()
                        fw.op(act, lambda: S.activation(out=sq[:, 0:N], in_=psum[pb_][:, 0:N], func=AF.Square),
                              reads=[bps[pb_]], writes=[bsq])
                        sq_l.append((sq, bsq))
                    mm_group(5, psum[5][:, 0:N], [(ones[:], sq[:, 0:N]) for sq, _ in sq_l], [bconst] + [b for _, b in sq_l])
                    fw.op(act, lambda: S.activation(out=tR[0][:, 0:N], in_=psum[5][:, 0:N], func=AF.Sqrt, bias=epsc[:, 1:2], scale=1.0 / nfeat),
                          reads=[bps[5], bconst], writes=[tR[1]])
                    fw.op(dve, lambda: V.reciprocal(out=tR[0][:, 0:N], in_=tR[0][:, 0:N]), reads=[tR[1]], writes=[tR[1]])

                def evac_to(pi, N, dst_dram, bdst_dram, use_act):
                    s_, bs_ = stg.next()
                    if use_act:
                        fw.op(act, lambda: S.copy(out=s_[:, 0:N], in_=psum[pi][:, 0:N]), reads=[bps[pi]], writes=[bs_])
                    else:
                        fw.op(dve, lambda: V.tensor_copy(out=s_[:, 0:N], in_=psum[pi][:, 0:N]), reads=[bps[pi]], writes=[bs_])
                    fw.dma(sp, dst_dram, s_[:, 0:N], reads=[bs_], writes=[bdst_dram])

                def v_proj(hT, bhT, t0, n, wv_fn, kc, vcols, reads):
                    for t in range(n):
                        sv, bsv = stv.next()
                        for hf in range(0, vcols, 512):
                            w_ = min(512, vcols - hf)
                            pi = 6 + ((hf // 512) % 2)
                            mm_group(pi, psum[pi][:, 0:w_], [(hT[:, c, t * 128:(t + 1) * 128], wv_fn(c, hf, w_)) for c in range(kc)], reads + [bhT])
                            if (hf // 512) % 2 == 0:
                                fw.op(act, lambda: S.copy(out=sv[:, hf:hf + w_], in_=psum[pi][:, 0:w_]), reads=[bps[pi]], writes=[bsv])
                            else:
                                fw.op(dve, lambda: V.tensor_copy(out=sv[:, hf:hf + w_], in_=psum[pi][:, 0:w_]), reads=[bps[pi]], writes=[bsv])
                        for cp in range(vcols // 256):
                            fw.dma(sp, sc["Vin"][cp][(t0 + t) * 128:(t0 + t + 1) * 128, :], sv[:, cp * 256:(cp + 1) * 256], reads=[bsv], writes=[sc["bVin"]])

                if kind == "A":
                    wq = sb(st, "wq", [128, DC, 2816], BF16)
                    bwq = Buf()
                    load_w_bf16(wq, bwq, a_wqkv[j], DC, 2816)
                    gains, bgain = load_colvecs(st, "gainA", a_gain[j], 4)
                    for (t0, n) in CHUNKS:
                        N = n * 128
                        hT, bhT = hTr.next()
                        make_hT(hT, bhT, t0, n, 1, 0)
                        for slab in range(10):
                            isq = slab < 8
                            if isq and not need_ctx and t0 == 16:
                                continue
                            pa, pb = (0, 1) if slab % 2 == 0 else (2, 3)
                            tA, tB, tU, tV, tR = tsets.next()
                            mm_group(pa, psum[pa][:, 0:N], [(wq[:, c, slab * 128:(slab + 1) * 128], hT[:, c, 0:N]) for c in range(DC)], [bwq, bhT])
                            mm_group(pb, psum[pb][:, 0:N], [(wq[:, c, 1536 + slab * 128:1536 + (slab + 1) * 128], hT[:, c, 0:N]) for c in range(DC)], [bwq, bhT])
                            rstd_from([pa], N, 128.0)
                            s_, bs_ = stg.next()
                            go = 0 if isq else 2
                            rope_combine(pa, pb, N, t0 * 128, s_, bs_, rstd=tR[0], gain=gains[:, go:go + 1], gain_sw=gains[:, go + 1:go + 2])
                            if isq:
                                fw.dma(sp, sc["QT"][slab * 128:(slab + 1) * 128, t0 * 128:t0 * 128 + N], s_[:, 0:N], reads=[bs_], writes=[sc["bQT"]])
                            else:
                                g = slab - 8
                                fw.dma(sp, kt_in(sc, g)[:, t0 * 128:t0 * 128 + N], s_[:, 0:N], reads=[bs_], writes=[sc["bKTin"]])
                        v_proj(hT, bhT, t0, n, lambda c, hf, w_: wq[:, c, 1280 + hf:1280 + hf + w_], DC, 256, [bwq])
                elif kind == "B":
                    wq = sb(st, "wq", [128, DC, 3072], BF16)
                    bwq = Buf()
                    load_w_bf16(wq, bwq, b_wqkv[0], DC, 3072)
                    k_ = 0
                    for (t0, n) in CHUNKS:
                        N = n * 128
                        hT, bhT = hTr.next()
                        make_hT(hT, bhT, t0, n, 1, 0)
                        for slab in range(16):
                            pi = k_ % 6
                            k_ += 1
                            mm_group(pi, psum[pi][:, 0:N], [(wq[:, c, slab * 128:(slab + 1) * 128], hT[:, c, 0:N]) for c in range(DC)], [bwq, bhT])
                            if slab < 8:
                                evac_to(pi, N, sc["QT"][slab * 128:(slab + 1) * 128, t0 * 128:t0 * 128 + N], sc["bQT"], slab % 2 == 0)
                            else:
                                p_ = slab - 8
                                evac_to(pi, N, kt_in(sc, p_)[:, t0 * 128:t0 * 128 + N], sc["bKTin"], slab % 2 == 0)
                        v_proj(hT, bhT, t0, n, lambda c, hf, w_: wq[:, c, 2048 + hf:2048 + hf + w_], DC, 1024, [bwq])
                else:
                    wd = sb(st, "wd", [128, DC, 896], BF16)
                    wuq = sb(st, "wuq", [128, 3, 2048], BF16)
                    wukv = sb(st, "wukv", [128, 2, 2048], BF16)
                    bwd, bwuq, bwukv = Buf(), Buf(), Buf()
                    load_w_bf16(wd, bwd, c_wdqkv, DC, 896)
                    load_w_bf16(wuq, bwuq, c_wuq, 3, 2048)
                    load_w_bf16(wukv, bwukv, c_wukv, 2, 2048)
                    gains, bgain = load_colvecs(st, "gainC", c_gain, 5)
                    qln = [sb(st, f"qln{i}", [128, 3, 512], BF16) for i in range(2)]
                    kvn = [sb(st, f"kvn{i}", [128, 2, 512], BF16) for i in range(2)]
                    lat_r = Rot(list(zip(qln, kvn, [Buf(), Buf()], [Buf(), Buf()])))
                    for (t0, n) in CHUNKS:
                        N = n * 128
                        c0 = t0 * 128
                        hT, bhT = hTr.next()
                        make_hT(hT, bhT, t0, n, 1, 0)
                        ql, kv, bql, bkv = lat_r.next()
                        for s_i in range(3):
                            mm_group(s_i, psum[s_i][:, 0:N], [(wd[:, c, s_i * 128:(s_i + 1) * 128], hT[:, c, 0:N]) for c in range(DC)], [bwd, bhT])
                        tA, tB, tU, tV, tR = tsets.next()
                        rstd_from([0, 1, 2], N, 384.0)
                        for s_i in range(3):
                            fw.op(dve, lambda: V.scalar_tensor_tensor(out=ql[:, s_i, 0:N], in0=psum[s_i][:, 0:N], scalar=gains[:, s_i:s_i + 1],
                                                                      in1=tR[0][:, 0:N], op0=ALU.mult, op1=ALU.mult),
                                  reads=[bps[s_i], tR[1], bgain], writes=[bql])
                        for s_i in range(2):
                            mm_group(3 + s_i, psum[3 + s_i][:, 0:N], [(wd[:, c, 384 + s_i * 128:384 + (s_i + 1) * 128], hT[:, c, 0:N]) for c in range(DC)], [bwd, bhT])
                        tA, tB, tU, tV, tR = tsets.next()
                        rstd_from([3, 4], N, 256.0)
                        for s_i in range(2):
                            fw.op(dve, lambda: V.scalar_tensor_tensor(out=kv[:, s_i, 0:N], in0=psum[3 + s_i][:, 0:N], scalar=gains[:, 3 + s_i:4 + s_i],
                                                                      in1=tR[0][:, 0:N], op0=ALU.mult, op1=ALU.mult),
                                  reads=[bps[3 + s_i], tR[1], bgain], writes=[bkv])
                        mm_group(0, psum[0][:, 0:N], [(wd[:, c, 640:768], hT[:, c, 0:N]) for c in range(DC)], [bwd, bhT])
                        mm_group(1, psum[1][:, 0:N], [(wd[:, c, 768:896], hT[:, c, 0:N]) for c in range(DC)], [bwd, bhT])
                        s_, bs_ = stg.next()
                        tA, tB, tU, tV, tR = tsets.next()
                        rope_combine(0, 1, N, c0, s_, bs_)
                        fw.dma(sp, kt_in(sc, 8)[:, c0:c0 + N], s_[:, 0:N], reads=[bs_], writes=[sc["bKTin"]])
                        do_q = need_ctx or t0 != 16
                        k_ = 0
                        for h in range(8):
                            if do_q:
                                pi = 2 + (k_ % 4)
                                k_ += 1
                                mm_group(pi, psum[pi][:, 0:N], [(wuq[:, s_i, h * 128:(h + 1) * 128], ql[:, s_i, 0:N]) for s_i in range(3)], [bwuq, bql])
                                evac_to(pi, N, sc["QT"][h * 128:(h + 1) * 128, c0:c0 + N], sc["bQT"], k_ % 2 == 0)
                            pi = 2 + (k_ % 4)
                            k_ += 1
                            mm_group(pi, psum[pi][:, 0:N], [(wukv[:, s_i, h * 128:(h + 1) * 128], kv[:, s_i, 0:N]) for s_i in range(2)], [bwukv, bkv])
                            evac_to(pi, N, kt_in(sc, h)[:, c0:c0 + N], sc["bKTin"], k_ % 2 == 0)
                        if do_q:
                            for pr_ in range(4):
                                mm_group(0, psum[0][:, 0:N], [(wuq[:, s_i, 1024 + pr_ * 128:1024 + (pr_ + 1) * 128], ql[:, s_i, 0:N]) for s_i in range(3)], [bwuq, bql])
                                mm_group(1, psum[1][:, 0:N], [(wuq[:, s_i, 1536 + pr_ * 128:1536 + (pr_ + 1) * 128], ql[:, s_i, 0:N]) for s_i in range(3)], [bwuq, bql])
                                s_, bs_ = stg.next()
                                tA, tB, tU, tV, tR = tsets.next()
                                rope_combine(0, 1, N, c0, s_, bs_)
                                fw.dma(sp, sc["QT"][(8 + pr_) * 128:(9 + pr_) * 128, c0:c0 + N], s_[:, 0:N], reads=[bs_], writes=[sc["bQT"]])
                        v_proj(kv, bkv, t0, n, lambda c, hf, w_: wukv[:, c, 1024 + hf:1024 + hf + w_], 2, 1024, [bwukv])
                for a_, b_ in zip(sc["KTin"], sc["KTall"]):
                    fw.collective("AllGather", a_, b_, reads=[sc["bKTin"]], writes=[sc["bKTall"]])
                for a_, b_ in zip(sc["Vin"], sc["Vall"]):
                    fw.collective("AllGather", a_, b_, reads=[sc["bVin"]], writes=[sc["bVall"]])
                fw.barrier()

            with ExitStack() as stO:
                OT = sb(stO, "OT", [128, 8, TOK], BF16)
                bOT = [[Buf() for _ in range(5)] for _ in range(8)]
                with ExitStack() as st:
                    st.enter_context(nc.named_scope(f"L{l}_B"))
                    pbufs = Rot([(sb(st, f"pT{i}", [128, 512], BF16), Buf()) for i in range(6)])
                    psums = Rot([(sb(st, f"pS{i}", [128, 512], BF16), Buf()) for i in range(4)])
                    rec = sb(st, "rec", [128, 512], F32)
                    brec = Buf()
                    nk = NK[kind]
                    par = 0
                    mod_step = make_mod_stepper(st, list(range(1, n_layers))) if (l == 0 and n_layers > 1) else None
                    if kind in ("A", "C"):
                        nunits = 2 if kind == "A" else 8
                        KTs = Rot([(sb(st, f"KT{i}", [128, NKT * 128], BF16), Buf()) for i in range(2)])
                        Vs = Rot([(sb(st, f"Vt{i}", [128, NKT, 128], BF16), Buf()) for i in range(2)])
                        QTs = Rot([(sb(st, f"QTh{i}", [128, TOK], BF16), Buf()) for i in range(2)])
                        scale = 128.0 ** -0.5 if kind == "A" else 192.0 ** -0.5
                        if kind == "C":
                            KRs = [sb(st, f"KR{i}", [128, NKT * 128], BF16) for i in range(2)]
                            bKR = Buf()
                            fw.op(pool, lambda: G.memset(KRs[0][64:128, :], 0.0), writes=[bKR])
                            fw.op(pool, lambda: G.memset(KRs[1][0:64, :], 0.0), writes=[bKR])
                            for s_ in range(2):
                                fw.dma(sp, KRs[0][0:64, s_ * TOK:(s_ + 1) * TOK], kt_all(sc, s_, 8)[0:64, :],
                                       reads=[sc["bKTall"]], writes=[bKR])
                                fw.dma(sp, KRs[1][64:128, s_ * TOK:(s_ + 1) * TOK], kt_all(sc, s_, 8)[64:128, :],
                                       reads=[sc["bKTall"]], writes=[bKR])
                            QRs = Rot([(sb(st, f"QR{i}", [128, TOK], BF16), Buf()) for i in range(2)])
                        for u in range(nunits):
                            KT, bKT = KTs.next()
                            Vt, bVt = Vs.next()
                            for s_ in range(2):
                                fw.dma(sp, KT[:, s_ * TOK:(s_ + 1) * TOK], kt_all(sc, s_, u),
                                       reads=[sc["bKTall"]], writes=[bKT])
                                fw.dma(sp, Vt[:, s_ * NT:(s_ + 1) * NT, :],
                                       sc["Vall"][u // 2][s_ * TOK:(s_ + 1) * TOK, (u % 2) * 128:(u % 2) * 128 + 128].rearrange("(t p) c -> p t c", p=128),
                                       reads=[sc["bVall"]], writes=[bVt])
                            heads = [u * 4 + i for i in range(4)] if kind == "A" else [u]
                            for h in heads:
                                QT, bQT = QTs.next()
                                fw.dma(sp, QT[:, 0:qcols], sc["QT"][h * 128:(h + 1) * 128, 0:qcols], reads=[sc["bQT"]], writes=[bQT])
                                if kind == "C" and h % 2 == 0:
                                    QR, bQR = QRs.next()
                                    fw.dma(sp, QR[:], sc["QT"][(8 + h // 2) * 128:(9 + h // 2) * 128, :], reads=[sc["bQT"]], writes=[bQR])
                                for ci, (t0, n) in enumerate(q_chunks):
                                    N = n * 128
                                    c0 = t0 * 128
                                    kts = list(range(NKT)) if t0 != 16 else [16, 33]
                                    if mod_step is not None:
                                        mod_step()
                                    if kind == "A":
                                        pf = lambda kt: [(KT[:, kt * 128:(kt + 1) * 128], QT[:, c0:c0 + N])]
                                        rd = [bKT, bVt, bQT]
                                    else:
                                        KR = KRs[h % 2]
                                        pf = lambda kt: [(KT[:, kt * 128:(kt + 1) * 128], QT[:, c0:c0 + N]),
                                                         (KR[:, kt * 128:(kt + 1) * 128], QR[:, c0:c0 + N])]
                                        rd = [bKT, bVt, bQT, bKR, bQR]
                                    attn_dense(pf, lambda kt: (Vt[:, kt, :], ones[:]), kts, N, scale,
                                               OT[:, h, c0:c0 + N], bOT[h][ci], rd, pbufs, rec, brec, par, psums)
                                    par ^= 1
                    else:
                        KTp = Rot([(sb(st, f"KTp{i}", [128, 22 * 128], BF16), Buf()) for i in range(2)])
                        Vp = [(sb(st, f"Vp{i}", [128, 22, 2, 128], BF16), Buf()) for i in range(2)]
                        for vp_, bvp_ in Vp:
                            fw.op(pool, lambda: G.memset(vp_[:], 0.0), writes=[bvp_])
                        Vpr = Rot(Vp)
                        QTp = Rot([(sb(st, f"QTp{i}", [128, TOK], BF16), Buf()) for i in range(2)])
                        tabs = sb(st, "tabs", [128, 5, 12, 128], F32)
                        btab = [Buf() for _ in range(5)]
                        sbs = Rot([(sb(st, f"sbs{i}", [128, 512], F32), Buf()) for i in range(4)])
                        pT4 = Rot([(sb(st, f"pq{i}", [128, 4, 512], BF16), Buf()) for i in range(2)])
                        E_SRCS = [("all", 0, 14 * 128, 256, 0), ("in", None, 0, 2048, 2),
                                  ("all", 1, 0, 256, 18), ("all", 0, 2048, 128, 20), ("all", 1, 2048, 128, 21)]
                        for p_ in range(8):
                            KT, bKT = KTp.next()
                            Vt, bVt = Vpr.next()
                            QT, bQT = QTp.next()
                            for (srck, slot, col0, ncol, e0) in E_SRCS:
                                if srck == "in":
                                    src, bsrc = kt_in(sc, p_), sc["bKTin"]
                                else:
                                    src, bsrc = kt_all(sc, slot, p_), sc["bKTall"]
                                fw.dma(sp, KT[:, e0 * 128:e0 * 128 + ncol], src[:, col0:col0 + ncol], reads=[bsrc], writes=[bKT])
                                for s2 in range(2):
                                    hc = (2 * p_ + s2) * 64
                                    if srck == "in":
                                        vsrc, bvs, rr0 = sc["Vin"][hc // 256], sc["bVin"], col0
                                    else:
                                        vsrc, bvs, rr0 = sc["Vall"][hc // 256], sc["bVall"], slot * TOK + col0
                                    fw.dma(sp, Vt[:, e0:e0 + ncol // 128, s2, 64 * s2:64 * s2 + 64],
                                           vsrc[rr0:rr0 + ncol, hc % 256:hc % 256 + 64].rearrange("(t p) c -> p t c", p=128), reads=[bvs], writes=[bVt])
                            fw.dma(sp, QT[:], sc["QT"][p_ * 128:(p_ + 1) * 128, :], reads=[sc["bQT"]], writes=[bQT])
                            for cf in range(5):
                                fw.dma(sp, tabs[:, cf, :, :], b_tab[p_, cf], writes=[btab[cf]])
                            for il in range(NT):
                                c0 = il * 128
                                isctx = il == 16
                                cfg = 0 if il == 0 else 1 if il == 1 else 3 if il == 14 else 4 if il == 15 else 2
                                base = min(il, 14)
                                units = []
                                for s2 in range(2):
                                    if not isctx:
                                        for jj in range(6):
                                            units.append((s2 * 8 + jj, s2, base + jj))
                                    for c_ in range(2):
                                        units.append((s2 * 8 + 6 + c_, s2, 20 + c_))
                                last_in_bank = {}
                                for (uu, s2, e) in units:
                                    last_in_bank[uu // 4] = uu
                                for (uu, s2, e) in units:
                                    pi = uu // 4
                                    fw.op(pe, lambda: T.matmul(psum[pi][:, (uu % 4) * 128:(uu % 4 + 1) * 128],
                                                               lhsT=KT[64 * s2:64 * s2 + 64, e * 128:(e + 1) * 128],
                                                               rhs=QT[64 * s2:64 * s2 + 64, c0:c0 + 128], start=True, stop=True),
                                          reads=[bKT, bQT], writes=[bps[pi]], signal=(last_in_bank[pi] == uu))
                                pq, bpq = pT4.next()
                                for s2 in range(2):
                                    b0, b1 = 2 * s2, 2 * s2 + 1
                                    if not isctx:
                                        sb_, bsb_ = sbs.next()
                                        fw.op(dve, lambda: V.scalar_tensor_tensor(out=sb_[:], in0=psum[b0][:], scalar=0.125,
                                                                                  in1=tabs[:, cfg, s2 * 6:s2 * 6 + 4, :].rearrange("p u q -> p (u q)"),
                                                                                  op0=ALU.mult, op1=ALU.add),
                                              reads=[bps[b0], btab[cfg]], writes=[bsb_])
                                        fw.op(act, lambda: S.activation(out=pq[:, b0, :], in_=sb_[:], func=AF.Exp), reads=[bsb_], writes=[bpq])
                                        sb_, bsb_ = sbs.next()
                                        fw.op(dve, lambda: V.scalar_tensor_tensor(out=sb_[:, 0:256], in0=psum[b1][:, 0:256], scalar=0.125,
                                                                                  in1=tabs[:, cfg, s2 * 6 + 4:s2 * 6 + 6, :].rearrange("p u q -> p (u q)"),
                                                                                  op0=ALU.mult, op1=ALU.add),
                                              reads=[bps[b1], btab[cfg]], writes=[bsb_])
                                        fw.op(act, lambda: S.activation(out=pq[:, b1, 0:256], in_=sb_[:, 0:256], func=AF.Exp), reads=[bsb_], writes=[bpq])
                                    fw.op(act, lambda: S.activation(out=pq[:, b1, 256:512], in_=psum[b1][:, 256:512], func=AF.Exp, scale=0.125),
                                          reads=[bps[b1]], writes=[bpq])
                                po = 4 + par
                                par ^= 1
                                nu = len(units)
                                for i_, (uu, s2, e) in enumerate(units):
                                    fw.op(pe, lambda: T.matmul(psum[po][:, 0:128], lhsT=Vt[:, e, s2, :], rhs=pq[:, uu // 4, (uu % 4) * 128:(uu % 4 + 1) * 128],
                                                               start=(i_ == 0), stop=(i_ == nu - 1)),
                                          reads=[bVt, bpq], writes=[bps[po]], signal=False)
                                for i_, (uu, s2, e) in enumerate(units):
                                    fw.op(pe, lambda: T.matmul(psum[po][:, 128:256], lhsT=onesp[:, s2, :], rhs=pq[:, uu // 4, (uu % 4) * 128:(uu % 4 + 1) * 128],
                                                               start=(i_ == 0), stop=(i_ == nu - 1)),
                                          reads=[bconst, bpq], writes=[bps[po]], signal=(i_ == nu - 1))
                                fw.op(dve, lambda: V.reciprocal(out=rec[:, 0:128], in_=psum[po][:, 128:256]), reads=[bps[po]], writes=[brec])
                                fw.op(dve, lambda: V.tensor_tensor(out=OT[:, p_, c0:c0 + 128], in0=psum[po][:, 0:128], in1=rec[:, 0:128], op=ALU.mult),
                                      reads=[bps[po], brec], writes=[bOT[p_][min(il // 4, 4)]])
                    if mod_step is not None:
                        while mod_step():
                            pass
                    fw.barrier()

                with ExitStack() as st:
                    st.enter_context(nc.named_scope(f"L{l}_C"))
                    wo = sb(st, "wo", [128, DC, D], BF16)
                    bwo = Buf()
                    wsrc = {"A": a_wo[j], "B": b_wo[0], "C": c_wo[0]}[kind]
                    load_w_bf16(wo, bwo, wsrc, DC, D)
                    gl, gc, lg, lb, bbc = load_bcast(st, l, 0)
                    tmp = Rot([(sb(st, f"tmp{i}", [128, D], F32), Buf()) for i in range(2)])
                    stt = Rot([((sb(st, f"bn{i}", [128, 2, 6], F32), sb(st, f"mv{i}", [128, 2], F32), sb(st, f"sd{i}", [128, 1], F32), Buf())) for i in range(2)])
                    allO = [b for row in bOT for b in row]
                    for t in range(n_tiles):
                        gt = gc if t == 16 else gl
                        tm, btm = tmp.next()
                        for hf in range(2):
                            pi = 2 * (t % 2) + hf
                            mm_group(pi, psum[pi][:], [(OT[:, s_i, t * 128:(t + 1) * 128], wo[:, s_i, hf * 512:(hf + 1) * 512]) for s_i in range(8)], [bwo] + allO)
                            fw.op(dve, lambda: V.tensor_tensor(out=tm[:, hf * 512:(hf + 1) * 512], in0=psum[pi][:], in1=gt[:, hf * 512:(hf + 1) * 512], op=ALU.mult),
                                  reads=[bps[pi], bbc], writes=[btm])
                        fw.op(dve, lambda: V.scalar_tensor_tensor(out=x_sb[:, t, :], in0=x_sb[:, t, :], scalar=ALPHA, in1=tm[:], op0=ALU.mult, op1=ALU.add),
                              reads=[bx[t], btm], writes=[bx[t]])
                        layer_norm_tile(stt.next(), t, lg, lb, bbc)
                    fw.barrier()

            with ExitStack() as st:
                st.enter_context(nc.named_scope(f"L{l}_D"))
                h2 = sb(st, "h2", [128, DC, TOK], BF16)
                bh2 = [Buf() for _ in CHUNKS]
                gl, gc, lg, lb, bbc = load_bcast(st, l, 1)
                PARTS = [(0, 4), (4, 4), (8, 4), (12, 4), (16, 4), (20, 2)]
                wgu = Rot([(sb(st, f"wg{i}", [128, DC, 512], BF16), sb(st, f"wu{i}", [128, DC, 512], BF16), Buf()) for i in range(2)])
                wdn = Rot([(sb(st, f"wdn{i}", [128, 4, D], BF16), sb(st, f"wdc{i}", [128, 4, D], BF16), Buf()) for i in range(2)])
                aT = Rot([(sb(st, f"aT{i}", [128, 4, 512], BF16), Buf()) for i in range(2)])
                sg = Rot([(sb(st, f"sg{i}", [128, 512], F32), Buf()) for i in range(2)])
                stt = Rot([((sb(st, f"bn{i}", [128, 2, 6], F32), sb(st, f"mv{i}", [128, 2], F32), sb(st, f"sd{i}", [128, 1], F32), Buf())) for i in range(2)])
                chs = q_chunks
                for ci, (t0, n) in enumerate(chs):
                    hv = h2[:, :, t0 * 128:(t0 + n) * 128]
                    make_hT(hv, bh2[ci], t0, n, 4, 3)
                for pi_, (f0, nf) in enumerate(PARTS):
                    wg, wu, bwgu = wgu.next()
                    wd_, wdc, bwdn = wdn.next()
                    fcols = slice(f0 * 128, (f0 + nf) * 128)
                    fw.dma(pool, wg[:, :, 0:nf * 128], w_gate[l][:, fcols].rearrange("(c p) f -> p c f", p=128), writes=[bwgu])
                    fw.dma(pool, wu[:, :, 0:nf * 128], w_up[l][:, fcols].rearrange("(c p) f -> p c f", p=128), writes=[bwgu])
                    fw.dma(pool, wd_[:, 0:nf, :], w_down[l][f0 * 128:(f0 + nf) * 128, :].rearrange("(c p) f -> p c f", p=128), writes=[bwdn])
                    if need_ctx:
                        for fi in range(nf):
                            fw.op(pool, lambda: G.tensor_tensor(out=wdc[:, fi, :], in0=wd_[:, fi, :], in1=gc[:], op=ALU.mult),
                                  reads=[bwdn, bbc], writes=[bwdn])
                    for fi in range(nf):
                        fw.op(pool, lambda: G.tensor_tensor(out=wd_[:, fi, :], in0=wd_[:, fi, :], in1=gl[:], op=ALU.mult),
                              reads=[bwdn, bbc], writes=[bwdn])
                    for ci, (t0, n) in enumerate(chs):
                        N = n * 128
                        c0 = t0 * 128
                        a_, ba_ = aT.next()
                        for fi in range(nf):
                            pg, pu = (0, 1) if fi % 2 == 0 else (2, 3)
                            mm_group(pg, psum[pg][:, 0:N], [(wg[:, c, fi * 128:(fi + 1) * 128], h2[:, c, c0:c0 + N]) for c in range(DC)], [bwgu, bh2[ci]])
                            mm_group(pu, psum[pu][:, 0:N], [(wu[:, c, fi * 128:(fi + 1) * 128], h2[:, c, c0:c0 + N]) for c in range(DC)], [bwgu, bh2[ci]])
                            s_, bs_ = sg.next()
                            fw.op(act, lambda: S.activation(out=s_[:, 0:N], in_=psum[pg][:, 0:N], func=AF.Silu), reads=[bps[pg]], writes=[bs_])
                            fw.op(dve, lambda: V.tensor_tensor(out=a_[:, fi, 0:N], in0=psum[pu][:, 0:N], in1=s_[:, 0:N], op=ALU.mult),
                                  reads=[bps[pu], bs_], writes=[ba_])
                        for t in range(n):
                            tt = t0 + t
                            wsel = wdc if tt == 16 else wd_
                            for hf in range(2):
                                pd = 4 + 2 * (t % 2) + hf
                                mm_group(pd, psum[pd][:], [(a_[:, fi, t * 128:(t + 1) * 128], wsel[:, fi, hf * 512:(hf + 1) * 512]) for fi in range(nf)], [ba_, bwdn])
                                xs = x_sb[:, tt, hf * 512:(hf + 1) * 512]
                                if pi_ == 0:
                                    fw.op(dve, lambda: V.scalar_tensor_tensor(out=xs, in0=xs, scalar=ALPHA, in1=psum[pd][:], op0=ALU.mult, op1=ALU.add),
                                          reads=[bx[tt], bps[pd]], writes=[bx[tt]])
                                else:
                                    fw.op(dve, lambda: V.tensor_tensor(out=xs, in0=xs, in1=psum[pd][:], op=ALU.add),
                                          reads=[bx[tt], bps[pd]], writes=[bx[tt]])
                for t in range(n_tiles):
                    layer_norm_tile(stt.next(), t, lg, lb, bbc)
                fw.barrier()

        if dbg:
            fw.dma(sp, dbg_x.rearrange("(t p) d -> p t d", p=128), x_sb[:], reads=bx, writes=[Buf()])
        for (t0, n) in CHUNKS[:4]:
            fw.dma(sp, out[t0 * 128:(t0 + n) * 128, :].rearrange("(t p) d -> p t d", p=128), x_sb[:, t0:t0 + n, :],
                   reads=bx[t0:t0 + n], writes=[Buf()])
        fw.barrier()
    return nc


def _rope_tables(half, hd, dup):
    hh = hd // 2
    nf = hh // 2
    freqs = (10000.0 ** (-np.arange(0, hh, 2, dtype=np.float32) / np.float32(hh))).astype(np.float32)
    t = np.arange(half * 2048, (half + 1) * 2048)
    pos_r = (t // 64).astype(np.float32)
    pos_c = (t % 64).astype(np.float32)
    cos = np.ones((hd, TOK), np.float32)
    sin = np.zeros((hd, TOK), np.float32)
    for blk in range(4):
        pos = pos_r if blk < 2 else pos_c
        ang = (pos[None, :] * freqs[:, None]).astype(np.float32)
        cos[blk * nf:(blk + 1) * nf, :2048] = np.cos(ang)
        sgn = -1.0 if blk % 2 == 0 else 1.0
        sin[blk * nf:(blk + 1) * nf, :2048] = sgn * np.sin(ang)
    if dup:
        cos = np.concatenate([cos, cos], 0)
        sin = np.concatenate([sin, sin], 0)
    return np.ascontiguousarray(np.stack([cos, sin], 0))


def _swap_perm(hd):
    nf = hd // 4
    idx = np.arange(hd).reshape(4, nf)
    return idx[[1, 0, 3, 2]].reshape(-1)


def _nbr_tables(rpb, rank):
    tab = np.full((8, 5, 128, 12, 128), NEG, np.float32)
    kk = np.arange(128)
    qq = np.arange(128)
    cfg_il = [0, 1, 5, 14, 15]
    for cf, il in enumerate(cfg_il):
        base = min(il, 14)
        gq = rank * 16 + il
        qrow = 2 * gq + qq // 64
        qc = qq % 64
        rs = np.clip(qrow - 4, 0, 56)
        cs = np.clip(qc - 8, 0, 48)
        for slot in range(6):
            e = base + slot
            if e < 2:
                g = None if rank == 0 else 14 + e
            elif e < 18:
                g = rank * 16 + (e - 2)
            else:
                g = 16 + (e - 18) if rank == 0 else None
            if g is None:
                continue
            krow = 2 * g + kk // 64
            kc = kk % 64
            valid = ((krow[:, None] >= rs[None, :]) & (krow[:, None] < rs[None, :] + 8) &
                     (kc[:, None] >= cs[None, :]) & (kc[:, None] < cs[None, :] + 16))
            dr = np.clip(krow[:, None] - qrow[None, :] + 7, 0, 14)
            dc = np.clip(kc[:, None] - qc[None, :] + 15, 0, 30)
            for h in range(16):
                vals = rpb[h][dr, dc]
                tab[h // 2, cf, :, (h % 2) * 6 + slot, :] = np.where(valid, vals, np.float32(NEG))
    return tab


_PROG = {}


def _host_inputs(inp):
    f = lambda a: np.ascontiguousarray(np.asarray(a, dtype=np.float32))
    x, c, ctx, c_ctx = f(inp["x"]), f(inp["c"]), f(inp["ctx"]), f(inp["c_ctx"])
    p128, p64 = _swap_perm(128), _swap_perm(64)
    aw = f(inp["a_w_qkv"])
    qk = aw[:, :, :1280].reshape(2, D, 10, 128)
    a_wqkv = np.concatenate([aw, qk[:, :, :, p128].reshape(2, D, 1280)], axis=2)
    aq, ak = f(inp["a_q_gain"]), f(inp["a_k_gain"])
    a_gain = np.stack([aq, aq[:, p128], ak, ak[:, p128]], axis=1)
    cw = f(inp["c_w_dqkv"])[0]
    rope = cw[:, 640:704]
    c_wdqkv = np.concatenate([cw[:, :640], rope, rope, rope[:, p64], rope[:, p64]], axis=1)
    uq = f(inp["c_w_uq"])[0].reshape(384, 8, 192)
    nope = uq[:, :, :128].reshape(384, 1024)
    rp = uq[:, :, 128:]
    c_wuq = np.concatenate([nope, rp.reshape(384, 512), rp[:, :, p64].reshape(384, 512)], axis=1)
    ukv = f(inp["c_w_ukv"])[0].reshape(256, 8, 256)
    c_wukv = np.concatenate([ukv[:, :, :128].reshape(256, 1024), ukv[:, :, 128:].reshape(256, 1024)], axis=1)
    c_gain = np.concatenate([f(inp["c_q_a_gain"])[0].reshape(3, 128), f(inp["c_kv_a_gain"])[0].reshape(2, 128)], axis=0)
    shared = dict(
        w_ada=f(inp["w_ada"]), b_ada=f(inp["b_ada"]), ln_g=f(inp["ln_g"]), ln_b=f(inp["ln_b"]),
        w_ffn_gate=f(inp["w_ffn_gate"]), w_ffn_up=f(inp["w_ffn_up"]), w_ffn_down=f(inp["w_ffn_down"]),
        a_wqkv=np.ascontiguousarray(a_wqkv), a_gain=np.ascontiguousarray(a_gain), a_w_o=f(inp["a_w_o"]),
        b_w_qkv=f(inp["b_w_qkv"]), b_w_o=f(inp["b_w_o"]),
        c_wdqkv=np.ascontiguousarray(c_wdqkv), c_gain=np.ascontiguousarray(c_gain),
        c_wuq=np.ascontiguousarray(c_wuq), c_wukv=np.ascontiguousarray(c_wukv), c_w_o=f(inp["c_w_o"]))
    rpb = f(inp["b_rpb"])[0]
    per_rank = [dict(ropeA=_rope_tables(r, 128, False), ropeC=_rope_tables(r, 64, True), b_tab=_nbr_tables(rpb, r)) for r in range(2)]
    maps = []
    for core in range(8):
        b, r = core // 2, core % 2
        m = dict(shared)
        m.update(per_rank[r])
        m["xin"] = np.ascontiguousarray(np.concatenate([x[b, r * 2048:(r + 1) * 2048], ctx[b, r * 128:(r + 1) * 128]], axis=0))
        m["cc"] = np.ascontiguousarray(np.stack([c[b], c_ctx], axis=0))
        maps.append(m)
    return maps


def kernel(**inputs):
    maps = _host_inputs(inputs)
    if "nc" not in _PROG:
        _PROG["nc"] = build_program()
    res = run_bass_kernel_spmd(_PROG["nc"], maps, core_ids=list(range(8)))
    out = np.zeros((4, 4096, D), np.float32)
    for core in range(8):
        b, r = core // 2, core % 2
        out[b, r * 2048:(r + 1) * 2048] = np.asarray(res.results[core]["out"], dtype=np.float32)
    return out
```

```python
import math
from contextlib import ExitStack

import numpy as np
import concourse.bass as bass
import concourse.mybir as mybir
from concourse.bass_utils import run_bass_kernel_spmd

F32 = mybir.dt.float32
BF16 = mybir.dt.bfloat16
AF = mybir.ActivationFunctionType
ALU = mybir.AluOpType

D = 1024
DC = 8
NT = 17
TOK = NT * 128
NKT = 2 * NT
DFF = 2816
DEPTH = 4
ALPHA = (2.0 * DEPTH) ** 0.25
LN_EPS = 1e-5
RMS_EPS = 1e-6
GROUPS = [[0, 1], [2, 3], [4, 5], [6, 7]]
CHUNKS = [(0, 4), (4, 4), (8, 4), (12, 4), (16, 1)]
NEG = -30000.0


class Buf:
    __slots__ = ("w", "r", "excl")

    def __init__(self, excl=False):
        self.w = None
        self.r = {}
        self.excl = excl


class Eng:
    def __init__(self, name, handle, sem):
        self.name, self.h, self.sem, self.count, self.seen = name, handle, sem, 0, {}


class FW:
    def __init__(self, nc, stack, n_dma_sems=40):
        self.nc, self.stack = nc, stack
        mk = lambda n: stack.enter_context(nc.semaphore(n))
        self.pe = Eng("pe", nc.tensor, mk("s_pe"))
        self.act = Eng("act", nc.scalar, mk("s_act"))
        self.dve = Eng("dve", nc.vector, mk("s_dve"))
        self.pool = Eng("pool", nc.gpsimd, mk("s_pool"))
        self.sp = Eng("sp", nc.sync, mk("s_sp"))
        self.engines = [self.pe, self.act, self.dve, self.pool, self.sp]
        self.cc_sem, self.cc_count = mk("s_cc"), 0
        self.dma_pools = {"sp": [[mk(f"s_dh{i}"), 0] for i in range(n_dma_sems - 12)],
                          "pool": [[mk(f"s_ds{i}"), 0] for i in range(12)]}
        self.dma_sems = self.dma_pools["sp"] + self.dma_pools["pool"]
        self.dma_rr = {"sp": 0, "pool": 0}

    def _deps(self, e, reads, writes, skip_self):
        toks = {}

        def add(t):
            if t is None:
                return
            s, v = t
            if skip_self and s.num == e.sem.num:
                return
            if toks.get(s.num, (None, 0))[1] < v:
                toks[s.num] = (s, v)
        for b in reads:
            add(b.w)
            if b.excl:
                for t in b.r.values():
                    if t[0].num != e.sem.num:
                        add(t)
        for b in writes:
            add(b.w)
            for t in b.r.values():
                add(t)
        for s, v in toks.values():
            if e.seen.get(s.num, 0) < v:
                e.h.wait_ge(s, v)
                e.seen[s.num] = v

    @staticmethod
    def _mark(tok, reads, writes):
        s, v = tok
        for b in reads:
            o = b.r.get(s.num)
            if o is None or o[1] < v:
                b.r[s.num] = tok
        for b in writes:
            b.w = tok
            b.r = {}

    def op(self, e, fn, reads=(), writes=(), signal=True):
        self._deps(e, reads, writes, skip_self=(e is self.pe))
        ins = fn()
        if signal:
            e.count += 1
            ins.then_inc(e.sem, 1)
            tok = (e.sem, e.count)
        else:
            tok = (e.sem, e.count + 1)
        self._mark(tok, reads, writes)
        return ins

    def dma(self, q, out, in_, reads=(), writes=()):
        pl = self.dma_pools[q.name]
        slot = pl[self.dma_rr[q.name]]
        self.dma_rr[q.name] = (self.dma_rr[q.name] + 1) % len(pl)
        sem, val = slot
        self._deps(q, reads, writes, skip_self=False)
        if val > 0 and q.seen.get(sem.num, 0) < val:
            q.h.wait_ge(sem, val)
            q.seen[sem.num] = val
        q.h.dma_start(out=out, in_=in_).then_inc(sem, 16)
        slot[1] = val + 16
        self._mark((sem, val + 16), reads, writes)

    def collective(self, kind, in_ap, out_ap, reads=(), writes=()):
        q = self.pool
        self._deps(q, reads, writes, skip_self=False)
        ins = q.h.collective_compute(kind, ALU.bypass, replica_groups=GROUPS, ins=[in_ap], outs=[out_ap])
        self.cc_count += 1
        ins.then_inc(self.cc_sem, 1)
        self._mark((self.cc_sem, self.cc_count), reads, writes)

    def barrier(self):
        for e in self.engines:
            for o in self.engines:
                if o is not e and o.count > e.seen.get(o.sem.num, 0):
                    e.h.wait_ge(o.sem, o.count)
                    e.seen[o.sem.num] = o.count
            for sem, val in self.dma_sems:
                if val > e.seen.get(sem.num, 0):
                    e.h.wait_ge(sem, val)
                    e.seen[sem.num] = val
            if self.cc_count > e.seen.get(self.cc_sem.num, 0):
                e.h.wait_ge(self.cc_sem, self.cc_count)
                e.seen[self.cc_sem.num] = self.cc_count


class Rot:
    def __init__(self, items):
        self.items, self.i = items, 0

    def next(self):
        it = self.items[self.i]
        self.i = (self.i + 1) % len(self.items)
        return it


def build_program(n_layers=DEPTH, dbg=False):
    nc = bass.Bass("TRN2", target_bir_lowering=False)
    dt_in = lambda name, shape, dt=F32: nc.dram_tensor(name, list(shape), dt, kind="ExternalInput").ap()
    xin = dt_in("xin", [TOK, D])
    cc = dt_in("cc", [2, D])
    w_ada = dt_in("w_ada", [DEPTH, D, 6 * D])
    b_ada = dt_in("b_ada", [DEPTH, 6 * D])
    ln_g = dt_in("ln_g", [DEPTH, 2, D])
    ln_b = dt_in("ln_b", [DEPTH, 2, D])
    w_gate = dt_in("w_ffn_gate", [DEPTH, D, DFF])
    w_up = dt_in("w_ffn_up", [DEPTH, D, DFF])
    w_down = dt_in("w_ffn_down", [DEPTH, DFF, D])
    a_wqkv = dt_in("a_wqkv", [2, D, 2816])
    a_gain = dt_in("a_gain", [2, 4, 128])
    a_wo = dt_in("a_w_o", [2, D, D])
    b_wqkv = dt_in("b_w_qkv", [1, D, 3072])
    b_tab = dt_in("b_tab", [8, 5, 128, 12, 128])
    b_wo = dt_in("b_w_o", [1, D, D])
    c_wdqkv = dt_in("c_wdqkv", [D, 896])
    c_gain = dt_in("c_gain", [5, 128])
    c_wuq = dt_in("c_wuq", [384, 2048])
    c_wukv = dt_in("c_wukv", [256, 2048])
    c_wo = dt_in("c_w_o", [1, D, D])
    ropeA = dt_in("ropeA", [2, 128, TOK])
    ropeC = dt_in("ropeC", [2, 128, TOK])
    out = nc.dram_tensor("out", [2048, D], F32, kind="ExternalOutput").ap()
    if dbg:
        dbg_x = nc.dram_tensor("dbg_x", [TOK, D], F32, kind="ExternalOutput").ap()

    modd = nc.dram_tensor("modd", [DEPTH, 2, 6 * D], F32).ap()
    KINDS = ["A", "B", "C", "A"]
    NQ = {"A": 8, "B": 8, "C": 12}
    NK = {"A": 2, "B": 8, "C": 9}
    VC = {"A": 256, "B": 1024, "C": 1024}
    scr = []
    for l in range(DEPTH):
        k = KINDS[l]
        nkp = (NK[k] + 1) // 2
        krows = [min(2, NK[k] - 2 * i) * 128 for i in range(nkp)]
        nvp = VC[k] // 256
        d = dict(
            QT=nc.dram_tensor(f"QT{l}", [NQ[k] * 128, TOK], BF16).ap(),
            KTin=[nc.dram_tensor(f"KTin{l}_{i}", [krows[i], TOK], BF16).ap() for i in range(nkp)],
            KTall=[nc.dram_tensor(f"KTall{l}_{i}", [2 * krows[i], TOK], BF16).ap() for i in range(nkp)],
            Vin=[nc.dram_tensor(f"Vin{l}_{i}", [TOK, 256], BF16).ap() for i in range(nvp)],
            Vall=[nc.dram_tensor(f"Vall{l}_{i}", [2 * TOK, 256], BF16).ap() for i in range(nvp)],
            krows=krows, bQT=Buf(), bKTin=Buf(), bKTall=Buf(), bVin=Buf(), bVall=Buf())
        scr.append(d)

    def kt_in(sc, u):
        return sc["KTin"][u // 2][(u % 2) * 128:(u % 2) * 128 + 128, :]

    def kt_all(sc, s, u):
        r0 = s * sc["krows"][u // 2] + (u % 2) * 128
        return sc["KTall"][u // 2][r0:r0 + 128, :]

    with ExitStack() as top:
        fw = FW(nc, top)
        pe, act, dve, pool, sp = fw.pe, fw.act, fw.dve, fw.pool, fw.sp
        V, S, G, T = nc.vector, nc.scalar, nc.gpsimd, nc.tensor

        uid = [0]

        def sb(stack, name, shape, dt):
            uid[0] += 1
            return stack.enter_context(nc.sbuf_tensor(f"{name}_{uid[0]}", list(shape), dt))

        x_sb = sb(top, "x_sb", [128, NT, D], F32)
        bx = [Buf() for _ in range(NT)]
        ident = sb(top, "ident", [128, 128], F32)
        ones = sb(top, "ones", [128, 128], BF16)
        ones32 = sb(top, "ones32", [128, 128], F32)
        onesp = sb(top, "onesp", [128, 2, 128], BF16)
        epsc = sb(top, "epsc", [128, 2], F32)
        bconst = Buf()
        modT = sb(top, "modT", [128, 2, 48], F32)
        bmodT = Buf()
        condT = sb(top, "condT", [128, DC, 2], F32)
        bcond = Buf()
        psum = [top.enter_context(nc.psum_tensor(f"ps{i}", [128, 512], F32)) for i in range(8)]
        bps = [Buf(excl=True) for _ in range(8)]

        fw.op(pool, lambda: G.memset(ident[:], 0.0), writes=[bconst])
        fw.op(pool, lambda: G.affine_select(out=ident[:], in_=ident[:], compare_op=ALU.not_equal, fill=1.0,
                                            base=0, pattern=[[-1, 128]], channel_multiplier=1),
              reads=[bconst], writes=[bconst])
        fw.op(dve, lambda: V.memset(ones[:], 1.0), writes=[bconst])
        fw.op(dve, lambda: V.memset(ones32[:], 1.0), writes=[bconst])
        fw.op(dve, lambda: V.memset(onesp[:], 0.0), writes=[bconst])
        fw.op(dve, lambda: V.memset(onesp[:, 0, 0:64], 1.0), writes=[bconst])
        fw.op(dve, lambda: V.memset(onesp[:, 1, 64:128], 1.0), writes=[bconst])
        fw.op(dve, lambda: V.memset(epsc[:, 0:1], LN_EPS), writes=[bconst])
        fw.op(dve, lambda: V.memset(epsc[:, 1:2], RMS_EPS), writes=[bconst])
        for (t0, n) in CHUNKS:
            fw.dma(sp, x_sb[:, t0:t0 + n, :], xin[t0 * 128:(t0 + n) * 128, :].rearrange("(t p) d -> p t d", p=128),
                   writes=bx[t0:t0 + n])

        def mm_group(pi, out_ap, pairs, reads):
            n = len(pairs)
            for i, (l_ap, r_ap) in enumerate(pairs):
                fw.op(pe, lambda: T.matmul(out_ap, lhsT=l_ap, rhs=r_ap, start=(i == 0), stop=(i == n - 1)),
                      reads=reads, writes=[bps[pi]], signal=(i == n - 1))

        def load_colvecs(stack, name, rows_ap, n):
            tmp = sb(stack, name + "_r", [n, 128], F32)
            dst = sb(stack, name, [128, n], F32)
            b1, b2 = Buf(), Buf()
            fw.dma(sp, tmp[:], rows_ap, writes=[b1])
            fw.op(pe, lambda: T.transpose(psum[7][:, 0:n], tmp[:], ident[0:n, 0:n]), reads=[b1, bconst], writes=[bps[7]])
            fw.op(dve, lambda: V.tensor_copy(out=dst[:], in_=psum[7][:, 0:n]), reads=[bps[7]], writes=[b2])
            return dst, b2

        tr_banks = Rot([6, 7])

        def make_hT(hT, bhT, t0, n, kind_sc, kind_sh):
            lc = 1 if t0 == 16 else 0
            N = n * 128
            for c in range(DC):
                pi = tr_banks.next()
                for t in range(n):
                    fw.op(pe, lambda: T.transpose(psum[pi][:, t * 128:(t + 1) * 128], x_sb[:, t0 + t, c * 128:(c + 1) * 128], ident[:]),
                          reads=[bx[t0 + t], bconst], writes=[bps[pi]], signal=(t == n - 1))
                sc_ap = modT[:, lc, kind_sc * 8 + c:kind_sc * 8 + c + 1]
                sh_ap = modT[:, lc, kind_sh * 8 + c:kind_sh * 8 + c + 1]
                if c % 2 == 0:
                    fw.op(act, lambda: S.activation(out=hT[:, c, 0:N], in_=psum[pi][:, 0:N], func=AF.Identity, scale=sc_ap, bias=sh_ap),
                          reads=[bps[pi], bmodT], writes=[bhT])
                else:
                    fw.op(dve, lambda: V.tensor_scalar(out=hT[:, c, 0:N], in0=psum[pi][:, 0:N], scalar1=sc_ap, scalar2=sh_ap,
                                                       op0=ALU.mult, op1=ALU.add),
                          reads=[bps[pi], bmodT], writes=[bhT])

        def load_w_bf16(dst, bdst, src, kc, ncols, step=512):
            for c0 in range(0, ncols, step):
                c1 = min(ncols, c0 + step)
                fw.dma(pool, dst[:, 0:kc, c0:c1], src[:, c0:c1].rearrange("(c p) f -> p c f", p=128), writes=[bdst])

        with ExitStack() as st:
            st.enter_context(nc.named_scope("prologue"))
            crow = sb(st, "crow", [16, 128], F32)
            bcrow = Buf()
            fw.dma(sp, crow[:], cc.rearrange("r (c p) -> (r c) p", p=128), writes=[bcrow])
            fw.op(pe, lambda: T.transpose(psum[7][:, 0:16], crow[:], ident[0:16, 0:16]), reads=[bcrow, bconst], writes=[bps[7]])
            fw.op(act, lambda: S.activation(out=condT[:].rearrange("p c r -> p r c"),
                                            in_=psum[7][:, 0:16].rearrange("p (r c) -> p r c", r=2), func=AF.Silu),
                  reads=[bps[7]], writes=[bcond])
            wa = [sb(st, f"wa{i}", [128, DC, 512], F32) for i in range(2)]
            bwa = [Buf(), Buf()]
            modrow = sb(st, "modrow", [2, 6 * D], F32)
            bada = sb(st, "bada", [2, 6 * D], F32)
            bmr, bba = Buf(), Buf()
            k = 0
            for l in range(1):
                fw.dma(sp, bada[:], b_ada[l, :].partition_broadcast(2), writes=[bba])
                for cb in range(12):
                    i = k % 2
                    k += 1
                    fw.dma(sp, wa[i][:], w_ada[l][:, cb * 512:(cb + 1) * 512].rearrange("(c p) f -> p c f", p=128), writes=[bwa[i]])
                    pi = 4 + (k % 2)
                    mm_group(pi, psum[pi][0:2, :], [(condT[:, c, :], wa[i][:, c, :]) for c in range(DC)], [bcond, bwa[i]])
                    fw.op(dve, lambda: V.tensor_tensor(out=modrow[:, cb * 512:(cb + 1) * 512], in0=psum[pi][0:2, :],
                                                       in1=bada[:, cb * 512:(cb + 1) * 512], op=ALU.add),
                          reads=[bps[pi], bba], writes=[bmr])
                fw.dma(sp, modd[l], modrow[:], reads=[bmr], writes=[Buf()])
            fw.barrier()

        def layer_norm_tile(st_bufs, t, lg, lb, bbc):
            bn, mv, sd, bsm = st_bufs
            fw.op(dve, lambda: V.bn_stats(out=bn[:, 0, :], in_=x_sb[:, t, 0:512]), reads=[bx[t]], writes=[bsm])
            fw.op(dve, lambda: V.bn_stats(out=bn[:, 1, :], in_=x_sb[:, t, 512:1024]), reads=[bx[t]], writes=[bsm])
            fw.op(dve, lambda: V.bn_aggr(out=mv[:], in_=bn[:].rearrange("p a b -> p (a b)")), reads=[bsm], writes=[bsm])
            fw.op(act, lambda: S.activation(out=sd[:], in_=mv[:, 1:2], func=AF.Sqrt, bias=epsc[:, 0:1], scale=1.0),
                  reads=[bsm, bconst], writes=[bsm])
            fw.op(dve, lambda: V.reciprocal(out=sd[:], in_=sd[:]), reads=[bsm], writes=[bsm])
            fw.op(dve, lambda: V.tensor_scalar(out=x_sb[:, t, :], in0=x_sb[:, t, :], scalar1=mv[:, 0:1], scalar2=sd[:, 0:1],
                                               op0=ALU.subtract, op1=ALU.mult),
                  reads=[bx[t], bsm], writes=[bx[t]])
            fw.op(pool, lambda: G.tensor_tensor(out=x_sb[:, t, :], in0=x_sb[:, t, :], in1=lg[:], op=ALU.mult),
                  reads=[bx[t], bbc], writes=[bx[t]])
            fw.op(pool, lambda: G.tensor_tensor(out=x_sb[:, t, :], in0=x_sb[:, t, :], in1=lb[:], op=ALU.add),
                  reads=[bx[t], bbc], writes=[bx[t]])

        def load_bcast(stack, l, sub):
            gl = sb(stack, f"gl{sub}", [128, D], F32)
            gc = sb(stack, f"gc{sub}", [128, D], F32)
            lg = sb(stack, f"lg{sub}", [128, D], F32)
            lb = sb(stack, f"lb{sub}", [128, D], F32)
            bbc = Buf()
            goff = 2 * D if sub == 0 else 5 * D
            fw.dma(sp, gl[:], modd[l, 0, goff:goff + D].partition_broadcast(128), writes=[bbc])
            fw.dma(sp, gc[:], modd[l, 1, goff:goff + D].partition_broadcast(128), writes=[bbc])
            fw.dma(sp, lg[:], ln_g[l, sub, :].partition_broadcast(128), writes=[bbc])
            fw.dma(sp, lb[:], ln_b[l, sub, :].partition_broadcast(128), writes=[bbc])
            return gl, gc, lg, lb, bbc

        def attn_dense(pieces_fn, v_fn, key_tiles, N, scale, o_dst, bo_dst, reads, pbufs, rec, brec, par):
            po, pr = 2 + par, 4 + par
            nk = len(key_tiles)

            SB = [0, 1, 6]

            def qk(i):
                pi = SB[i % 3]
                mm_group(pi, psum[pi][:, 0:N], pieces_fn(key_tiles[i]), reads)
            qk(0)
            if nk > 1:
                qk(1)
            for i, kt in enumerate(key_tiles):
                if i + 2 < nk:
                    qk(i + 2)
                pT, bp = pbufs.next()
                si = SB[i % 3]
                fw.op(act, lambda: S.activation(out=pT[:, 0:N], in_=psum[si][:, 0:N], func=AF.Exp, scale=scale),
                      reads=[bps[si]], writes=[bp])
                lv, lo = v_fn(kt)
                fw.op(pe, lambda: T.matmul(psum[po][:, 0:N], lhsT=lv, rhs=pT[:, 0:N], start=(i == 0), stop=(i == nk - 1)),
                      reads=reads + [bp], writes=[bps[po]], signal=False)
                fw.op(pe, lambda: T.matmul(psum[pr][:, 0:N], lhsT=lo, rhs=pT[:, 0:N], start=(i == 0), stop=(i == nk - 1)),
                      reads=[bp, bconst], writes=[bps[pr]])
            fw.op(dve, lambda: V.reciprocal(out=rec[:, 0:N], in_=psum[pr][:, 0:N]), reads=[bps[pr]], writes=[brec])
            fw.op(dve, lambda: V.tensor_tensor(out=o_dst, in0=psum[po][:, 0:N], in1=rec[:, 0:N], op=ALU.mult),
                  reads=[bps[po], brec], writes=[bo_dst])

        def make_mod_stepper(stack, layers):
            blocks = [(l_, cb) for l_ in layers for cb in range(12)]
            wa_ = [sb(stack, f"mwa{i}", [128, DC, 512], F32) for i in range(2)]
            bb_ = [sb(stack, f"mbb{i}", [2, 512], F32) for i in range(2)]
            mr_ = [sb(stack, f"mmr{i}", [2, 512], F32) for i in range(2)]
            bw_, bm_ = [Buf(), Buf()], [Buf(), Buf()]
            state = [0]

            def step():
                k = state[0]
                if k > len(blocks):
                    return False
                state[0] += 1
                if k < len(blocks):
                    l_, cb = blocks[k]
                    i = k % 2
                    fw.dma(pool, wa_[i][:], w_ada[l_][:, cb * 512:(cb + 1) * 512].rearrange("(c p) f -> p c f", p=128), writes=[bw_[i]])
                    fw.dma(pool, bb_[i][:], b_ada[l_, cb * 512:(cb + 1) * 512].partition_broadcast(2), writes=[bw_[i]])
                if k >= 1:
                    l_, cb = blocks[k - 1]
                    i = (k - 1) % 2
                    pi = 7
                    mm_group(pi, psum[pi][0:2, :], [(condT[:, c, :], wa_[i][:, c, :]) for c in range(DC)], [bcond, bw_[i]])
                    fw.op(dve, lambda: V.tensor_tensor(out=mr_[i][:], in0=psum[pi][0:2, :], in1=bb_[i][:], op=ALU.add),
                          reads=[bps[pi], bw_[i]], writes=[bm_[i]])
                    fw.dma(sp, modd[l_][:, cb * 512:(cb + 1) * 512], mr_[i][:], reads=[bm_[i]], writes=[Buf()])
                return True
            return step

        for l in range(n_layers):
            kind = KINDS[l]
            j = l // 3
            need_ctx = l < DEPTH - 1
            sc = scr[l]
            q_chunks = CHUNKS if need_ctx else CHUNKS[:4]
            n_tiles = NT if need_ctx else 16
            qcols = n_tiles * 128

            with ExitStack() as st:
                st.enter_context(nc.named_scope(f"L{l}_mod"))
                mrow = sb(st, "mrow", [96, 128], F32)
                bm = Buf()
                fw.dma(sp, mrow[:], modd[l].rearrange("r (k p) -> (r k) p", p=128), writes=[bm])
                fw.op(pe, lambda: T.transpose(psum[7][:, 0:96], mrow[:], ident[0:96, 0:96]), reads=[bm, bconst], writes=[bps[7]])
                fw.op(dve, lambda: V.tensor_copy(out=modT[:].rearrange("p r k -> p (r k)"), in_=psum[7][:, 0:96]),
                      reads=[bps[7]], writes=[bmodT])
                for kk in (1, 4):
                    fw.op(dve, lambda: V.tensor_scalar_add(modT[:, :, kk * 8:kk * 8 + 8], modT[:, :, kk * 8:kk * 8 + 8], 1.0),
                          reads=[bmodT], writes=[bmodT])
                fw.barrier()

            with ExitStack() as st:
                st.enter_context(nc.named_scope(f"L{l}_A"))
                hTs = [sb(st, f"hT{i}", [128, DC, 512], BF16) for i in range(2)]
                hTr = Rot(list(zip(hTs, [Buf(), Buf()])))
                stg = Rot([(sb(st, f"stg{i}", [128, 512], BF16), Buf()) for i in range(4)])
                stv = Rot([(sb(st, f"stv{i}", [128, 1024], BF16), Buf()) for i in range(2)])
                f32t = lambda nm: (sb(st, nm, [128, 512], F32), Buf())
                if kind in ("A", "C"):
                    cosT = sb(st, "cosT", [128, TOK], F32)
                    sinT = sb(st, "sinT", [128, TOK], F32)
                    brope = Buf()
                    rsrc = ropeA if kind == "A" else ropeC
                    fw.dma(sp, cosT[:], rsrc[0], writes=[brope])
                    fw.dma(sp, sinT[:], rsrc[1], writes=[brope])
                    sqs = Rot([(sb(st, f"sq{i}", [128, 512], BF16), Buf()) for i in range(3)])
                    tsets = Rot([[f32t(f"t{nm}{i}") for nm in "ABUVR"] for i in range(2)])
                    tA, tB, tU, tV, tR = tsets.items[0]

                def rope_combine(pa, pb, N, c0, dst, bdst, rstd=None, gain=None, gain_sw=None):
                    cs, sn = cosT[:, c0:c0 + N], sinT[:, c0:c0 + N]
                    if rstd is not None:
                        fw.op(dve, lambda: V.tensor_tensor(out=tA[0][:, 0:N], in0=rstd[:, 0:N], in1=cs, op=ALU.mult),
                              reads=[tR[1], brope], writes=[tA[1]])
                        fw.op(dve, lambda: V.tensor_tensor(out=tB[0][:, 0:N], in0=rstd[:, 0:N], in1=sn, op=ALU.mult),
                              reads=[tR[1], brope], writes=[tB[1]])
                        fw.op(dve, lambda: V.scalar_tensor_tensor(out=tU[0][:, 0:N], in0=psum[pa][:, 0:N], scalar=gain, in1=tA[0][:, 0:N],
                                                                  op0=ALU.mult, op1=ALU.mult),
                              reads=[bps[pa], tA[1], bgain], writes=[tU[1]])
                        fw.op(dve, lambda: V.scalar_tensor_tensor(out=tV[0][:, 0:N], in0=psum[pb][:, 0:N], scalar=gain_sw, in1=tB[0][:, 0:N],
                                                                  op0=ALU.mult, op1=ALU.mult),
                              reads=[bps[pb], tB[1], bgain], writes=[tV[1]])
                    else:
                        fw.op(dve, lambda: V.tensor_tensor(out=tU[0][:, 0:N], in0=psum[pa][:, 0:N], in1=cs, op=ALU.mult),
                              reads=[bps[pa], brope], writes=[tU[1]])
                        fw.op(dve, lambda: V.tensor_tensor(out=tV[0][:, 0:N], in0=psum[pb][:, 0:N], in1=sn, op=ALU.mult),
                              reads=[bps[pb], brope], writes=[tV[1]])
                    fw.op(pool, lambda: G.tensor_tensor(out=dst[:, 0:N], in0=tU[0][:, 0:N], in1=tV[0][:, 0:N], op=ALU.add),
                          reads=[tU[1], tV[1]], writes=[bdst])

                def rstd_from(banks, N, nfeat):
                    sq_l = []
                    for pb_ in banks:
                        sq, bsq = sqs.next()
                        fw.op(act, lambda: S.activation(out=sq[:, 0:N], in_=psum[pb_][:, 0:N], func=AF.Square),
                              reads=[bps[pb_]], writes=[bsq])
                        sq_l.append((sq, bsq))
                    mm_group(5, psum[5][:, 0:N], [(ones[:], sq[:, 0:N]) for sq, _ in sq_l], [bconst] + [b for _, b in sq_l])
                    fw.op(act, lambda: S.activation(out=tR[0][:, 0:N], in_=psum[5][:, 0:N], func=AF.Sqrt, bias=epsc[:, 1:2], scale=1.0 / nfeat),
                          reads=[bps[5], bconst], writes=[tR[1]])
                    fw.op(dve, lambda: V.reciprocal(out=tR[0][:, 0:N], in_=tR[0][:, 0:N]), reads=[tR[1]], writes=[tR[1]])

                def evac_to(pi, N, dst_dram, bdst_dram, use_act):
                    s_, bs_ = stg.next()
                    if use_act:
                        fw.op(act, lambda: S.copy(out=s_[:, 0:N], in_=psum[pi][:, 0:N]), reads=[bps[pi]], writes=[bs_])
                    else:
                        fw.op(dve, lambda: V.tensor_copy(out=s_[:, 0:N], in_=psum[pi][:, 0:N]), reads=[bps[pi]], writes=[bs_])
                    fw.dma(sp, dst_dram, s_[:, 0:N], reads=[bs_], writes=[bdst_dram])

                def v_proj(hT, bhT, t0, n, wv_fn, kc, vcols, reads):
                    for t in range(n):
                        sv, bsv = stv.next()
                        for hf in range(0, vcols, 512):
                            w_ = min(512, vcols - hf)
                            pi = 6 + ((hf // 512) % 2)
                            mm_group(pi, psum[pi][:, 0:w_], [(hT[:, c, t * 128:(t + 1) * 128], wv_fn(c, hf, w_)) for c in range(kc)], reads + [bhT])
                            if (hf // 512) % 2 == 0:
                                fw.op(act, lambda: S.copy(out=sv[:, hf:hf + w_], in_=psum[pi][:, 0:w_]), reads=[bps[pi]], writes=[bsv])
                            else:
                                fw.op(dve, lambda: V.tensor_copy(out=sv[:, hf:hf + w_], in_=psum[pi][:, 0:w_]), reads=[bps[pi]], writes=[bsv])
                        for cp in range(vcols // 256):
                            fw.dma(sp, sc["Vin"][cp][(t0 + t) * 128:(t0 + t + 1) * 128, :], sv[:, cp * 256:(cp + 1) * 256], reads=[bsv], writes=[sc["bVin"]])

                if kind == "A":
                    wq = sb(st, "wq", [128, DC, 2816], BF16)
                    bwq = Buf()
                    load_w_bf16(wq, bwq, a_wqkv[j], DC, 2816)
                    gains, bgain = load_colvecs(st, "gainA", a_gain[j], 4)
                    for (t0, n) in CHUNKS:
                        N = n * 128
                        hT, bhT = hTr.next()
                        make_hT(hT, bhT, t0, n, 1, 0)
                        for slab in range(10):
                            isq = slab < 8
                            if isq and not need_ctx and t0 == 16:
                                continue
                            pa, pb = (0, 1) if slab % 2 == 0 else (2, 3)
                            tA, tB, tU, tV, tR = tsets.next()
                            mm_group(pa, psum[pa][:, 0:N], [(wq[:, c, slab * 128:(slab + 1) * 128], hT[:, c, 0:N]) for c in range(DC)], [bwq, bhT])
                            mm_group(pb, psum[pb][:, 0:N], [(wq[:, c, 1536 + slab * 128:1536 + (slab + 1) * 128], hT[:, c, 0:N]) for c in range(DC)], [bwq, bhT])
                            rstd_from([pa], N, 128.0)
                            s_, bs_ = stg.next()
                            go = 0 if isq else 2
                            rope_combine(pa, pb, N, t0 * 128, s_, bs_, rstd=tR[0], gain=gains[:, go:go + 1], gain_sw=gains[:, go + 1:go + 2])
                            if isq:
                                fw.dma(sp, sc["QT"][slab * 128:(slab + 1) * 128, t0 * 128:t0 * 128 + N], s_[:, 0:N], reads=[bs_], writes=[sc["bQT"]])
                            else:
                                g = slab - 8
                                fw.dma(sp, kt_in(sc, g)[:, t0 * 128:t0 * 128 + N], s_[:, 0:N], reads=[bs_], writes=[sc["bKTin"]])
                        v_proj(hT, bhT, t0, n, lambda c, hf, w_: wq[:, c, 1280 + hf:1280 + hf + w_], DC, 256, [bwq])
                elif kind == "B":
                    wq = sb(st, "wq", [128, DC, 3072], BF16)
                    bwq = Buf()
                    load_w_bf16(wq, bwq, b_wqkv[0], DC, 3072)
                    k_ = 0
                    for (t0, n) in CHUNKS:
                        N = n * 128
                        hT, bhT = hTr.next()
                        make_hT(hT, bhT, t0, n, 1, 0)
                        for slab in range(16):
                            pi = k_ % 6
                            k_ += 1
                            mm_group(pi, psum[pi][:, 0:N], [(wq[:, c, slab * 128:(slab + 1) * 128], hT[:, c, 0:N]) for c in range(DC)], [bwq, bhT])
                            if slab < 8:
                                evac_to(pi, N, sc["QT"][slab * 128:(slab + 1) * 128, t0 * 128:t0 * 128 + N], sc["bQT"], slab % 2 == 0)
                            else:
                                p_ = slab - 8
                                evac_to(pi, N, kt_in(sc, p_)[:, t0 * 128:t0 * 128 + N], sc["bKTin"], slab % 2 == 0)
                        v_proj(hT, bhT, t0, n, lambda c, hf, w_: wq[:, c, 2048 + hf:2048 + hf + w_], DC, 1024, [bwq])
                else:
                    wd = sb(st, "wd", [128, DC, 896], BF16)
                    wuq = sb(st, "wuq", [128, 3, 2048], BF16)
                    wukv = sb(st, "wukv", [128, 2, 2048], BF16)
                    bwd, bwuq, bwukv = Buf(), Buf(), Buf()
                    load_w_bf16(wd, bwd, c_wdqkv, DC, 896)
                    load_w_bf16(wuq, bwuq, c_wuq, 3, 2048)
                    load_w_bf16(wukv, bwukv, c_wukv, 2, 2048)
                    gains, bgain = load_colvecs(st, "gainC", c_gain, 5)
                    qln = [sb(st, f"qln{i}", [128, 3, 512], BF16) for i in range(2)]
                    kvn = [sb(st, f"kvn{i}", [128, 2, 512], BF16) for i in range(2)]
                    lat_r = Rot(list(zip(qln, kvn, [Buf(), Buf()], [Buf(), Buf()])))
                    for (t0, n) in CHUNKS:
                        N = n * 128
                        c0 = t0 * 128
                        hT, bhT = hTr.next()
                        make_hT(hT, bhT, t0, n, 1, 0)
                        ql, kv, bql, bkv = lat_r.next()
                        for s_i in range(3):
                            mm_group(s_i, psum[s_i][:, 0:N], [(wd[:, c, s_i * 128:(s_i + 1) * 128], hT[:, c, 0:N]) for c in range(DC)], [bwd, bhT])
                        tA, tB, tU, tV, tR = tsets.next()
                        rstd_from([0, 1, 2], N, 384.0)
                        for s_i in range(3):
                            fw.op(dve, lambda: V.scalar_tensor_tensor(out=ql[:, s_i, 0:N], in0=psum[s_i][:, 0:N], scalar=gains[:, s_i:s_i + 1],
                                                                      in1=tR[0][:, 0:N], op0=ALU.mult, op1=ALU.mult),
                                  reads=[bps[s_i], tR[1], bgain], writes=[bql])
                        for s_i in range(2):
                            mm_group(3 + s_i, psum[3 + s_i][:, 0:N], [(wd[:, c, 384 + s_i * 128:384 + (s_i + 1) * 128], hT[:, c, 0:N]) for c in range(DC)], [bwd, bhT])
                        tA, tB, tU, tV, tR = tsets.next()
                        rstd_from([3, 4], N, 256.0)
                        for s_i in range(2):
                            fw.op(dve, lambda: V.scalar_tensor_tensor(out=kv[:, s_i, 0:N], in0=psum[3 + s_i][:, 0:N], scalar=gains[:, 3 + s_i:4 + s_i],
                                                                      in1=tR[0][:, 0:N], op0=ALU.mult, op1=ALU.mult),
                                  reads=[bps[3 + s_i], tR[1], bgain], writes=[bkv])
                        mm_group(0, psum[0][:, 0:N], [(wd[:, c, 640:768], hT[:, c, 0:N]) for c in range(DC)], [bwd, bhT])
                        mm_group(1, psum[1][:, 0:N], [(wd[:, c, 768:896], hT[:, c, 0:N]) for c in range(DC)], [bwd, bhT])
                        s_, bs_ = stg.next()
                        tA, tB, tU, tV, tR = tsets.next()
                        rope_combine(0, 1, N, c0, s_, bs_)
                        fw.dma(sp, kt_in(sc, 8)[:, c0:c0 + N], s_[:, 0:N], reads=[bs_], writes=[sc["bKTin"]])
                        do_q = need_ctx or t0 != 16
                        k_ = 0
                        for h in range(8):
                            if do_q:
                                pi = 2 + (k_ % 4)
                                k_ += 1
                                mm_group(pi, psum[pi][:, 0:N], [(wuq[:, s_i, h * 128:(h + 1) * 128], ql[:, s_i, 0:N]) for s_i in range(3)], [bwuq, bql])
                                evac_to(pi, N, sc["QT"][h * 128:(h + 1) * 128, c0:c0 + N], sc["bQT"], k_ % 2 == 0)
                            pi = 2 + (k_ % 4)
                            k_ += 1
                            mm_group(pi, psum[pi][:, 0:N], [(wukv[:, s_i, h * 128:(h + 1) * 128], kv[:, s_i, 0:N]) for s_i in range(2)], [bwukv, bkv])
                            evac_to(pi, N, kt_in(sc, h)[:, c0:c0 + N], sc["bKTin"], k_ % 2 == 0)
                        if do_q:
                            for pr_ in range(4):
                                mm_group(0, psum[0][:, 0:N], [(wuq[:, s_i, 1024 + pr_ * 128:1024 + (pr_ + 1) * 128], ql[:, s_i, 0:N]) for s_i in range(3)], [bwuq, bql])
                                mm_group(1, psum[1][:, 0:N], [(wuq[:, s_i, 1536 + pr_ * 128:1536 + (pr_ + 1) * 128], ql[:, s_i, 0:N]) for s_i in range(3)], [bwuq, bql])
                                s_, bs_ = stg.next()
                                tA, tB, tU, tV, tR = tsets.next()
                                rope_combine(0, 1, N, c0, s_, bs_)
                                fw.dma(sp, sc["QT"][(8 + pr_) * 128:(9 + pr_) * 128, c0:c0 + N], s_[:, 0:N], reads=[bs_], writes=[sc["bQT"]])
                        v_proj(kv, bkv, t0, n, lambda c, hf, w_: wukv[:, c, 1024 + hf:1024 + hf + w_], 2, 1024, [bwukv])
                for a_, b_ in zip(sc["KTin"], sc["KTall"]):
                    fw.collective("AllGather", a_, b_, reads=[sc["bKTin"]], writes=[sc["bKTall"]])
                for a_, b_ in zip(sc["Vin"], sc["Vall"]):
                    fw.collective("AllGather", a_, b_, reads=[sc["bVin"]], writes=[sc["bVall"]])
                fw.barrier()

            with ExitStack() as stO:
                OT = sb(stO, "OT", [128, 8, TOK], BF16)
                bOT = [[Buf() for _ in range(5)] for _ in range(8)]
                with ExitStack() as st:
                    st.enter_context(nc.named_scope(f"L{l}_B"))
                    pbufs = Rot([(sb(st, f"pT{i}", [128, 512], BF16), Buf()) for i in range(5)])
                    rec = sb(st, "rec", [128, 512], F32)
                    brec = Buf()
                    nk = NK[kind]
                    par = 0
                    mod_step = make_mod_stepper(st, list(range(1, n_layers))) if (l == 0 and n_layers > 1) else None
                    if kind in ("A", "C"):
                        nunits = 2 if kind == "A" else 8
                        KTs = Rot([(sb(st, f"KT{i}", [128, NKT * 128], BF16), Buf()) for i in range(2)])
                        Vs = Rot([(sb(st, f"Vt{i}", [128, NKT, 128], BF16), Buf()) for i in range(2)])
                        QTs = Rot([(sb(st, f"QTh{i}", [128, TOK], BF16), Buf()) for i in range(2)])
                        scale = 128.0 ** -0.5 if kind == "A" else 192.0 ** -0.5
                        if kind == "C":
                            KRs = [sb(st, f"KR{i}", [128, NKT * 128], BF16) for i in range(2)]
                            bKR = Buf()
                            fw.op(pool, lambda: G.memset(KRs[0][64:128, :], 0.0), writes=[bKR])
                            fw.op(pool, lambda: G.memset(KRs[1][0:64, :], 0.0), writes=[bKR])
                            for s_ in range(2):
                                fw.dma(sp, KRs[0][0:64, s_ * TOK:(s_ + 1) * TOK], kt_all(sc, s_, 8)[0:64, :],
                                       reads=[sc["bKTall"]], writes=[bKR])
                                fw.dma(sp, KRs[1][64:128, s_ * TOK:(s_ + 1) * TOK], kt_all(sc, s_, 8)[64:128, :],
                                       reads=[sc["bKTall"]], writes=[bKR])
                            QRs = Rot([(sb(st, f"QR{i}", [128, TOK], BF16), Buf()) for i in range(2)])
                        for u in range(nunits):
                            KT, bKT = KTs.next()
                            Vt, bVt = Vs.next()
                            for s_ in range(2):
                                fw.dma(sp, KT[:, s_ * TOK:(s_ + 1) * TOK], kt_all(sc, s_, u),
                                       reads=[sc["bKTall"]], writes=[bKT])
                                fw.dma(sp, Vt[:, s_ * NT:(s_ + 1) * NT, :],
                                       sc["Vall"][u // 2][s_ * TOK:(s_ + 1) * TOK, (u % 2) * 128:(u % 2) * 128 + 128].rearrange("(t p) c -> p t c", p=128),
                                       reads=[sc["bVall"]], writes=[bVt])
                            heads = [u * 4 + i for i in range(4)] if kind == "A" else [u]
                            for h in heads:
                                QT, bQT = QTs.next()
                                fw.dma(sp, QT[:, 0:qcols], sc["QT"][h * 128:(h + 1) * 128, 0:qcols], reads=[sc["bQT"]], writes=[bQT])
                                if kind == "C" and h % 2 == 0:
                                    QR, bQR = QRs.next()
                                    fw.dma(sp, QR[:], sc["QT"][(8 + h // 2) * 128:(9 + h // 2) * 128, :], reads=[sc["bQT"]], writes=[bQR])
                                for ci, (t0, n) in enumerate(q_chunks):
                                    N = n * 128
                                    c0 = t0 * 128
                                    kts = list(range(NKT)) if t0 != 16 else [16, 33]
                                    if mod_step is not None:
                                        mod_step()
                                    if kind == "A":
                                        pf = lambda kt: [(KT[:, kt * 128:(kt + 1) * 128], QT[:, c0:c0 + N])]
                                        rd = [bKT, bVt, bQT]
                                    else:
                                        KR = KRs[h % 2]
                                        pf = lambda kt: [(KT[:, kt * 128:(kt + 1) * 128], QT[:, c0:c0 + N]),
                                                         (KR[:, kt * 128:(kt + 1) * 128], QR[:, c0:c0 + N])]
                                        rd = [bKT, bVt, bQT, bKR, bQR]
                                    attn_dense(pf, lambda kt: (Vt[:, kt, :], ones[:]), kts, N, scale,
                                               OT[:, h, c0:c0 + N], bOT[h][ci], rd, pbufs, rec, brec, par)
                                    par ^= 1
                    else:
                        KTp = Rot([(sb(st, f"KTp{i}", [128, 22 * 128], BF16), Buf()) for i in range(2)])
                        Vp = [(sb(st, f"Vp{i}", [128, 22, 2, 128], BF16), Buf()) for i in range(2)]
                        for vp_, bvp_ in Vp:
                            fw.op(pool, lambda: G.memset(vp_[:], 0.0), writes=[bvp_])
                        Vpr = Rot(Vp)
                        QTp = Rot([(sb(st, f"QTp{i}", [128, TOK], BF16), Buf()) for i in range(2)])
                        tabs = sb(st, "tabs", [128, 5, 12, 128], F32)
                        btab = [Buf() for _ in range(5)]
                        sbs = Rot([(sb(st, f"sbs{i}", [128, 512], F32), Buf()) for i in range(4)])
                        pT4 = Rot([(sb(st, f"pq{i}", [128, 4, 512], BF16), Buf()) for i in range(2)])
                        E_SRCS = [("all", 0, 14 * 128, 256, 0), ("in", None, 0, 2048, 2),
                                  ("all", 1, 0, 256, 18), ("all", 0, 2048, 128, 20), ("all", 1, 2048, 128, 21)]
                        for p_ in range(8):
                            KT, bKT = KTp.next()
                            Vt, bVt = Vpr.next()
                            QT, bQT = QTp.next()
                            for (srck, slot, col0, ncol, e0) in E_SRCS:
                                if srck == "in":
                                    src, bsrc = kt_in(sc, p_), sc["bKTin"]
                                else:
                                    src, bsrc = kt_all(sc, slot, p_), sc["bKTall"]
                                fw.dma(sp, KT[:, e0 * 128:e0 * 128 + ncol], src[:, col0:col0 + ncol], reads=[bsrc], writes=[bKT])
                                for s2 in range(2):
                                    hc = (2 * p_ + s2) * 64
                                    if srck == "in":
                                        vsrc, bvs, rr0 = sc["Vin"][hc // 256], sc["bVin"], col0
                                    else:
                                        vsrc, bvs, rr0 = sc["Vall"][hc // 256], sc["bVall"], slot * TOK + col0
                                    fw.dma(sp, Vt[:, e0:e0 + ncol // 128, s2, 64 * s2:64 * s2 + 64],
                                           vsrc[rr0:rr0 + ncol, hc % 256:hc % 256 + 64].rearrange("(t p) c -> p t c", p=128), reads=[bvs], writes=[bVt])
                            fw.dma(sp, QT[:], sc["QT"][p_ * 128:(p_ + 1) * 128, :], reads=[sc["bQT"]], writes=[bQT])
                            for cf in range(5):
                                fw.dma(sp, tabs[:, cf, :, :], b_tab[p_, cf], writes=[btab[cf]])
                            for il in range(NT):
                                c0 = il * 128
                                isctx = il == 16
                                cfg = 0 if il == 0 else 1 if il == 1 else 3 if il == 14 else 4 if il == 15 else 2
                                base = min(il, 14)
                                units = []
                                for s2 in range(2):
                                    if not isctx:
                                        for jj in range(6):
                                            units.append((s2 * 8 + jj, s2, base + jj))
                                    for c_ in range(2):
                                        units.append((s2 * 8 + 6 + c_, s2, 20 + c_))
                                last_in_bank = {}
                                for (uu, s2, e) in units:
                                    last_in_bank[uu // 4] = uu
                                for (uu, s2, e) in units:
                                    pi = uu // 4
                                    fw.op(pe, lambda: T.matmul(psum[pi][:, (uu % 4) * 128:(uu % 4 + 1) * 128],
                                                               lhsT=KT[64 * s2:64 * s2 + 64, e * 128:(e + 1) * 128],
                                                               rhs=QT[64 * s2:64 * s2 + 64, c0:c0 + 128], start=True, stop=True),
                                          reads=[bKT, bQT], writes=[bps[pi]], signal=(last_in_bank[pi] == uu))
                                pq, bpq = pT4.next()
                                for s2 in range(2):
                                    b0, b1 = 2 * s2, 2 * s2 + 1
                                    if not isctx:
                                        sb_, bsb_ = sbs.next()
                                        fw.op(dve, lambda: V.scalar_tensor_tensor(out=sb_[:], in0=psum[b0][:], scalar=0.125,
                                                                                  in1=tabs[:, cfg, s2 * 6:s2 * 6 + 4, :].rearrange("p u q -> p (u q)"),
                                                                                  op0=ALU.mult, op1=ALU.add),
                                              reads=[bps[b0], btab[cfg]], writes=[bsb_])
                                        fw.op(act, lambda: S.activation(out=pq[:, b0, :], in_=sb_[:], func=AF.Exp), reads=[bsb_], writes=[bpq])
                                        sb_, bsb_ = sbs.next()
                                        fw.op(dve, lambda: V.scalar_tensor_tensor(out=sb_[:, 0:256], in0=psum[b1][:, 0:256], scalar=0.125,
                                                                                  in1=tabs[:, cfg, s2 * 6 + 4:s2 * 6 + 6, :].rearrange("p u q -> p (u q)"),
                                                                                  op0=ALU.mult, op1=ALU.add),
                                              reads=[bps[b1], btab[cfg]], writes=[bsb_])
                                        fw.op(act, lambda: S.activation(out=pq[:, b1, 0:256], in_=sb_[:, 0:256], func=AF.Exp), reads=[bsb_], writes=[bpq])
                                    fw.op(act, lambda: S.activation(out=pq[:, b1, 256:512], in_=psum[b1][:, 256:512], func=AF.Exp, scale=0.125),
                                          reads=[bps[b1]], writes=[bpq])
                                po = 4 + par
                                par ^= 1
                                nu = len(units)
                                for i_, (uu, s2, e) in enumerate(units):
                                    fw.op(pe, lambda: T.matmul(psum[po][:, 0:128], lhsT=Vt[:, e, s2, :], rhs=pq[:, uu // 4, (uu % 4) * 128:(uu % 4 + 1) * 128],
                                                               start=(i_ == 0), stop=(i_ == nu - 1)),
                                          reads=[bVt, bpq], writes=[bps[po]], signal=False)
                                for i_, (uu, s2, e) in enumerate(units):
                                    fw.op(pe, lambda: T.matmul(psum[po][:, 128:256], lhsT=onesp[:, s2, :], rhs=pq[:, uu // 4, (uu % 4) * 128:(uu % 4 + 1) * 128],
                                                               start=(i_ == 0), stop=(i_ == nu - 1)),
                                          reads=[bconst, bpq], writes=[bps[po]], signal=(i_ == nu - 1))
                                fw.op(dve, lambda: V.reciprocal(out=rec[:, 0:128], in_=psum[po][:, 128:256]), reads=[bps[po]], writes=[brec])
                                fw.op(dve, lambda: V.tensor_tensor(out=OT[:, p_, c0:c0 + 128], in0=psum[po][:, 0:128], in1=rec[:, 0:128], op=ALU.mult),
                                      reads=[bps[po], brec], writes=[bOT[p_][min(il // 4, 4)]])
                    if mod_step is not None:
                        while mod_step():
                            pass
                    fw.barrier()

                with ExitStack() as st:
                    st.enter_context(nc.named_scope(f"L{l}_C"))
                    wo = sb(st, "wo", [128, DC, D], BF16)
                    bwo = Buf()
                    wsrc = {"A": a_wo[j], "B": b_wo[0], "C": c_wo[0]}[kind]
                    load_w_bf16(wo, bwo, wsrc, DC, D)
                    gl, gc, lg, lb, bbc = load_bcast(st, l, 0)
                    tmp = Rot([(sb(st, f"tmp{i}", [128, D], F32), Buf()) for i in range(2)])
                    stt = Rot([((sb(st, f"bn{i}", [128, 2, 6], F32), sb(st, f"mv{i}", [128, 2], F32), sb(st, f"sd{i}", [128, 1], F32), Buf())) for i in range(2)])
                    allO = [b for row in bOT for b in row]
                    for t in range(n_tiles):
                        gt = gc if t == 16 else gl
                        tm, btm = tmp.next()
                        for hf in range(2):
                            pi = 2 * (t % 2) + hf
                            mm_group(pi, psum[pi][:], [(OT[:, s_i, t * 128:(t + 1) * 128], wo[:, s_i, hf * 512:(hf + 1) * 512]) for s_i in range(8)], [bwo] + allO)
                            fw.op(dve, lambda: V.tensor_tensor(out=tm[:, hf * 512:(hf + 1) * 512], in0=psum[pi][:], in1=gt[:, hf * 512:(hf + 1) * 512], op=ALU.mult),
                                  reads=[bps[pi], bbc], writes=[btm])
                        fw.op(dve, lambda: V.scalar_tensor_tensor(out=x_sb[:, t, :], in0=x_sb[:, t, :], scalar=ALPHA, in1=tm[:], op0=ALU.mult, op1=ALU.add),
                              reads=[bx[t], btm], writes=[bx[t]])
                        layer_norm_tile(stt.next(), t, lg, lb, bbc)
                    fw.barrier()

            with ExitStack() as st:
                st.enter_context(nc.named_scope(f"L{l}_D"))
                h2 = sb(st, "h2", [128, DC, TOK], BF16)
                bh2 = [Buf() for _ in CHUNKS]
                gl, gc, lg, lb, bbc = load_bcast(st, l, 1)
                PARTS = [(0, 4), (4, 4), (8, 4), (12, 4), (16, 4), (20, 2)]
                wgu = Rot([(sb(st, f"wg{i}", [128, DC, 512], BF16), sb(st, f"wu{i}", [128, DC, 512], BF16), Buf()) for i in range(2)])
                wdn = Rot([(sb(st, f"wdn{i}", [128, 4, D], BF16), sb(st, f"wdc{i}", [128, 4, D], BF16), Buf()) for i in range(2)])
                aT = Rot([(sb(st, f"aT{i}", [128, 4, 512], BF16), Buf()) for i in range(2)])
                sg = Rot([(sb(st, f"sg{i}", [128, 512], F32), Buf()) for i in range(2)])
                stt = Rot([((sb(st, f"bn{i}", [128, 2, 6], F32), sb(st, f"mv{i}", [128, 2], F32), sb(st, f"sd{i}", [128, 1], F32), Buf())) for i in range(2)])
                chs = q_chunks
                for ci, (t0, n) in enumerate(chs):
                    hv = h2[:, :, t0 * 128:(t0 + n) * 128]
                    make_hT(hv, bh2[ci], t0, n, 4, 3)
                for pi_, (f0, nf) in enumerate(PARTS):
                    wg, wu, bwgu = wgu.next()
                    wd_, wdc, bwdn = wdn.next()
                    fcols = slice(f0 * 128, (f0 + nf) * 128)
                    fw.dma(pool, wg[:, :, 0:nf * 128], w_gate[l][:, fcols].rearrange("(c p) f -> p c f", p=128), writes=[bwgu])
                    fw.dma(pool, wu[:, :, 0:nf * 128], w_up[l][:, fcols].rearrange("(c p) f -> p c f", p=128), writes=[bwgu])
                    fw.dma(pool, wd_[:, 0:nf, :], w_down[l][f0 * 128:(f0 + nf) * 128, :].rearrange("(c p) f -> p c f", p=128), writes=[bwdn])
                    if need_ctx:
                        for fi in range(nf):
                            fw.op(pool, lambda: G.tensor_tensor(out=wdc[:, fi, :], in0=wd_[:, fi, :], in1=gc[:], op=ALU.mult),
                                  reads=[bwdn, bbc], writes=[bwdn])
                    for fi in range(nf):
                        fw.op(pool, lambda: G.tensor_tensor(out=wd_[:, fi, :], in0=wd_[:, fi, :], in1=gl[:], op=ALU.mult),
                              reads=[bwdn, bbc], writes=[bwdn])
                    for ci, (t0, n) in enumerate(chs):
                        N = n * 128
                        c0 = t0 * 128
                        a_, ba_ = aT.next()
                        for fi in range(nf):
                            pg, pu = (0, 1) if fi % 2 == 0 else (2, 3)
                            mm_group(pg, psum[pg][:, 0:N], [(wg[:, c, fi * 128:(fi + 1) * 128], h2[:, c, c0:c0 + N]) for c in range(DC)], [bwgu, bh2[ci]])
                            mm_group(pu, psum[pu][:, 0:N], [(wu[:, c, fi * 128:(fi + 1) * 128], h2[:, c, c0:c0 + N]) for c in range(DC)], [bwgu, bh2[ci]])
                            s_, bs_ = sg.next()
                            fw.op(act, lambda: S.activation(out=s_[:, 0:N], in_=psum[pg][:, 0:N], func=AF.Silu), reads=[bps[pg]], writes=[bs_])
                            fw.op(dve, lambda: V.tensor_tensor(out=a_[:, fi, 0:N], in0=psum[pu][:, 0:N], in1=s_[:, 0:N], op=ALU.mult),
                                  reads=[bps[pu], bs_], writes=[ba_])
                        for t in range(n):
                            tt = t0 + t
                            wsel = wdc if tt == 16 else wd_
                            for hf in range(2):
                                pd = 4 + 2 * (t % 2) + hf
                                mm_group(pd, psum[pd][:], [(a_[:, fi, t * 128:(t + 1) * 128], wsel[:, fi, hf * 512:(hf + 1) * 512]) for fi in range(nf)], [ba_, bwdn])
                                xs = x_sb[:, tt, hf * 512:(hf + 1) * 512]
                                if pi_ == 0:
                                    fw.op(dve, lambda: V.scalar_tensor_tensor(out=xs, in0=xs, scalar=ALPHA, in1=psum[pd][:], op0=ALU.mult, op1=ALU.add),
                                          reads=[bx[tt], bps[pd]], writes=[bx[tt]])
                                else:
                                    fw.op(dve, lambda: V.tensor_tensor(out=xs, in0=xs, in1=psum[pd][:], op=ALU.add),
                                          reads=[bx[tt], bps[pd]], writes=[bx[tt]])
                for t in range(n_tiles):
                    layer_norm_tile(stt.next(), t, lg, lb, bbc)
                fw.barrier()

        if dbg:
            fw.dma(sp, dbg_x.rearrange("(t p) d -> p t d", p=128), x_sb[:], reads=bx, writes=[Buf()])
        for (t0, n) in CHUNKS[:4]:
            fw.dma(sp, out[t0 * 128:(t0 + n) * 128, :].rearrange("(t p) d -> p t d", p=128), x_sb[:, t0:t0 + n, :],
                   reads=bx[t0:t0 + n], writes=[Buf()])
        fw.barrier()
    return nc


def _rope_tables(half, hd, dup):
    hh = hd // 2
    nf = hh // 2
    freqs = (10000.0 ** (-np.arange(0, hh, 2, dtype=np.float32) / np.float32(hh))).astype(np.float32)
    t = np.arange(half * 2048, (half + 1) * 2048)
    pos_r = (t // 64).astype(np.float32)
    pos_c = (t % 64).astype(np.float32)
    cos = np.ones((hd, TOK), np.float32)
    sin = np.zeros((hd, TOK), np.float32)
    for blk in range(4):
        pos = pos_r if blk < 2 else pos_c
        ang = (pos[None, :] * freqs[:, None]).astype(np.float32)
        cos[blk * nf:(blk + 1) * nf, :2048] = np.cos(ang)
        sgn = -1.0 if blk % 2 == 0 else 1.0
        sin[blk * nf:(blk + 1) * nf, :2048] = sgn * np.sin(ang)
    if dup:
        cos = np.concatenate([cos, cos], 0)
        sin = np.concatenate([sin, sin], 0)
    return np.ascontiguousarray(np.stack([cos, sin], 0))


def _swap_perm(hd):
    nf = hd // 4
    idx = np.arange(hd).reshape(4, nf)
    return idx[[1, 0, 3, 2]].reshape(-1)


def _nbr_tables(rpb, rank):
    tab = np.full((8, 5, 128, 12, 128), NEG, np.float32)
    kk = np.arange(128)
    qq = np.arange(128)
    cfg_il = [0, 1, 5, 14, 15]
    for cf, il in enumerate(cfg_il):
        base = min(il, 14)
        gq = rank * 16 + il
        qrow = 2 * gq + qq // 64
        qc = qq % 64
        rs = np.clip(qrow - 4, 0, 56)
        cs = np.clip(qc - 8, 0, 48)
        for slot in range(6):
            e = base + slot
            if e < 2:
                g = None if rank == 0 else 14 + e
            elif e < 18:
                g = rank * 16 + (e - 2)
            else:
                g = 16 + (e - 18) if rank == 0 else None
            if g is None:
                continue
            krow = 2 * g + kk // 64
            kc = kk % 64
            valid = ((krow[:, None] >= rs[None, :]) & (krow[:, None] < rs[None, :] + 8) &
                     (kc[:, None] >= cs[None, :]) & (kc[:, None] < cs[None, :] + 16))
            dr = np.clip(krow[:, None] - qrow[None, :] + 7, 0, 14)
            dc = np.clip(kc[:, None] - qc[None, :] + 15, 0, 30)
            for h in range(16):
                vals = rpb[h][dr, dc]
                tab[h // 2, cf, :, (h % 2) * 6 + slot, :] = np.where(valid, vals, np.float32(NEG))
    return tab


_PROG = {}


def _host_inputs(inp):
    f = lambda a: np.ascontiguousarray(np.asarray(a, dtype=np.float32))
    x, c, ctx, c_ctx = f(inp["x"]), f(inp["c"]), f(inp["ctx"]), f(inp["c_ctx"])
    p128, p64 = _swap_perm(128), _swap_perm(64)
    aw = f(inp["a_w_qkv"])
    qk = aw[:, :, :1280].reshape(2, D, 10, 128)
    a_wqkv = np.concatenate([aw, qk[:, :, :, p128].reshape(2, D, 1280)], axis=2)
    aq, ak = f(inp["a_q_gain"]), f(inp["a_k_gain"])
    a_gain = np.stack([aq, aq[:, p128], ak, ak[:, p128]], axis=1)
    cw = f(inp["c_w_dqkv"])[0]
    rope = cw[:, 640:704]
    c_wdqkv = np.concatenate([cw[:, :640], rope, rope, rope[:, p64], rope[:, p64]], axis=1)
    uq = f(inp["c_w_uq"])[0].reshape(384, 8, 192)
    nope = uq[:, :, :128].reshape(384, 1024)
    rp = uq[:, :, 128:]
    c_wuq = np.concatenate([nope, rp.reshape(384, 512), rp[:, :, p64].reshape(384, 512)], axis=1)
    ukv = f(inp["c_w_ukv"])[0].reshape(256, 8, 256)
    c_wukv = np.concatenate([ukv[:, :, :128].reshape(256, 1024), ukv[:, :, 128:].reshape(256, 1024)], axis=1)
    c_gain = np.concatenate([f(inp["c_q_a_gain"])[0].reshape(3, 128), f(inp["c_kv_a_gain"])[0].reshape(2, 128)], axis=0)
    shared = dict(
        w_ada=f(inp["w_ada"]), b_ada=f(inp["b_ada"]), ln_g=f(inp["ln_g"]), ln_b=f(inp["ln_b"]),
        w_ffn_gate=f(inp["w_ffn_gate"]), w_ffn_up=f(inp["w_ffn_up"]), w_ffn_down=f(inp["w_ffn_down"]),
        a_wqkv=np.ascontiguousarray(a_wqkv), a_gain=np.ascontiguousarray(a_gain), a_w_o=f(inp["a_w_o"]),
        b_w_qkv=f(inp["b_w_qkv"]), b_w_o=f(inp["b_w_o"]),
        c_wdqkv=np.ascontiguousarray(c_wdqkv), c_gain=np.ascontiguousarray(c_gain),
        c_wuq=np.ascontiguousarray(c_wuq), c_wukv=np.ascontiguousarray(c_wukv), c_w_o=f(inp["c_w_o"]))
    rpb = f(inp["b_rpb"])[0]
    per_rank = [dict(ropeA=_rope_tables(r, 128, False), ropeC=_rope_tables(r, 64, True), b_tab=_nbr_tables(rpb, r)) for r in range(2)]
    maps = []
    for core in range(8):
        b, r = core // 2, core % 2
        m = dict(shared)
        m.update(per_rank[r])
        m["xin"] = np.ascontiguousarray(np.concatenate([x[b, r * 2048:(r + 1) * 2048], ctx[b, r * 128:(r + 1) * 128]], axis=0))
        m["cc"] = np.ascontiguousarray(np.stack([c[b], c_ctx], axis=0))
        maps.append(m)
    return maps


def kernel(**inputs):
    maps = _host_inputs(inputs)
    if "nc" not in _PROG:
        _PROG["nc"] = build_program()
    res = run_bass_kernel_spmd(_PROG["nc"], maps, core_ids=list(range(8)))
    out = np.zeros((4, 4096, D), np.float32)
    for core in range(8):
        b, r = core // 2, core % 2
        out[b, r * 2048:(r + 1) * 2048] = np.asarray(res.results[core]["out"], dtype=np.float32)
    return out
```
